# Optimizing a Trainium2 kernel written in Bass

```python
import jax, jax.numpy as jnp
from jax import lax
import numpy as np

D_MODEL = 1024
BATCH = 2
SEQ = 16384
DEPTH = 4

CONV_WIDTH = D_MODEL // 2
CONV_K = 31
HGRN_HEAD_DIM = 128
HGRN_HEADS = D_MODEL // HGRN_HEAD_DIM
HGRN_WIDTH = HGRN_HEADS * HGRN_HEAD_DIM
CHUNK = 64
D_FF = 256 * ((8 * D_MODEL // 3 + 255) // 256)
FFN_CONV_K = 3
EPS = 1e-6
MIN_FORGET = 1e-6

IN_WIDTH = 2 * CONV_WIDTH + 5 * HGRN_WIDTH + 2 * D_MODEL
SPLIT_POINTS = (
    2 * CONV_WIDTH,
    2 * CONV_WIDTH + 1 * HGRN_WIDTH,
    2 * CONV_WIDTH + 2 * HGRN_WIDTH,
    2 * CONV_WIDTH + 3 * HGRN_WIDTH,
    2 * CONV_WIDTH + 4 * HGRN_WIDTH,
    2 * CONV_WIDTH + 5 * HGRN_WIDTH,
    2 * CONV_WIDTH + 5 * HGRN_WIDTH + D_MODEL,
)

kernel_name = "bidir_conformer_hgrn2_gated_hybrid"


def _rmsnorm(x, w):
    xf = x.astype(jnp.float32)
    y = xf * lax.rsqrt(jnp.mean(xf * xf, axis=-1, keepdims=True) + EPS)
    return (y * w.astype(jnp.float32)).astype(x.dtype)


def _layernorm(x, w, b):
    xf = x.astype(jnp.float32)
    mu = jnp.mean(xf, axis=-1, keepdims=True)
    var = jnp.mean(jnp.square(xf - mu), axis=-1, keepdims=True)
    y = (xf - mu) * lax.rsqrt(var + EPS)
    return (y * w.astype(jnp.float32) + b.astype(jnp.float32)).astype(x.dtype)


def _dwconv_centred(x, w, b):
    k = w.shape[0]
    pad = k // 2
    y = lax.conv_general_dilated(
        x, w[:, None, :].astype(x.dtype), window_strides=(1,), padding=[(pad, pad)],
        dimension_numbers=("NWC", "WIO", "NWC"), feature_group_count=x.shape[-1])
    return y + b.astype(x.dtype)


def _chunk_gated_scan(q, k, v, lf):
    b_, s_, h_, dk = q.shape
    dv = v.shape[-1]
    nc = s_ // CHUNK

    def to_chunks(t):
        return t.astype(jnp.float32).reshape(b_, nc, CHUNK, h_, t.shape[-1]).transpose(1, 0, 3, 2, 4)

    qc, kc, vc, lfc = to_chunks(q), to_chunks(k), to_chunks(v), to_chunks(lf)
    pos = jnp.arange(CHUNK)
    tril = (pos[:, None] >= pos[None, :])[:, :, None]

    def step(state, inp):
        qb, kb, vb, lfb = inp
        a = jnp.cumsum(lfb, axis=2)
        diff = a[:, :, :, None, :] - a[:, :, None, :, :]
        decay = jnp.where(tril, jnp.exp(jnp.minimum(diff, 0.0)), 0.0)
        scores = jnp.einsum("bhtk,bhsk,bhtsk->bhts", qb, kb, decay)
        o = (jnp.einsum("bhts,bhsv->bhtv", scores, vb)
             + jnp.einsum("bhtk,bhkv->bhtv", qb * jnp.exp(a), state))
        a_last = a[:, :, -1:, :]
        new_state = (jnp.exp(a_last[:, :, 0, :])[..., None] * state
                     + jnp.einsum("bhsk,bhsv->bhkv", kb * jnp.exp(a_last - a), vb))
        return new_state, o

    s0 = jnp.zeros((b_, h_, dk, dv), jnp.float32)
    _, o = lax.scan(step, s0, (qc, kc, vc, lfc))
    return o.transpose(1, 0, 3, 2, 4).reshape(b_, s_, h_, dv)


def _forget_log_and_key(z, lb):
    zf = z.astype(jnp.float32)
    f = lb + (1.0 - lb) * jax.nn.sigmoid(zf)
    lf = jnp.log(jnp.clip(f, MIN_FORGET, 1.0))
    key_gate = (1.0 - lb) * jax.nn.sigmoid(-zf)
    return lf, key_gate


def _bi_hgrn2(q, v, z_fwd, z_bwd, lb_fwd, lb_bwd):
    b_, s_, _ = q.shape
    heads = lambda t: t.reshape(b_, s_, HGRN_HEADS, HGRN_HEAD_DIM)
    qh = heads(jax.nn.silu(q.astype(jnp.float32)) * (HGRN_HEAD_DIM ** -0.5))
    vh = heads(v)
    lf_f, k_f = _forget_log_and_key(z_fwd, lb_fwd)
    lf_b, k_b = _forget_log_and_key(z_bwd, lb_bwd)
    o_fwd = _chunk_gated_scan(qh, heads(k_f), vh, heads(lf_f))
    flip = lambda t: jnp.flip(t, axis=1)
    o_bwd = flip(_chunk_gated_scan(flip(qh), flip(heads(k_b)), flip(vh), flip(heads(lf_b))))
    return o_fwd + o_bwd


def _token_mixers(h, w_in, dw_w, dw_b, ln_w, ln_b, pw_w, lb, norm_w, o_w):
    proj = jnp.einsum("bsd,de->bse", h, w_in)
    glu_in, q, v, z_f, z_b, og, g_a, g_b = jnp.split(proj, SPLIT_POINTS, axis=-1)
    a = jax.nn.glu(glu_in, axis=-1)
    a = _dwconv_centred(a, dw_w, dw_b)
    a = jax.nn.silu(_layernorm(a, ln_w, ln_b))
    a = jnp.einsum("bsc,cd->bsd", a, pw_w)
    o = _bi_hgrn2(q, v, z_f, z_b, lb[0], lb[1])
    of = o * lax.rsqrt(jnp.mean(o * o, axis=-1, keepdims=True) + EPS)
    of = of.reshape(o.shape[0], o.shape[1], HGRN_WIDTH) * norm_w.astype(jnp.float32)
    bmix = (of * jax.nn.silu(og.astype(jnp.float32))).astype(h.dtype)
    bmix = jnp.einsum("bse,ed->bsd", bmix, o_w)
    return jax.nn.sigmoid(g_a) * a + jax.nn.sigmoid(g_b) * bmix


def _conv_glu_ffn(h, w_up, dw_w, dw_b, w_down):
    up = jnp.einsum("bsd,df->bsf", h, w_up)
    gate, val = jnp.split(up, 2, axis=-1)
    gate = _dwconv_centred(gate, dw_w, dw_b)
    return jnp.einsum("bsf,fd->bsd", jax.nn.silu(gate) * val, w_down)


def setup_inputs(seed: int = 0) -> dict:
    key = jax.random.key(seed)
    ks = jax.random.split(key, 18)

    def nrm(k, shape, scale):
        return scale * jax.random.normal(k, shape, jnp.float32)

    def gain(k, shape):
        return 1.0 + 0.02 * jax.random.normal(k, shape, jnp.float32)

    return {
        "x": nrm(ks[0], (BATCH, SEQ, D_MODEL), 1.0),
        "attn_norm_w": gain(ks[1], (DEPTH, D_MODEL)),
        "w_in": nrm(ks[2], (DEPTH, D_MODEL, IN_WIDTH), D_MODEL ** -0.5),
        "conv_dw_w": nrm(ks[3], (DEPTH, CONV_K, CONV_WIDTH), CONV_K ** -0.5),
        "conv_dw_b": nrm(ks[4], (DEPTH, CONV_WIDTH), 0.02),
        "conv_ln_w": gain(ks[5], (DEPTH, CONV_WIDTH)),
        "conv_ln_b": nrm(ks[6], (DEPTH, CONV_WIDTH), 0.02),
        "conv_pw_w": nrm(ks[7], (DEPTH, CONV_WIDTH, D_MODEL), CONV_WIDTH ** -0.5),
        "lb_logits": nrm(ks[8], (DEPTH, 2, HGRN_WIDTH), 0.5),
        "hgrn_norm_w": gain(ks[9], (DEPTH, HGRN_WIDTH)),
        "hgrn_o_w": nrm(ks[10], (DEPTH, HGRN_WIDTH, D_MODEL), HGRN_WIDTH ** -0.5),
        "w_out": nrm(ks[11], (DEPTH, D_MODEL, D_MODEL), D_MODEL ** -0.5),
        "ffn_norm_w": gain(ks[12], (DEPTH, D_MODEL)),
        "ffn_w_up": nrm(ks[13], (DEPTH, D_MODEL, 2 * D_FF), D_MODEL ** -0.5),
        "ffn_dw_w": nrm(ks[14], (DEPTH, FFN_CONV_K, D_FF), FFN_CONV_K ** -0.5),
        "ffn_dw_b": nrm(ks[15], (DEPTH, D_FF), 0.02),
        "ffn_w_down": nrm(ks[16], (DEPTH, D_FF, D_MODEL), D_FF ** -0.5),
        "final_norm_w": gain(ks[17], (D_MODEL,)),
    }


def reference(x, attn_norm_w, w_in, conv_dw_w, conv_dw_b, conv_ln_w, conv_ln_b, conv_pw_w,
              lb_logits, hgrn_norm_w, hgrn_o_w, w_out, ffn_norm_w, ffn_w_up, ffn_dw_w,
              ffn_dw_b, ffn_w_down, final_norm_w):
    p = jax.nn.softmax(lb_logits.astype(jnp.float32), axis=0)
    lower_bounds = jnp.cumsum(p, axis=0) - p[0:1]
    for layer in range(DEPTH):
        h = _rmsnorm(x, attn_norm_w[layer])
        y = _token_mixers(h, w_in[layer], conv_dw_w[layer], conv_dw_b[layer], conv_ln_w[layer],
                          conv_ln_b[layer], conv_pw_w[layer], lower_bounds[layer],
                          hgrn_norm_w[layer], hgrn_o_w[layer])
        x = x + jnp.einsum("bsd,de->bse", y, w_out[layer])
        h2 = _rmsnorm(x, ffn_norm_w[layer])
        x = x + _conv_glu_ffn(h2, ffn_w_up[layer], ffn_dw_w[layer], ffn_dw_b[layer], ffn_w_down[layer])
    return _rmsnorm(x, final_norm_w)
```

```python
import numpy as np
import concourse.bass as bass
import concourse.mybir as mybir
from concourse.bass_utils import run_bass_kernel_spmd

F32 = mybir.dt.float32
BF16 = mybir.dt.bfloat16
AF = mybir.ActivationFunctionType
ALU = mybir.AluOpType

P = 128
D = 1024
KC = 8
T = 512
CH = 64
NCH = T // CH
HL = 16
NH = 8
DFF = 2816
NFB = 22
DEPTH = 4
EPS = 1e-6
NV = 248
NCM = 24
QSCALE = 128 ** -0.5

CB_GLA, CB_GLB, CB_Q, CB_V, CB_ZF, CB_ZB, CB_OG, CB_GA, CB_GB = 0, 4, 8, 16, 24, 32, 40, 48, 56


class Reg:
    __slots__ = ("w", "r")

    def __init__(self):
        self.w = {}
        self.r = {}


class PG:
    NDS = 24

    def __init__(self, nc):
        self.nc = nc
        self.engs = {"pe": nc.tensor, "act": nc.scalar, "dve": nc.vector, "pool": nc.gpsimd, "sp": nc.sync}
        self.sems = {}
        self.cnt = {e: 0 for e in self.engs}
        self.waited = {e: {} for e in self.engs}
        self.ndma = 0
        self.ncc = 0
        self._stack = []
        self.ekey = {}
        self.nep = {}
        for e in self.engs:
            self.sems[e] = self._sem("e_" + e)
        for i in range(self.NDS):
            self.sems["d%d" % i] = self._sem("dma%d" % i)
        for i in range(8):
            self.sems["g%d" % i] = self._sem("gdma%d" % i)
        self.ngdma = 0
        for i in range(4):
            self.sems["c%d" % i] = self._sem("cc%d" % i)

    def _sem(self, name):
        cm = self.nc.semaphore(name)
        s = cm.__enter__()
        self._stack.append(cm)
        return s

    def _deps(self, eng, reads, writes, extra=(), join=False):
        deps = {}

        def add(k, v):
            if deps.get(k, 0) < v:
                deps[k] = v
        for r in reads:
            for k, v in r.w.items():
                add(k, v)
        for w in writes:
            if not join:
                for k, v in w.w.items():
                    add(k, v)
            for k, v in w.r.items():
                add(k, v)
        for (k, v) in extra:
            add(k, v)
        E = self.engs[eng]
        wd = self.waited[eng]
        for k, v in deps.items():
            if eng == "pe" and k.startswith("pe"):
                continue
            if wd.get(k, 0) >= v:
                continue
            E.wait_ge(self.sems[k], v)
            wd[k] = v

    def _mark(self, t, reads, writes, join=False):
        k, v = t
        for r in reads:
            if r.r.get(k, 0) < v:
                r.r[k] = v
        for w in writes:
            if join:
                if w.w.get(k, 0) < v:
                    w.w[k] = v
            else:
                w.w = {k: v}
                w.r = {}

    EPOCH_LEN = 16000

    def op(self, eng, reads, writes, fn):
        self._deps(eng, reads, writes)
        ins = fn(self.engs[eng])
        key = self.ekey.get(eng, eng)
        self.cnt[eng] += 1
        ins.then_inc(self.sems[key], 1)
        t = (key, self.cnt[eng])
        self._mark(t, reads, writes)
        if self.cnt[eng] >= self.EPOCH_LEN:
            self.nep[eng] = self.nep.get(eng, 0) + 1
            nk = "%s#%d" % (eng, self.nep[eng])
            self.sems[nk] = self._sem("e_%s_%d" % (eng, self.nep[eng]))
            self.ekey[eng] = nk
            self.cnt[eng] = 0
        return t

    def dma(self, q, out, in_, reads, writes, join=False, slow=False):
        if q == "pool":
            i = self.ngdma
            self.ngdma += 1
            s = "g%d" % (i % 8)
            v = 16 * (i // 8 + 1)
        else:
            i = self.ndma
            self.ndma += 1
            s = "d%d" % (i % self.NDS)
            v = 16 * (i // self.NDS + 1)
        extra = [(s, v - 16)] if v > 16 else []
        self._deps(q, reads, writes, extra, join)
        if slow:
            ins = self.engs[q].dma_start(out=out, in_=in_, allow_slow_non_contiguous=True)
        else:
            ins = self.engs[q].dma_start(out=out, in_=in_)
        ins.then_inc(self.sems[s], 16)
        t = (s, v)
        self._mark(t, reads, writes, join)
        return t

    def allgather(self, in_ap, out_ap, reads, writes):
        i = self.ncc
        self.ncc += 1
        s = "c%d" % (i % 4)
        v = i // 4 + 1
        extra = [(s, v - 1)] if v > 1 else []
        self._deps("pool", reads, writes, extra)
        ins = self.nc.gpsimd.collective_compute("AllGather", ALU.bypass, replica_groups=[[0, 1, 2, 3], [4, 5, 6, 7]],
                                                ins=[in_ap], outs=[out_ap])
        ins.then_inc(self.sems[s], 1)
        t = (s, v)
        self._mark(t, reads, writes)
        return t

    def barrier(self):
        comp = ("pe", "act", "dve", "pool")
        for e in comp:
            E = self.engs[e]
            for o in comp:
                if o == e:
                    continue
                ok = self.ekey.get(o, o)
                val = self.cnt[o]
                if val == 0:
                    n = self.nep.get(o, 0)
                    if n == 0:
                        continue
                    ok = o if n == 1 else "%s#%d" % (o, n - 1)
                    val = self.EPOCH_LEN
                if self.waited[e].get(ok, 0) >= val:
                    continue
                E.wait_ge(self.sems[ok], val)
                self.waited[e][ok] = val

    def wait_all(self, eng, regs):
        self._deps(eng, regs, regs)


class Buf:
    def __init__(self, nc, name, shape, dtype, psum=False):
        if psum:
            cm = nc.psum_tensor(name, shape, dtype)
        else:
            cm = nc.sbuf_tensor(name, shape, dtype)
        self.cm = cm
        self.t = cm.__enter__()
        self.reg = Reg()

    def __getitem__(self, k):
        return self.t[k]


class WStream:
    def __init__(self, pg, nc, nslot, ws, bufs):
        self.pg = pg
        self.nslot = nslot
        self.slots = [Buf(nc, "wslot%d" % i, [P, ws], BF16) for i in range(nslot)]
        bufs.extend(self.slots)
        self.q = []
        self.issued = []
        self.nxt = 0

    def plan(self, items):
        self.q.extend(items)

    def _issue(self):
        pieces, reg = self.q.pop(0)
        s = self.nxt
        self.nxt = (s + 1) % self.nslot
        sl = self.slots[s]
        off = 0
        views = []
        for (ap, n, e) in pieces:
            out = sl.t[:, off:off + n * e].rearrange("p (j e) -> p j e", j=n)
            self.pg.dma("sp", out, ap.rearrange("j p e -> p j e"), list(reg), [sl.reg], join=(off > 0))
            views.append(out)
            off += n * e
        self.issued.append((s, views))

    def get(self):
        while len(self.issued) < self.nslot - 1 and self.q:
            self._issue()
        s, views = self.issued.pop(0)
        return self.slots[s], views


def build_program(TOK, layers, final_norm):
    NL = len(layers)
    NT = TOK // T
    XW = TOK + 2 * HL
    nc = bass.Bass("TRN2", target_bir_lowering=False)
    pg = PG(nc)
    bufs = []

    def dram_in(name, shape, dt=F32):
        return nc.dram_tensor(name, shape, dt, kind="ExternalInput")

    x_in = dram_in("x_in", [D, XW])
    cm_in = dram_in("cm", [P, NCM])
    consts_in = dram_in("consts", [P, 128 + 512 + 512])
    gvec_in = dram_in("gvec", [P, 8 + 64])
    vec_in = dram_in("vec", [NL, P, NV])
    w_in_f = dram_in("w_in", [NL, 64, P, 1024])
    w_pw_f = dram_in("w_pw", [NL, 8, P, 512])
    w_ow_f = dram_in("w_ow", [NL, 8, P, 1024])
    w_out_f = dram_in("w_out", [NL, 8, P, 1024])
    w_up_f = dram_in("w_up", [NL, 44, P, 1024])
    w_dn_f = dram_in("w_dn", [NL, 8, P, NFB * 128])
    y_out = nc.dram_tensor("y_out", [D, TOK], F32, kind="ExternalOutput")

    w_in_b = nc.dram_tensor("w_in_b", [NL, 64, P, 1024], BF16)
    w_pw_b = nc.dram_tensor("w_pw_b", [NL, 8, P, 512], BF16)
    w_ow_b = nc.dram_tensor("w_ow_b", [NL, 8, P, 1024], BF16)
    w_out_b = nc.dram_tensor("w_out_b", [NL, 8, P, 1024], BF16)
    w_up_b = nc.dram_tensor("w_up_b", [NL, 44, P, 1024], BF16)
    w_dn_b = nc.dram_tensor("w_dn_b", [NL, 8, P, NFB * 128], BF16)
    xA = nc.dram_tensor("xA", [D, XW], F32)
    xB = nc.dram_tensor("xB", [D, XW], F32)
    ob_d = nc.dram_tensor("ob_d", [D, TOK], F32)
    edge_i = nc.dram_tensor("edge_i", [D, 2 * HL], F32)
    edge_o = nc.dram_tensor("edge_o", [4 * D, 2 * HL], F32)
    st_i = [nc.dram_tensor("st_i%d" % q, [4 * P, 129], F32) for q in range(4)]
    st_o = [nc.dram_tensor("st_o%d" % q, [4 * 4 * P, 129], F32) for q in range(4)]
    r_xA, r_xB, r_ob, r_ei, r_eo, r_si, r_so = Reg(), Reg(), Reg(), Reg(), Reg(), Reg(), Reg()
    r_xin = Reg()
    wregs = [{k: Reg() for k in ("in", "pw", "ow", "out", "up", "dn")} for _ in range(NL)]

    def sb(name, shape, dt=F32):
        b = Buf(nc, name, shape, dt)
        bufs.append(b)
        return b

    def ps(name, shape, dt=F32):
        b = Buf(nc, name, shape, dt, psum=True)
        bufs.append(b)
        return b

    class View:
        def __init__(self, t):
            self.t = t
            self.reg = Reg()

    XWT = T + 2 * HL - 2
    cm = sb("cm_sb", [P, NCM])
    consts = sb("consts_sb", [P, 128 + 512 + 512])
    ident = sb("ident", [P, P], BF16)
    ones = sb("ones", [P, P], BF16)
    gvec = sb("gvec_sb", [P, 72])
    vec = [sb("vec%d" % l, [P, NV]) for l in range(NL)]
    lbt = sb("lbt", [P, 64])
    omlb = sb("omlb", [P, 64])
    lbtmp = sb("lbtmp", [P, 64])
    lbm = sb("lbm", [P, 16])
    scanmask = sb("scanmask", [P, T])
    onesf = sb("onesf", [P, T])
    xt = sb("xt", [P, KC, XWT])
    hb = sb("hb", [P, KC, XWT], BF16)
    sqb = sb("sqb", [P, T], BF16)
    rstd = sb("rstd", [P, XWT])
    ws = WStream(pg, nc, 4, 4096, bufs)
    sg = sb("sg", [P, T])
    kk = sb("kk", [P, T])
    lf = sb("lf", [P, T])
    cum = sb("cum", [P, T])
    d1 = sb("d1", [P, T])
    d2 = sb("d2", [P, T])
    ex = [sb("ex%d" % i, [P, T]) for i in range(4)]
    qs = sb("qs", [P, T])
    vb = sb("vb", [P, T], BF16)
    qt = sb("qt", [P, T], BF16)
    qS = sb("qS", [P, T], BF16)
    kt = sb("kt", [P, T], BF16)
    kh = sb("kh", [P, T], BF16)
    khT = sb("khT", [P, NCH * P], BF16)
    vT = sb("vT", [P, NCH * P], BF16)
    scT = sb("scT", [CH, T], BF16)
    S_f = [sb("Sf%d_%d" % (d, h), [P, P]) for d in range(2) for h in range(NH)]
    S_b = [sb("Sb%d_%d" % (d, h), [P, P], BF16) for d in range(2) for h in range(NH)]
    edec = sb("edec", [P, NH * NCH])
    Ltot = sb("Ltot", [P, 2 * NH])
    Abase = sb("Abase", [P, NH])
    eAb = sb("eAb", [P, NH])
    oh = sb("oh", [P, T])
    obh = sb("obh", [P, T])
    bmix = sb("bmix", [P, NH, T], BF16)
    arena = sb("arena", [P, 6400])
    a_ext = View(arena.t[:, 0:4 * XWT].rearrange("p (c n) -> p c n", c=4))
    acc_t = arena.t[:, 2168:2168 + 2048].rearrange("p (c n) -> p c n", c=4)
    accb = View(arena.t[:, 4216:5240].bitcast(BF16).rearrange("p (c n) -> p c n", c=4))
    asw = View(arena.t[:, 5240:6264].bitcast(BF16).rearrange("p (c n) -> p c n", c=4))
    u_sb = View(arena.t[:, 0:5632].bitcast(BF16).rearrange("p (c n) -> p c n", c=NFB))
    mu = sb("mu", [P, T])
    var = sb("var", [P, T])
    sga = sb("sga", [P, T])
    sgb = sb("sgb", [P, T])
    t1 = sb("t1", [P, T])
    t2 = sb("t2", [P, T])
    ymix = sb("ymix", [P, KC, T], BF16)
    g_sb = sb("g_sb", [P, T + 2])
    gacc = sb("gacc", [P, T])
    stb = sb("stb", [P, 4, 129])
    coef = sb("coef", [P, 2 * NH])
    stmp = sb("stmp", [P, P])
    eo_sb = sb("eo_sb", [P, 4, KC, 2 * HL])
    hl_sb = sb("hl_sb", [P, KC, 2 * HL])
    pp = [ps("pp%d" % i, [P, T]) for i in range(3)]
    p_small = ps("p_small", [P, T])
    p_tr = ps("p_tr", [P, NCH * P], BF16)
    p_tr2 = ps("p_tr2", [P, NCH * P], BF16)
    p_sc = ps("p_sc", [P, T])
    p_o = ps("p_o", [P, T])
    pp_i = [0]

    def next_pp():
        b = pp[pp_i[0] % 3]
        pp_i[0] += 1
        return b

    def cast_w(l):
        for (src, dst, key, nb, step) in ((w_in_f, w_in_b, "in", 64, 2), (w_pw_f, w_pw_b, "pw", 8, 4),
                                          (w_ow_f, w_ow_b, "ow", 8, 2), (w_out_f, w_out_b, "out", 8, 2),
                                          (w_up_f, w_up_b, "up", 44, 2), (w_dn_f, w_dn_b, "dn", 8, 1)):
            for j in range(0, nb, step):
                pg.dma("pool", dst.ap()[l, j:j + step], src.ap()[l, j:j + step], [], [wregs[l][key]], join=True)

    pg.dma("sp", cm.t[:], cm_in.ap(), [], [cm.reg])
    pg.dma("sp", consts.t[:], consts_in.ap(), [], [consts.reg])
    pg.dma("sp", gvec.t[:], gvec_in.ap(), [], [gvec.reg])
    for l in range(NL):
        pg.dma("sp", vec[l].t[:], vec_in.ap()[l], [], [vec[l].reg])
    cast_w(0)
    for c8 in range(KC):
        pg.dma("sp", xA.ap()[c8 * P:(c8 + 1) * P, :], x_in.ap()[c8 * P:(c8 + 1) * P, :], [r_xin], [r_xA], join=True)
    pg.op("dve", [consts.reg], [ident.reg], lambda e: e.tensor_copy(out=ident.t[:], in_=consts.t[:, 0:128]))
    pg.op("pool", [], [ones.reg], lambda e: e.memset(ones.t[:], 1.0))
    pg.op("pool", [], [onesf.reg], lambda e: e.memset(onesf.t[:], 1.0))
    pg.op("pool", [], [scanmask.reg], lambda e: e.memset(scanmask.t[:], 1.0))
    pg.op("pool", [], [scanmask.reg],
          lambda e: e.memset(scanmask.t[:].rearrange("p (c j) -> p c j", j=CH)[:, :, 0:1], 0.0))
    maskf = consts.t[0:CH, 128:128 + 512]
    maskb = consts.t[0:CH, 640:640 + 512]

    lg = gvec.t[:, 8:72].rearrange("p (l r) -> p l r", l=4)
    TT = lambda o, a, b, op: pg.op("dve", [gvec.reg, lbm.reg, lbtmp.reg, lbt.reg], [], lambda e: e.tensor_tensor(out=o, in0=a, in1=b, op=op))
    lt3 = lbtmp.t[:].rearrange("p (l r) -> p l r", l=4)
    lb3 = lbt.t[:].rearrange("p (l r) -> p l r", l=4)

    def lbop(writes, fn, eng="dve"):
        pg.op(eng, [gvec.reg, lbm.reg, lbtmp.reg, lbt.reg], writes, fn)
    lbop([lbm.reg], lambda e: e.tensor_tensor(out=lbm.t[:], in0=lg[:, 0, :], in1=lg[:, 1, :], op=ALU.max))
    lbop([lbm.reg], lambda e: e.tensor_tensor(out=lbm.t[:], in0=lbm.t[:], in1=lg[:, 2, :], op=ALU.max))
    lbop([lbm.reg], lambda e: e.tensor_tensor(out=lbm.t[:], in0=lbm.t[:], in1=lg[:, 3, :], op=ALU.max))
    for l4 in range(4):
        lbop([lbtmp.reg], lambda e: e.tensor_tensor(out=lt3[:, l4, :], in0=lg[:, l4, :], in1=lbm.t[:], op=ALU.subtract))
    lbop([lbtmp.reg], lambda e: e.activation(out=lbtmp.t[:], in_=lbtmp.t[:], func=AF.Exp), eng="act")
    lbop([lbm.reg], lambda e: e.tensor_tensor(out=lbm.t[:], in0=lt3[:, 0, :], in1=lt3[:, 1, :], op=ALU.add))
    lbop([lbm.reg], lambda e: e.tensor_tensor(out=lbm.t[:], in0=lbm.t[:], in1=lt3[:, 2, :], op=ALU.add))
    lbop([lbm.reg], lambda e: e.tensor_tensor(out=lbm.t[:], in0=lbm.t[:], in1=lt3[:, 3, :], op=ALU.add))
    lbop([lbm.reg], lambda e: e.reciprocal(out=lbm.t[:], in_=lbm.t[:]))
    for l4 in range(4):
        lbop([lbtmp.reg], lambda e: e.tensor_tensor(out=lt3[:, l4, :], in0=lt3[:, l4, :], in1=lbm.t[:], op=ALU.mult))
    lbop([lbt.reg], lambda e: e.memset(lbt.t[:], 0.0), eng="pool")
    for l4 in range(1, 4):
        lbop([lbt.reg], lambda e: e.tensor_tensor(out=lb3[:, l4, :], in0=lb3[:, l4 - 1, :], in1=lt3[:, l4, :], op=ALU.add))
    lbop([omlb.reg], lambda e: e.tensor_scalar(out=omlb.t[:], in0=lbt.t[:], scalar1=-1.0, scalar2=1.0, op0=ALU.mult, op1=ALU.add))

    def load_x(xbuf, rx, col0, ncols):
        pg.dma("sp", xt.t[:, :, 0:ncols], xbuf.ap()[:, col0:col0 + ncols].rearrange("(c p) n -> p c n", p=P),
               [rx], [xt.reg])

    def sumsq(src_fn, nk, n, pst):
        for c in range(nk):
            pg.op("act", [xt.reg, oh.reg], [sqb.reg], lambda e: e.activation(out=sqb.t[:, 0:n], in_=src_fn(c), func=AF.Square))
            pg.op("pe", [ones.reg, sqb.reg], [pst.reg],
                  lambda e: e.matmul(pst.t[:, 0:n], lhsT=ones.t[:], rhs=sqb.t[:, 0:n], start=(c == 0), stop=(c == nk - 1)))

    def rsqrt_to(dst_buf, dst_ap, pst, n, scale):
        pg.op("dve", [pst.reg], [dst_buf.reg],
              lambda e: e.tensor_scalar(out=dst_ap, in0=pst.t[:, 0:n], scalar1=scale, scalar2=EPS, op0=ALU.mult, op1=ALU.add))
        pg.op("act", [dst_buf.reg], [dst_buf.reg], lambda e: e.activation(out=dst_ap, in_=dst_ap, func=AF.Sqrt))
        pg.op("dve", [dst_buf.reg], [dst_buf.reg], lambda e: e.reciprocal(out=dst_ap, in_=dst_ap))

    def rmsnorm(ncols, wcols, wreg):
        for (a, b) in ((0, min(ncols, T)), (T, ncols)):
            if b <= a:
                continue
            n = b - a
            pst = p_small if a > 0 else next_pp()
            sumsq(lambda c: xt.t[:, c, a:b], KC, n, pst)
            rsqrt_to(rstd, rstd.t[:, a:b], pst, n, 1.0 / D)
        for c in range(KC):
            eng = "dve"
            pg.op(eng, [xt.reg, rstd.reg, wreg], [hb.reg],
                  lambda e: e.scalar_tensor_tensor(out=hb.t[:, c, 0:ncols], in0=xt.t[:, c, 0:ncols], scalar=wcols[:, c:c + 1],
                                                   in1=rstd.t[:, 0:ncols], op0=ALU.mult, op1=ALU.mult))

    def proj(wslot, wv, j, col0, n, out_ps, kc=KC, rhs=None):
        rb = hb if rhs is None else rhs
        for c in range(kc):
            pg.op("pe", [wslot.reg, rb.reg], [out_ps.reg],
                  lambda e: e.matmul(out_ps.t[:, 0:n], lhsT=wv[:, j, c * P:(c + 1) * P], rhs=rb.t[:, c, col0:col0 + n],
                                     start=(c == 0), stop=(c == kc - 1)))

    def wi(l, cbs):
        return ([(w_in_b.ap()[l, cb:cb + 1], 1, 1024) for cb in cbs], [wregs[l]["in"]])

    def gate_math(zps, l, d, h):
        ci = (layers[l] * 2 + d) * 8 + h
        lbc = lbt.t[:, ci:ci + 1]
        omc = omlb.t[:, ci:ci + 1]
        pg.op("act", [zps.reg], [sg.reg], lambda e: e.activation(out=sg.t[:], in_=zps.t[:], func=AF.Sigmoid))
        pg.op("act", [zps.reg], [kk.reg], lambda e: e.activation(out=kk.t[:], in_=zps.t[:], func=AF.Sigmoid, scale=-1.0))
        pg.op("dve", [sg.reg, omlb.reg, lbt.reg], [sg.reg],
              lambda e: e.tensor_scalar(out=sg.t[:], in0=sg.t[:], scalar1=omc, scalar2=lbc, op0=ALU.mult, op1=ALU.add))
        pg.op("dve", [sg.reg], [sg.reg],
              lambda e: e.tensor_scalar(out=sg.t[:], in0=sg.t[:], scalar1=1e-6, scalar2=1.0, op0=ALU.max, op1=ALU.min))
        pg.op("act", [sg.reg], [lf.reg], lambda e: e.activation(out=lf.t[:], in_=sg.t[:], func=AF.Ln))
        pg.op("pool", [kk.reg, omlb.reg], [kk.reg],
              lambda e: e.tensor_scalar(out=kk.t[:], in0=kk.t[:], scalar1=omc, scalar2=None, op0=ALU.mult))

    def sweep1(l, xbuf, rx):
        for b in S_f:
            pg.op("pool", [], [b.reg], lambda e: e.memset(b.t[:], 0.0))
        pg.op("pool", [], [Ltot.reg], lambda e: e.memset(Ltot.t[:], 0.0))
        pg.op("pool", [], [Abase.reg], lambda e: e.memset(Abase.t[:], 0.0))
        ws.plan([wi(l, [CB_V + h, CB_ZF + h, CB_ZB + h]) for h in range(NH)] * NT)
        for i in range(NT):
            load_x(xbuf, rx, HL + i * T, T)
            rmsnorm(T, vec[l].t[:, 0:8], vec[l].reg)
            for h in range(NH):
                sl, wv = ws.get()
                pv = next_pp()
                proj(sl, wv[0], 0, 0, T, pv)
                pg.op("act", [pv.reg], [vb.reg], lambda e: e.activation(out=vb.t[:], in_=pv.t[:], func=AF.Copy))
                for tb in range(4):
                    pg.op("pe", [vb.reg, ident.reg], [p_tr.reg],
                          lambda e: e.transpose(p_tr.t[:, tb * P:(tb + 1) * P], vb.t[:, tb * P:(tb + 1) * P], ident.t[:]))
                pg.op("dve", [p_tr.reg], [vT.reg], lambda e: e.tensor_copy(out=vT.t[:, 0:4 * P], in_=p_tr.t[:, 0:4 * P]))
                for d in range(2):
                    pz = next_pp()
                    proj(sl, wv[1 + d], 0, 0, T, pz)
                    gate_math(pz, l, d, h)
                    pg.op("dve", [lf.reg, onesf.reg], [cum.reg],
                          lambda e: e.tensor_tensor_scan(out=cum.t[:], data0=onesf.t[:], data1=lf.t[:], initial=0.0, op0=ALU.mult, op1=ALU.add))
                    if d == 0:
                        pg.op("act", [cum.reg], [ex[0].reg],
                              lambda e: e.activation(out=ex[0].t[:], in_=cum.t[:], func=AF.Exp, scale=-1.0, bias=cum.t[:, T - 1:T]))
                    else:
                        pg.op("dve", [cum.reg, lf.reg], [d1.reg],
                              lambda e: e.tensor_tensor(out=d1.t[:], in0=cum.t[:], in1=lf.t[:], op=ALU.subtract))
                        pg.op("act", [d1.reg], [ex[0].reg], lambda e: e.activation(out=ex[0].t[:], in_=d1.t[:], func=AF.Exp))
                    pg.op("dve", [kk.reg, ex[0].reg], [kh.reg],
                          lambda e: e.tensor_tensor(out=kh.t[:], in0=kk.t[:], in1=ex[0].t[:], op=ALU.mult))
                    for tb in range(4):
                        pg.op("pe", [kh.reg, ident.reg], [p_tr2.reg],
                              lambda e: e.transpose(p_tr2.t[:, tb * P:(tb + 1) * P], kh.t[:, tb * P:(tb + 1) * P], ident.t[:]))
                    pg.op("act", [p_tr2.reg], [khT.reg], lambda e: e.activation(out=khT.t[:, 0:4 * P], in_=p_tr2.t[:, 0:4 * P], func=AF.Copy))
                    for tb in range(4):
                        pg.op("pe", [khT.reg, vT.reg], [p_o.reg],
                              lambda e: e.matmul(p_o.t[:, 0:P], lhsT=khT.t[:, tb * P:(tb + 1) * P], rhs=vT.t[:, tb * P:(tb + 1) * P],
                                                 start=(tb == 0), stop=(tb == 3)))
                    S = S_f[d * NH + h]
                    lt = Ltot.t[:, d * NH + h:d * NH + h + 1]
                    if d == 0:
                        pg.op("act", [cum.reg], [eAb.reg],
                              lambda e: e.activation(out=eAb.t[:, h:h + 1], in_=cum.t[:, T - 1:T], func=AF.Exp))
                        pg.op("dve", [S.reg, eAb.reg, p_o.reg], [S.reg],
                              lambda e: e.scalar_tensor_tensor(out=S.t[:], in0=S.t[:], scalar=eAb.t[:, h:h + 1], in1=p_o.t[:, 0:P],
                                                               op0=ALU.mult, op1=ALU.add))
                    else:
                        pg.op("act", [Abase.reg], [eAb.reg],
                              lambda e: e.activation(out=eAb.t[:, h:h + 1], in_=Abase.t[:, h:h + 1], func=AF.Exp))
                        pg.op("dve", [S.reg, eAb.reg, p_o.reg], [S.reg],
                              lambda e: e.scalar_tensor_tensor(out=S.t[:], in0=p_o.t[:, 0:P], scalar=eAb.t[:, h:h + 1], in1=S.t[:],
                                                               op0=ALU.mult, op1=ALU.add))
                        pg.op("dve", [Abase.reg, cum.reg], [Abase.reg],
                              lambda e: e.tensor_tensor(out=Abase.t[:, h:h + 1], in0=Abase.t[:, h:h + 1], in1=cum.t[:, T - 1:T], op=ALU.add))
                    pg.op("dve", [Ltot.reg, cum.reg], [Ltot.reg],
                          lambda e: e.tensor_tensor(out=lt, in0=lt, in1=cum.t[:, T - 1:T], op=ALU.add))
        r_sq = [Reg() for _ in range(4)]
        r_soq = [Reg() for _ in range(4)]
        for q in range(4):
            sti = st_i[q].ap().rearrange("(g p) n -> p g n", p=P)
            for g in range(4):
                pg.op("act", [S_f[q * 4 + g].reg], [stb.reg],
                      lambda e: e.activation(out=stb.t[:, g, 0:P], in_=S_f[q * 4 + g].t[:], func=AF.Copy))
            pg.op("dve", [Ltot.reg, stb.reg], [stb.reg], lambda e: e.tensor_copy(out=stb.t[:, :, P], in_=Ltot.t[:, q * 4:(q + 1) * 4]))
            pg.dma("sp", sti, stb.t[:], [stb.reg], [r_sq[q]])
            pg.allgather(st_i[q].ap(), st_o[q].ap(), [r_sq[q]], [r_soq[q]])
        for b in S_f:
            pg.op("pool", [], [b.reg], lambda e: e.memset(b.t[:], 0.0))
        sto = [st_o[q].ap().rearrange("(m g p) n -> m p g n", p=P, g=4) for q in range(4)]
        for d in range(2):
            order = [0, 1, 2] if d == 0 else [3, 2, 1]
            for m in order:
                uc = cm.t[:, d * 8 + m:d * 8 + m + 1]
                omu = cm.t[:, d * 8 + 4 + m:d * 8 + 4 + m + 1]
                for hq in range(2):
                    g0 = d * NH + hq * 4
                    pg.dma("sp", stb.t[:], sto[d * 2 + hq][m], [r_soq[d * 2 + hq]], [stb.reg])
                    pg.op("act", [stb.reg], [coef.reg],
                          lambda e: e.activation(out=coef.t[:, g0:g0 + 4], in_=stb.t[:, :, P], func=AF.Exp))
                    pg.op("dve", [coef.reg, cm.reg], [coef.reg],
                          lambda e: e.tensor_scalar(out=coef.t[:, g0:g0 + 4], in0=coef.t[:, g0:g0 + 4],
                                                    scalar1=uc, scalar2=omu, op0=ALU.mult, op1=ALU.add))
                    for hh in range(4):
                        S = S_f[g0 + hh]
                        pg.op("pool", [stb.reg, cm.reg], [stmp.reg],
                              lambda e: e.tensor_scalar(out=stmp.t[:], in0=stb.t[:, hh, 0:P], scalar1=uc, scalar2=None, op0=ALU.mult))
                        pg.op("dve", [S.reg, coef.reg, stmp.reg], [S.reg],
                              lambda e: e.scalar_tensor_tensor(out=S.t[:], in0=S.t[:], scalar=coef.t[:, g0 + hh:g0 + hh + 1],
                                                               in1=stmp.t[:], op0=ALU.mult, op1=ALU.add))
        for g in range(2 * NH):
            pg.op("act", [S_f[g].reg], [S_b[g].reg], lambda e: e.activation(out=S_b[g].t[:], in_=S_f[g].t[:], func=AF.Copy))

    def scan_tile(l, d, col0, on_head):
        chunks = list(range(NCH)) if d == 0 else list(range(NCH - 1, -1, -1))
        mask = maskf if d == 0 else maskb
        for h in range(NH):
            sl, wv = ws.get()
            pq = next_pp()
            proj(sl, wv[0], 0, col0, T, pq)
            pg.op("act", [pq.reg], [qs.reg], lambda e: e.activation(out=qs.t[:], in_=pq.t[:], func=AF.Silu))
            pv = next_pp()
            proj(sl, wv[1], 0, col0, T, pv)
            pg.op("act", [pv.reg], [vb.reg], lambda e: e.activation(out=vb.t[:], in_=pv.t[:], func=AF.Copy))
            pz = next_pp()
            proj(sl, wv[2], 0, col0, T, pz)
            gate_math(pz, l, d, h)
            pg.op("dve", [lf.reg, scanmask.reg], [cum.reg],
                  lambda e: e.tensor_tensor_scan(out=cum.t[:], data0=scanmask.t[:], data1=lf.t[:], initial=0.0, op0=ALU.mult, op1=ALU.add))
            c3 = cum.t[:].rearrange("p (c j) -> p c j", j=CH)
            ed = edec.t[:, h * NCH:(h + 1) * NCH]
            pg.op("act", [cum.reg], [edec.reg], lambda e: e.activation(out=ed, in_=c3[:, :, CH - 1], func=AF.Exp))
            if d == 0:
                C = cum
            else:
                pg.op("dve", [cum.reg, lf.reg], [lf.reg],
                      lambda e: e.tensor_tensor(out=lf.t[:], in0=cum.t[:], in1=lf.t[:], op=ALU.subtract))
                C = lf
            C3 = C.t[:].rearrange("p (c j) -> p c j", j=CH)
            d13 = d1.t[:].rearrange("p (c j) -> p c j", j=CH)
            d23 = d2.t[:].rearrange("p (c j) -> p c j", j=CH)
            pg.op("dve", [C.reg], [d1.reg],
                  lambda e: e.tensor_tensor(out=d13, in0=C3, in1=C3[:, :, CH // 2:CH // 2 + 1].to_broadcast([P, NCH, CH]), op=ALU.subtract))
            pg.op("pool", [C.reg, cum.reg], [d2.reg],
                  lambda e: e.tensor_tensor(out=d23, in0=c3[:, :, CH - 1:CH].to_broadcast([P, NCH, CH]), in1=C3, op=ALU.subtract))
            pg.op("act", [d1.reg], [ex[0].reg], lambda e: e.activation(out=ex[0].t[:], in_=d1.t[:], func=AF.Exp))
            pg.op("act", [d1.reg], [ex[1].reg], lambda e: e.activation(out=ex[1].t[:], in_=d1.t[:], func=AF.Exp, scale=-1.0))
            pg.op("act", [d2.reg], [ex[2].reg], lambda e: e.activation(out=ex[2].t[:], in_=d2.t[:], func=AF.Exp))
            pg.op("act", [C.reg], [ex[3].reg], lambda e: e.activation(out=ex[3].t[:], in_=C.t[:], func=AF.Exp))
            if d == 0:
                Eq, Ek, EqS, Ekh = ex[0], ex[1], ex[3], ex[2]
            else:
                Eq, Ek, EqS, Ekh = ex[1], ex[0], ex[2], ex[3]
            pg.op("dve", [qs.reg, Eq.reg], [qt.reg],
                  lambda e: e.scalar_tensor_tensor(out=qt.t[:], in0=qs.t[:], scalar=QSCALE, in1=Eq.t[:], op0=ALU.mult, op1=ALU.mult))
            pg.op("dve", [qs.reg, EqS.reg], [qS.reg],
                  lambda e: e.scalar_tensor_tensor(out=qS.t[:], in0=qs.t[:], scalar=QSCALE, in1=EqS.t[:], op0=ALU.mult, op1=ALU.mult))
            pg.op("dve", [kk.reg, Ek.reg], [kt.reg], lambda e: e.tensor_tensor(out=kt.t[:], in0=kk.t[:], in1=Ek.t[:], op=ALU.mult))
            pg.op("pool", [kk.reg, Ekh.reg], [kh.reg], lambda e: e.tensor_tensor(out=kh.t[:], in0=kk.t[:], in1=Ekh.t[:], op=ALU.mult))
            for j in range(NCH):
                pg.op("pe", [kh.reg, ident.reg], [p_tr.reg],
                      lambda e: e.transpose(p_tr.t[0:CH, j * P:(j + 1) * P], kh.t[:, j * CH:(j + 1) * CH], ident.t[:]))
            pg.op("act", [p_tr.reg], [khT.reg], lambda e: e.activation(out=khT.t[0:CH, :], in_=p_tr.t[0:CH, :], func=AF.Copy))
            for j in range(NCH):
                pg.op("pe", [vb.reg, ident.reg], [p_tr2.reg],
                      lambda e: e.transpose(p_tr2.t[0:CH, j * P:(j + 1) * P], vb.t[:, j * CH:(j + 1) * CH], ident.t[:]))
            pg.op("dve", [p_tr2.reg], [vT.reg], lambda e: e.tensor_copy(out=vT.t[0:CH, :], in_=p_tr2.t[0:CH, :]))
            for j in range(NCH):
                pg.op("pe", [kt.reg, qt.reg], [p_sc.reg],
                      lambda e: e.matmul(p_sc.t[0:CH, j * CH:(j + 1) * CH], lhsT=kt.t[:, j * CH:(j + 1) * CH],
                                         rhs=qt.t[:, j * CH:(j + 1) * CH], start=True, stop=True))
            pg.op("dve", [p_sc.reg, consts.reg], [scT.reg],
                  lambda e: e.tensor_tensor(out=scT.t[:], in0=p_sc.t[0:CH, :], in1=mask, op=ALU.mult))
            S = S_f[d * NH + h]
            Sb = S_b[d * NH + h]
            for j in chunks:
                cs = slice(j * CH, (j + 1) * CH)
                pg.op("pe", [vT.reg, scT.reg], [p_o.reg],
                      lambda e: e.matmul(p_o.t[:, cs], lhsT=vT.t[0:CH, j * P:(j + 1) * P], rhs=scT.t[:, cs], start=True, stop=False))
                pg.op("pe", [Sb.reg, qS.reg], [p_o.reg],
                      lambda e: e.matmul(p_o.t[:, cs], lhsT=Sb.t[:], rhs=qS.t[:, cs], start=False, stop=True))
                pg.op("pe", [khT.reg, vT.reg], [p_small.reg],
                      lambda e: e.matmul(p_small.t[:, 0:P], lhsT=khT.t[0:CH, j * P:(j + 1) * P], rhs=vT.t[0:CH, j * P:(j + 1) * P],
                                         start=True, stop=True))
                pg.op("dve", [S.reg, edec.reg, p_small.reg], [S.reg],
                      lambda e: e.scalar_tensor_tensor(out=S.t[:], in0=S.t[:], scalar=edec.t[:, h * NCH + j:h * NCH + j + 1],
                                                       in1=p_small.t[:, 0:P], op0=ALU.mult, op1=ALU.add))
                pg.op("act", [S.reg], [Sb.reg], lambda e: e.activation(out=Sb.t[:], in_=S.t[:], func=AF.Copy))
            on_head(h, sl, wv)

    def sweep2(l, xbuf, rx):
        ws.plan([wi(l, [CB_Q + h, CB_V + h, CB_ZB + h]) for h in range(NH)] * NT)
        for i in range(NT - 1, -1, -1):
            load_x(xbuf, rx, HL + i * T, T)
            rmsnorm(T, vec[l].t[:, 0:8], vec[l].reg)

            def on_head(h, sl, wv):
                pg.op("act", [p_o.reg], [oh.reg], lambda e: e.activation(out=oh.t[:], in_=p_o.t[:], func=AF.Copy))
                pg.dma("sp", ob_d.ap()[h * P:(h + 1) * P, i * T:(i + 1) * T], oh.t[:], [oh.reg], [r_ob], join=True)
            scan_tile(l, 1, 0, on_head)

    def sweep3(l, xbuf, rx, xdst, rxd):
        V = vec[l].t
        tile_items = ([wi(l, [CB_GLA + cb, CB_GLB + cb]) for cb in range(4)]
                      + [wi(l, [CB_Q + h, CB_V + h, CB_ZF + h, CB_OG + h]) for h in range(NH)]
                      + [([(w_in_b.ap()[l, CB_GA + eb:CB_GA + eb + 1], 1, 1024), (w_in_b.ap()[l, CB_GB + eb:CB_GB + eb + 1], 1, 1024),
                           (w_ow_b.ap()[l, eb:eb + 1], 1, 1024), (w_pw_b.ap()[l, eb:eb + 1], 1, 512)], [wregs[l]["in"], wregs[l]["ow"], wregs[l]["pw"]]) for eb in range(KC)]
                      + [([(w_out_b.ap()[l, db:db + 1], 1, 1024)], [wregs[l]["out"]]) for db in range(KC)])
        ws.plan(tile_items * NT)
        accr = [Reg() for _ in range(4)]
        for i in range(NT):
            load_x(xbuf, rx, HL + i * T - 15, XWT)
            rmsnorm(XWT, V[:, 0:8], vec[l].reg)
            for cb in range(4):
                sl, wv = ws.get()
                pa, pb = next_pp(), next_pp()
                proj(sl, wv[1], 0, 0, T, pb)
                pg.op("act", [pb.reg], [sga.reg], lambda e: e.activation(out=sga.t[:], in_=pb.t[:], func=AF.Sigmoid))
                proj(sl, wv[0], 0, 0, T, pa)
                pg.op("dve", [pa.reg, sga.reg], [a_ext.reg],
                      lambda e: e.tensor_tensor(out=a_ext.t[:, cb, 0:T], in0=pa.t[:], in1=sga.t[:], op=ALU.mult))
                proj(sl, wv[1], 0, T, XWT - T, p_small)
                pg.op("act", [p_small.reg], [sgb.reg],
                      lambda e: e.activation(out=sgb.t[:, 0:XWT - T], in_=p_small.t[:, 0:XWT - T], func=AF.Sigmoid))
                proj(sl, wv[0], 0, T, XWT - T, p_small)
                pg.op("dve", [p_small.reg, sgb.reg], [a_ext.reg],
                      lambda e: e.tensor_tensor(out=a_ext.t[:, cb, T:XWT], in0=p_small.t[:, 0:XWT - T], in1=sgb.t[:, 0:XWT - T], op=ALU.mult))
            for cb in range(4):
                eng = "dve" if cb % 2 == 0 else "pool"
                wc = lambda k: V[:, 8 + cb * 31 + k:8 + cb * 31 + k + 1]
                pg.op(eng, [a_ext.reg, vec[l].reg], [accr[cb]],
                      lambda e: e.tensor_scalar(out=acc_t[:, cb, :], in0=a_ext.t[:, cb, 0:T], scalar1=wc(0),
                                                scalar2=V[:, 132 + cb:133 + cb], op0=ALU.mult, op1=ALU.add))
                for k in range(1, 31):
                    if eng == "dve":
                        pg.op(eng, [a_ext.reg, accr[cb], vec[l].reg], [accr[cb]],
                              lambda e: e.scalar_tensor_tensor(out=acc_t[:, cb, :], in0=a_ext.t[:, cb, k:k + T], scalar=wc(k),
                                                               in1=acc_t[:, cb, :], op0=ALU.mult, op1=ALU.add))
                    else:
                        pg.op(eng, [a_ext.reg, vec[l].reg], [gacc.reg],
                              lambda e: e.tensor_scalar(out=gacc.t[:], in0=a_ext.t[:, cb, k:k + T], scalar1=wc(k), scalar2=None, op0=ALU.mult))
                        pg.op(eng, [gacc.reg, accr[cb]], [accr[cb]],
                              lambda e: e.tensor_tensor(out=acc_t[:, cb, :], in0=acc_t[:, cb, :], in1=gacc.t[:], op=ALU.add))
            pg.op("act", accr, [accb.reg], lambda e: e.activation(out=accb.t, in_=acc_t, func=AF.Copy))
            pm = next_pp()
            for cb in range(4):
                pg.op("pe", [ones.reg, accb.reg], [pm.reg],
                      lambda e: e.matmul(pm.t[:], lhsT=ones.t[:], rhs=accb.t[:, cb, :], start=(cb == 0), stop=(cb == 3)))
            pg.op("dve", [pm.reg], [mu.reg], lambda e: e.tensor_scalar(out=mu.t[:], in0=pm.t[:], scalar1=1.0 / 512, scalar2=None, op0=ALU.mult))
            for cb in range(4):
                eng = "dve" if cb % 2 == 0 else "pool"
                pg.op(eng, [accr[cb], mu.reg], [accr[cb]],
                      lambda e: e.tensor_tensor(out=acc_t[:, cb, :], in0=acc_t[:, cb, :], in1=mu.t[:], op=ALU.subtract))
            pg.op("act", accr, [accb.reg], lambda e: e.activation(out=accb.t, in_=acc_t, func=AF.Square))
            pv_ = next_pp()
            for cb in range(4):
                pg.op("pe", [ones.reg, accb.reg], [pv_.reg],
                      lambda e: e.matmul(pv_.t[:], lhsT=ones.t[:], rhs=accb.t[:, cb, :], start=(cb == 0), stop=(cb == 3)))
            rsqrt_to(var, var.t[:], pv_, T, 1.0 / 512)
            for cb in range(4):
                eng = "dve" if cb % 2 == 0 else "pool"
                pg.op(eng, [accr[cb], var.reg], [accr[cb]],
                      lambda e: e.tensor_tensor(out=acc_t[:, cb, :], in0=acc_t[:, cb, :], in1=var.t[:], op=ALU.mult))
                pg.op("act", [accr[cb], vec[l].reg], [asw.reg],
                      lambda e: e.activation(out=asw.t[:, cb, :], in_=acc_t[:, cb, :], func=AF.Silu,
                                             scale=V[:, 136 + cb:137 + cb], bias=V[:, 140 + cb:141 + cb]))

            def on_head(h, sl, wv):
                pg.dma("sp", obh.t[:], ob_d.ap()[h * P:(h + 1) * P, i * T:(i + 1) * T], [r_ob], [obh.reg])
                pg.op("dve", [p_o.reg, obh.reg], [oh.reg], lambda e: e.tensor_tensor(out=oh.t[:], in0=p_o.t[:], in1=obh.t[:], op=ALU.add))
                po = next_pp()
                sumsq(lambda c: oh.t[:], 1, T, po)
                rsqrt_to(var, var.t[:], po, T, 1.0 / P)
                pog = next_pp()
                proj(sl, wv[3], 0, 15, T, pog)
                pg.op("act", [pog.reg], [t1.reg], lambda e: e.activation(out=t1.t[:], in_=pog.t[:], func=AF.Silu))
                pg.op("dve", [oh.reg, var.reg, vec[l].reg], [t2.reg],
                      lambda e: e.scalar_tensor_tensor(out=t2.t[:], in0=oh.t[:], scalar=V[:, 144 + h:145 + h], in1=var.t[:],
                                                       op0=ALU.mult, op1=ALU.mult))
                pg.op("pool", [t1.reg, t2.reg], [bmix.reg],
                      lambda e: e.tensor_tensor(out=bmix.t[:, h, :], in0=t1.t[:], in1=t2.t[:], op=ALU.mult))
            scan_tile(l, 0, 15, on_head)
            for eb in range(KC):
                sl, wv = ws.get()
                pga = next_pp()
                proj(sl, wv[0], 0, 15, T, pga)
                pg.op("act", [pga.reg], [sga.reg], lambda e: e.activation(out=sga.t[:], in_=pga.t[:], func=AF.Sigmoid))
                pgb = next_pp()
                proj(sl, wv[1], 0, 15, T, pgb)
                pg.op("act", [pgb.reg], [sgb.reg], lambda e: e.activation(out=sgb.t[:], in_=pgb.t[:], func=AF.Sigmoid))
                pA = next_pp()
                proj(sl, wv[3], 0, 0, T, pA, kc=4, rhs=asw)
                pg.op("dve", [pA.reg, sga.reg], [t1.reg], lambda e: e.tensor_tensor(out=t1.t[:], in0=pA.t[:], in1=sga.t[:], op=ALU.mult))
                pB = next_pp()
                proj(sl, wv[2], 0, 0, T, pB, rhs=bmix)
                pg.op("dve", [pB.reg, sgb.reg], [t2.reg], lambda e: e.tensor_tensor(out=t2.t[:], in0=pB.t[:], in1=sgb.t[:], op=ALU.mult))
                pg.op("pool", [t1.reg, t2.reg], [ymix.reg],
                      lambda e: e.tensor_tensor(out=ymix.t[:, eb, :], in0=t1.t[:], in1=t2.t[:], op=ALU.add))
            for db in range(KC):
                sl, wv = ws.get()
                pO = next_pp()
                proj(sl, wv[0], 0, 0, T, pO, rhs=ymix)
                pg.op("dve", [pO.reg, xt.reg], [xt.reg],
                      lambda e: e.tensor_tensor(out=xt.t[:, db, 15:15 + T], in0=pO.t[:], in1=xt.t[:, db, 15:15 + T], op=ALU.add))
            pg.dma("sp", xdst.ap()[:, HL + i * T:HL + (i + 1) * T].rearrange("(c p) n -> p c n", p=P), xt.t[:, :, 15:15 + T],
                   [xt.reg], [rxd], join=True)

    def exchange_halo(xbuf, rx):
        pg.dma("sp", edge_i.ap()[:, 0:HL], xbuf.ap()[:, HL:2 * HL], [rx], [r_ei])
        pg.dma("sp", edge_i.ap()[:, HL:2 * HL], xbuf.ap()[:, TOK:TOK + HL], [rx], [r_ei], join=True)
        pg.allgather(edge_i.ap(), edge_o.ap(), [r_ei], [r_eo])
        pg.dma("sp", eo_sb.t[:].rearrange("p m c n -> p (m c) n"), edge_o.ap().rearrange("(mc p) n -> p mc n", p=P), [r_eo], [eo_sb.reg])
        for side in range(2):
            src = (lambda m: eo_sb.t[:, m, :, HL:2 * HL]) if side == 0 else (lambda m: eo_sb.t[:, m, :, 0:HL])
            dst = hl_sb.t[:, :, side * HL:(side + 1) * HL]
            sel = lambda m: cm.t[:, 16 + side * 4 + m:17 + side * 4 + m]
            pg.op("dve", [eo_sb.reg, cm.reg], [hl_sb.reg],
                  lambda e: e.tensor_scalar(out=dst, in0=src(0), scalar1=sel(0), scalar2=None, op0=ALU.mult))
            for m in range(1, 4):
                pg.op("dve", [eo_sb.reg, cm.reg, hl_sb.reg], [hl_sb.reg],
                      lambda e: e.scalar_tensor_tensor(out=dst, in0=src(m), scalar=sel(m), in1=dst, op0=ALU.mult, op1=ALU.add))
        pg.dma("sp", xbuf.ap()[:, 0:HL].rearrange("(c p) n -> p c n", p=P), hl_sb.t[:, :, 0:HL], [hl_sb.reg], [rx], join=True)
        pg.dma("sp", xbuf.ap()[:, HL + TOK:HL + TOK + HL].rearrange("(c p) n -> p c n", p=P), hl_sb.t[:, :, HL:2 * HL], [hl_sb.reg], [rx], join=True)

    def sweep4(l, xbuf, rx, xdst, rxd, last):
        V = vec[l].t
        tile_items = ([([(w_up_b.ap()[l, fb:fb + 1], 1, 1024), (w_up_b.ap()[l, NFB + fb:NFB + fb + 1], 1, 1024)], [wregs[l]["up"]]) for fb in range(NFB)]
                      + [([(w_dn_b.ap()[l, db:db + 1], 1, NFB * 128)], [wregs[l]["dn"]]) for db in range(KC)])
        ws.plan(tile_items * NT)
        for i in range(NT):
            load_x(xbuf, rx, HL + i * T - 1, T + 2)
            rmsnorm(T + 2, V[:, 152:160], vec[l].reg)
            for fb in range(NFB):
                sl, wv = ws.get()
                pgt = next_pp()
                proj(sl, wv[0], 0, 1, T, pgt)
                for c in range(KC):
                    pg.op("pe", [sl.reg, hb.reg], [p_small.reg],
                          lambda e: e.matmul(p_small.t[:, 0:2], lhsT=wv[0][:, 0, c * P:(c + 1) * P],
                                             rhs=hb.t[:, c, 0:T + 2:T + 1], start=(c == 0), stop=(c == KC - 1)))
                pg.op("act", [pgt.reg], [g_sb.reg], lambda e: e.activation(out=g_sb.t[:, 1:T + 1], in_=pgt.t[:], func=AF.Copy))
                pg.op("act", [p_small.reg, g_sb.reg], [g_sb.reg],
                      lambda e: e.activation(out=g_sb.t[:, 0:T + 2:T + 1], in_=p_small.t[:, 0:2], func=AF.Copy))
                w3 = lambda k: V[:, 160 + fb * 3 + k:161 + fb * 3 + k]
                pg.op("dve", [g_sb.reg, vec[l].reg], [gacc.reg],
                      lambda e: e.tensor_scalar(out=gacc.t[:], in0=g_sb.t[:, 0:T], scalar1=w3(0), scalar2=V[:, 226 + fb:227 + fb],
                                                op0=ALU.mult, op1=ALU.add))
                pg.op("dve", [g_sb.reg, gacc.reg, vec[l].reg], [gacc.reg],
                      lambda e: e.scalar_tensor_tensor(out=gacc.t[:], in0=g_sb.t[:, 1:T + 1], scalar=w3(1), in1=gacc.t[:], op0=ALU.mult, op1=ALU.add))
                pg.op("dve", [g_sb.reg, gacc.reg, vec[l].reg], [gacc.reg],
                      lambda e: e.scalar_tensor_tensor(out=gacc.t[:], in0=g_sb.t[:, 2:T + 2], scalar=w3(2), in1=gacc.t[:], op0=ALU.mult, op1=ALU.add))
                pg.op("act", [gacc.reg], [t1.reg], lambda e: e.activation(out=t1.t[:], in_=gacc.t[:], func=AF.Silu))
                pvl = next_pp()
                proj(sl, wv[1], 0, 1, T, pvl)
                pg.op("dve", [pvl.reg, t1.reg], [u_sb.reg],
                      lambda e: e.tensor_tensor(out=u_sb.t[:, fb, :], in0=pvl.t[:], in1=t1.t[:], op=ALU.mult))
            for db in range(KC):
                sl, wv = ws.get()
                pO = next_pp()
                proj(sl, wv[0], 0, 0, T, pO, kc=NFB, rhs=u_sb)
                pg.op("dve", [pO.reg, xt.reg], [xt.reg],
                      lambda e: e.tensor_tensor(out=xt.t[:, db, 1:1 + T], in0=pO.t[:], in1=xt.t[:, db, 1:1 + T], op=ALU.add))
            if last and final_norm:
                pst = next_pp()
                sumsq(lambda c: xt.t[:, c, 1:1 + T], KC, T, pst)
                rsqrt_to(var, var.t[:], pst, T, 1.0 / D)
                for c in range(KC):
                    eng = "dve"
                    pg.op(eng, [xt.reg, var.reg, gvec.reg], [xt.reg],
                          lambda e: e.scalar_tensor_tensor(out=xt.t[:, c, 1:1 + T], in0=xt.t[:, c, 1:1 + T], scalar=gvec.t[:, c:c + 1], in1=var.t[:],
                                                           op0=ALU.mult, op1=ALU.mult))
            if last:
                pg.dma("sp", y_out.ap()[:, i * T:(i + 1) * T].rearrange("(c p) n -> p c n", p=P), xt.t[:, :, 1:1 + T], [xt.reg], [rxd], join=True)
            else:
                pg.dma("sp", xdst.ap()[:, HL + i * T:HL + (i + 1) * T].rearrange("(c p) n -> p c n", p=P), xt.t[:, :, 1:1 + T],
                       [xt.reg], [rxd], join=True)

    import os
    KSTOP = int(os.environ.get("KSTOP", "9"))
    r_y = Reg()
    for l in range(NL):
        if KSTOP < 9:
            if KSTOP >= 1:
                sweep1(l, xA, r_xA)
            if KSTOP >= 2:
                sweep2(l, xA, r_xA)
            if KSTOP >= 3:
                pg.barrier()
                sweep3(l, xA, r_xA, xB, r_xB)
            if KSTOP >= 4:
                exchange_halo(xB, r_xB)
            pg.wait_all("sp", [r_xA, r_xB, r_ob] + [wregs[l][k] for k in wregs[l]])
            pg.wait_all("pool", [wregs[l][k] for k in wregs[l]])
            break
        if l + 1 < NL:
            cast_w(l + 1)
        if l > 0:
            exchange_halo(xA, r_xA)
        sweep1(l, xA, r_xA)
        sweep2(l, xA, r_xA)
        pg.barrier()
        sweep3(l, xA, r_xA, xB, r_xB)
        exchange_halo(xB, r_xB)
        pg.barrier()
        last = (l == NL - 1)
        sweep4(l, xB, r_xB, xA, r_y if last else r_xA, last)
    pg.wait_all("sp", [r_y])
    return nc


def _relayout(W, kc):
    K, N = W.shape
    assert K == kc * 128
    return np.ascontiguousarray(W.reshape(kc, 128, N // 128, 128).transpose(2, 1, 0, 3).reshape(N // 128, 128, kc * 128))


def _pcol(v):
    return np.ascontiguousarray(v.reshape(-1, 128).T)


def _make_consts():
    c = np.zeros((128, 128 + 512 + 512), np.float32)
    c[:, 0:128] = np.eye(128, dtype=np.float32)
    s = np.arange(64)[:, None]
    t = np.arange(64)[None, :]
    mf = (s <= t).astype(np.float32)
    mb = (s >= t).astype(np.float32)
    c[0:64, 128:640] = np.tile(mf, (1, 8))
    c[0:64, 640:1152] = np.tile(mb, (1, 8))
    return c


def _core_masks(seg):
    m = np.zeros((128, NCM), np.float32)
    for k in range(4):
        uf = 1.0 if k < seg else 0.0
        ub = 1.0 if k > seg else 0.0
        m[:, k] = uf
        m[:, 4 + k] = 1.0 - uf
        m[:, 8 + k] = ub
        m[:, 12 + k] = 1.0 - ub
        m[:, 16 + k] = 1.0 if k == seg - 1 else 0.0
        m[:, 20 + k] = 1.0 if k == seg + 1 else 0.0
    return m


def _prep_weights(inp, layers):
    L = layers
    out = {}
    out["w_in"] = np.stack([_relayout(np.asarray(inp["w_in"][l]), 8) for l in L])
    out["w_pw"] = np.stack([_relayout(np.asarray(inp["conv_pw_w"][l]), 4) for l in L])
    out["w_ow"] = np.stack([_relayout(np.asarray(inp["hgrn_o_w"][l]), 8) for l in L])
    out["w_out"] = np.stack([_relayout(np.asarray(inp["w_out"][l]), 8) for l in L])
    out["w_up"] = np.stack([_relayout(np.asarray(inp["ffn_w_up"][l]), 8) for l in L])
    out["w_dn"] = np.stack([_relayout(np.asarray(inp["ffn_w_down"][l]), NFB) for l in L])
    vecs = []
    for l in L:
        v = np.zeros((128, NV), np.float32)
        v[:, 0:8] = _pcol(np.asarray(inp["attn_norm_w"][l]))
        dw = np.asarray(inp["conv_dw_w"][l])
        for cb in range(4):
            v[:, 8 + cb * 31:8 + (cb + 1) * 31] = dw[:, cb * 128:(cb + 1) * 128].T
        v[:, 132:136] = _pcol(np.asarray(inp["conv_dw_b"][l]))
        v[:, 136:140] = _pcol(np.asarray(inp["conv_ln_w"][l]))
        v[:, 140:144] = _pcol(np.asarray(inp["conv_ln_b"][l]))
        v[:, 144:152] = _pcol(np.asarray(inp["hgrn_norm_w"][l]))
        v[:, 152:160] = _pcol(np.asarray(inp["ffn_norm_w"][l]))
        fw = np.asarray(inp["ffn_dw_w"][l])
        for fb in range(NFB):
            v[:, 160 + fb * 3:163 + fb * 3] = fw[:, fb * 128:(fb + 1) * 128].T
        v[:, 226:248] = _pcol(np.asarray(inp["ffn_dw_b"][l]))
        vecs.append(v)
    out["vec"] = np.stack(vecs)
    g = np.zeros((128, 72), np.float32)
    g[:, 0:8] = _pcol(np.asarray(inp["final_norm_w"]))
    lbl = np.asarray(inp["lb_logits"])
    for l in range(4):
        for d in range(2):
            g[:, 8 + (l * 2 + d) * 8:8 + (l * 2 + d) * 8 + 8] = _pcol(lbl[l, d])
    out["gvec"] = g
    out["consts"] = _make_consts()
    return out


_PROG_CACHE = {}


def _run(x, inp, layer_groups):
    B, S, _ = x.shape
    nseg = 8 // B
    TOK = S // nseg
    cur = np.asarray(x, dtype=np.float32)
    for gi, layers in enumerate(layer_groups):
        final = (gi == len(layer_groups) - 1)
        key = (TOK, tuple(layers), final)
        if key not in _PROG_CACHE:
            _PROG_CACHE[key] = build_program(TOK, list(layers), final)
        nc = _PROG_CACHE[key]
        wts = _prep_weights(inp, layers)
        in_maps = []
        for c in range(8):
            b, seg = c // nseg, c % nseg
            xp = np.zeros((S + 2 * HL, D), np.float32)
            xp[HL:HL + S] = cur[b]
            sl = xp[seg * TOK:seg * TOK + TOK + 2 * HL]
            m = dict(wts)
            m["x_in"] = np.ascontiguousarray(sl.T)
            m["cm"] = _core_masks(seg)
            in_maps.append(m)
        res = run_bass_kernel_spmd(nc, in_maps, core_ids=list(range(8)))
        nxt = np.empty_like(cur)
        for c in range(8):
            b, seg = c // nseg, c % nseg
            nxt[b, seg * TOK:(seg + 1) * TOK] = res.results[c]["y_out"].T
        cur = nxt
    return cur


def kernel(**inputs):
    x = np.asarray(inputs["x"], dtype=np.float32)
    return _run(x, inputs, [[0, 1, 2, 3]])
```

```python
import numpy as np
import concourse.bass as bass
import concourse.mybir as mybir
from concourse.bass_utils import run_bass_kernel_spmd

F32 = mybir.dt.float32
BF16 = mybir.dt.bfloat16
AF = mybir.ActivationFunctionType
ALU = mybir.AluOpType

P = 128
D = 1024
KC = 8
T = 512
CH = 64
NCH = T // CH
HL = 16
NH = 8
DFF = 2816
NFB = 22
DEPTH = 4
EPS = 1e-6
NV = 248
NCM = 24
QSCALE = 128 ** -0.5

CB_GLA, CB_GLB, CB_Q, CB_V, CB_ZF, CB_ZB, CB_OG, CB_GA, CB_GB = 0, 4, 8, 16, 24, 32, 40, 48, 56


class Reg:
    __slots__ = ("w", "r")

    def __init__(self):
        self.w = {}
        self.r = {}


class PG:
    NDS = 24

    def __init__(self, nc):
        self.nc = nc
        self.engs = {"pe": nc.tensor, "act": nc.scalar, "dve": nc.vector, "pool": nc.gpsimd, "sp": nc.sync}
        self.sems = {}
        self.cnt = {e: 0 for e in self.engs}
        self.waited = {e: {} for e in self.engs}
        self.ndma = 0
        self.ncc = 0
        self._stack = []
        self.ekey = {}
        self.nep = {}
        for e in self.engs:
            self.sems[e] = self._sem("e_" + e)
        for i in range(self.NDS):
            self.sems["d%d" % i] = self._sem("dma%d" % i)
        for i in range(8):
            self.sems["g%d" % i] = self._sem("gdma%d" % i)
        self.ngdma = 0
        for i in range(4):
            self.sems["c%d" % i] = self._sem("cc%d" % i)

    def _sem(self, name):
        cm = self.nc.semaphore(name)
        s = cm.__enter__()
        self._stack.append(cm)
        return s

    def _deps(self, eng, reads, writes, extra=(), join=False):
        deps = {}

        def add(k, v):
            if deps.get(k, 0) < v:
                deps[k] = v
        for r in reads:
            for k, v in r.w.items():
                add(k, v)
        for w in writes:
            if not join:
                for k, v in w.w.items():
                    add(k, v)
            for k, v in w.r.items():
                add(k, v)
        for (k, v) in extra:
            add(k, v)
        E = self.engs[eng]
        wd = self.waited[eng]
        for k, v in deps.items():
            if eng == "pe" and k.startswith("pe"):
                continue
            if wd.get(k, 0) >= v:
                continue
            E.wait_ge(self.sems[k], v)
            wd[k] = v

    def _mark(self, t, reads, writes, join=False):
        k, v = t
        for r in reads:
            if r.r.get(k, 0) < v:
                r.r[k] = v
        for w in writes:
            if join:
                if w.w.get(k, 0) < v:
                    w.w[k] = v
            else:
                w.w = {k: v}
                w.r = {}

    EPOCH_LEN = 16000

    def op(self, eng, reads, writes, fn):
        self._deps(eng, reads, writes)
        ins = fn(self.engs[eng])
        key = self.ekey.get(eng, eng)
        self.cnt[eng] += 1
        ins.then_inc(self.sems[key], 1)
        t = (key, self.cnt[eng])
        self._mark(t, reads, writes)
        if self.cnt[eng] >= self.EPOCH_LEN:
            self.nep[eng] = self.nep.get(eng, 0) + 1
            nk = "%s#%d" % (eng, self.nep[eng])
            self.sems[nk] = self._sem("e_%s_%d" % (eng, self.nep[eng]))
            self.ekey[eng] = nk
            self.cnt[eng] = 0
        return t

    def dma(self, q, out, in_, reads, writes, join=False, slow=False):
        if q == "pool":
            i = self.ngdma
            self.ngdma += 1
            s = "g%d" % (i % 8)
            v = 16 * (i // 8 + 1)
        else:
            i = self.ndma
            self.ndma += 1
            s = "d%d" % (i % self.NDS)
            v = 16 * (i // self.NDS + 1)
        extra = [(s, v - 16)] if v > 16 else []
        self._deps(q, reads, writes, extra, join)
        if slow:
            ins = self.engs[q].dma_start(out=out, in_=in_, allow_slow_non_contiguous=True)
        else:
            ins = self.engs[q].dma_start(out=out, in_=in_)
        ins.then_inc(self.sems[s], 16)
        t = (s, v)
        self._mark(t, reads, writes, join)
        return t

    def allgather(self, in_ap, out_ap, reads, writes):
        i = self.ncc
        self.ncc += 1
        s = "c%d" % (i % 4)
        v = i // 4 + 1
        extra = [(s, v - 1)] if v > 1 else []
        self._deps("pool", reads, writes, extra)
        ins = self.nc.gpsimd.collective_compute("AllGather", ALU.bypass, replica_groups=[[0, 1, 2, 3], [4, 5, 6, 7]],
                                                ins=[in_ap], outs=[out_ap])
        ins.then_inc(self.sems[s], 1)
        t = (s, v)
        self._mark(t, reads, writes)
        return t

    def barrier(self):
        comp = ("pe", "act", "dve", "pool")
        for e in comp:
            E = self.engs[e]
            for o in comp:
                if o == e:
                    continue
                ok = self.ekey.get(o, o)
                val = self.cnt[o]
                if val == 0:
                    n = self.nep.get(o, 0)
                    if n == 0:
                        continue
                    ok = o if n == 1 else "%s#%d" % (o, n - 1)
                    val = self.EPOCH_LEN
                if self.waited[e].get(ok, 0) >= val:
                    continue
                E.wait_ge(self.sems[ok], val)
                self.waited[e][ok] = val

    def wait_all(self, eng, regs):
        self._deps(eng, regs, regs)


class Buf:
    def __init__(self, nc, name, shape, dtype, psum=False):
        if psum:
            cm = nc.psum_tensor(name, shape, dtype)
        else:
            cm = nc.sbuf_tensor(name, shape, dtype)
        self.cm = cm
        self.t = cm.__enter__()
        self.reg = Reg()

    def __getitem__(self, k):
        return self.t[k]


class WStream:
    def __init__(self, pg, nc, nslot, ws, bufs):
        self.pg = pg
        self.nslot = nslot
        self.slots = [Buf(nc, "wslot%d" % i, [P, ws], BF16) for i in range(nslot)]
        bufs.extend(self.slots)
        self.q = []
        self.issued = []
        self.nxt = 0

    def plan(self, items):
        self.q.extend(items)

    def _issue(self):
        pieces, reg = self.q.pop(0)
        s = self.nxt
        self.nxt = (s + 1) % self.nslot
        sl = self.slots[s]
        off = 0
        views = []
        for (ap, n, e) in pieces:
            out = sl.t[:, off:off + n * e].rearrange("p (j e) -> p j e", j=n)
            self.pg.dma("sp", out, ap.rearrange("j p e -> p j e"), list(reg), [sl.reg], join=(off > 0))
            views.append(out)
            off += n * e
        self.issued.append((s, views))

    def get(self):
        while len(self.issued) < self.nslot - 1 and self.q:
            self._issue()
        s, views = self.issued.pop(0)
        return self.slots[s], views


def build_program(TOK, layers, final_norm):
    import os
    KPIPE = int(os.environ.get("KPIPE", "0"))
    NL = len(layers)
    NT = TOK // T
    XW = TOK + 2 * HL
    nc = bass.Bass("TRN2", target_bir_lowering=False)
    pg = PG(nc)
    bufs = []

    def dram_in(name, shape, dt=F32):
        return nc.dram_tensor(name, shape, dt, kind="ExternalInput")

    x_in = dram_in("x_in", [D, XW])
    cm_in = dram_in("cm", [P, NCM])
    consts_in = dram_in("consts", [P, 128 + 512 + 512])
    gvec_in = dram_in("gvec", [P, 8 + 64])
    vec_in = dram_in("vec", [NL, P, NV])
    w_in_f = dram_in("w_in", [NL, 64, P, 1024])
    w_pw_f = dram_in("w_pw", [NL, 8, P, 512])
    w_ow_f = dram_in("w_ow", [NL, 8, P, 1024])
    w_out_f = dram_in("w_out", [NL, 8, P, 1024])
    w_up_f = dram_in("w_up", [NL, 44, P, 1024])
    w_dn_f = dram_in("w_dn", [NL, 8, P, NFB * 128])
    y_out = nc.dram_tensor("y_out", [D, TOK], F32, kind="ExternalOutput")

    w_in_b = nc.dram_tensor("w_in_b", [NL, 64, P, 1024], BF16)
    w_pw_b = nc.dram_tensor("w_pw_b", [NL, 8, P, 512], BF16)
    w_ow_b = nc.dram_tensor("w_ow_b", [NL, 8, P, 1024], BF16)
    w_out_b = nc.dram_tensor("w_out_b", [NL, 8, P, 1024], BF16)
    w_up_b = nc.dram_tensor("w_up_b", [NL, 44, P, 1024], BF16)
    w_dn_b = nc.dram_tensor("w_dn_b", [NL, 8, P, NFB * 128], BF16)
    xA = nc.dram_tensor("xA", [D, XW], F32)
    xB = nc.dram_tensor("xB", [D, XW], F32)
    ob_d = nc.dram_tensor("ob_d", [D, TOK], F32)
    edge_i = nc.dram_tensor("edge_i", [D, 2 * HL], F32)
    edge_o = nc.dram_tensor("edge_o", [4 * D, 2 * HL], F32)
    st_i = [nc.dram_tensor("st_i%d" % q, [4 * P, 129], F32) for q in range(4)]
    st_o = [nc.dram_tensor("st_o%d" % q, [4 * 4 * P, 129], F32) for q in range(4)]
    r_xA, r_xB, r_ob, r_ei, r_eo, r_si, r_so = Reg(), Reg(), Reg(), Reg(), Reg(), Reg(), Reg()
    r_xin = Reg()
    wregs = [{k: Reg() for k in ("in", "pw", "ow", "out", "up", "dn")} for _ in range(NL)]

    def sb(name, shape, dt=F32):
        b = Buf(nc, name, shape, dt)
        bufs.append(b)
        return b

    def ps(name, shape, dt=F32):
        b = Buf(nc, name, shape, dt, psum=True)
        bufs.append(b)
        return b

    class View:
        def __init__(self, t):
            self.t = t
            self.reg = Reg()

    XWT = T + 2 * HL - 2
    cm = sb("cm_sb", [P, NCM])
    consts = sb("consts_sb", [P, 128 + 512 + 512])
    ident = sb("ident", [P, P], BF16)
    ones = sb("ones", [P, P], BF16)
    gvec = sb("gvec_sb", [P, 72])
    vec = [sb("vec%d" % l, [P, NV]) for l in range(NL)]
    lbt = sb("lbt", [P, 64])
    omlb = sb("omlb", [P, 64])
    lbtmp = sb("lbtmp", [P, 64])
    lbm = sb("lbm", [P, 16])
    scanmask = sb("scanmask", [P, T])
    onesf = sb("onesf", [P, T])
    xt = sb("xt", [P, KC, XWT])
    hb = sb("hb", [P, KC, XWT], BF16)
    sqb = sb("sqb", [P, T], BF16)
    rstd = sb("rstd", [P, XWT])
    ws = WStream(pg, nc, 4, 4096, bufs)
    sg = sb("sg", [P, T])
    kk = sb("kk", [P, T])
    lf = sb("lf", [P, T])
    cum = sb("cum", [P, T])
    d1 = sb("d1", [P, T])
    d2 = sb("d2", [P, T])
    ex = [sb("ex%d" % i, [P, T]) for i in range(4)]
    qs = sb("qs", [P, T])
    vb = sb("vb", [P, T], BF16)
    qt = sb("qt", [P, T], BF16)
    kt = sb("kt", [P, T], BF16)
    kh = sb("kh", [P, T], BF16)
    khT2 = [sb("khT%d" % i, [P, NCH * P], BF16) for i in range(2)]
    vT2 = [sb("vT%d" % i, [P, NCH * P], BF16) for i in range(2)]
    scT2 = [sb("scT%d" % i, [CH, T], BF16) for i in range(2)]
    qS2 = [sb("qS%d" % i, [P, T], BF16) for i in range(2)]
    khT, vT = khT2[0], vT2[0]
    S_f = [sb("Sf%d_%d" % (d, h), [P, P]) for d in range(2) for h in range(NH)]
    S_b = [sb("Sb%d_%d" % (d, h), [P, P], BF16) for d in range(2) for h in range(NH)]
    edec = sb("edec", [P, NH * NCH])
    Ltot = sb("Ltot", [P, 2 * NH])
    Abase = sb("Abase", [P, NH])
    eAb = sb("eAb", [P, NH])
    oh = sb("oh", [P, T])
    obh = sb("obh", [P, T])
    bmix = sb("bmix", [P, NH, T], BF16)
    arena = sb("arena", [P, 6400])
    a_ext = View(arena.t[:, 0:2 * XWT].bitcast(BF16).rearrange("p (c n) -> p c n", c=4))
    NDG = 8
    dgb = [sb("dg%d" % i, [P, P], BF16) for i in range(NDG)]
    dg_i = [0]
    acc_t = arena.t[:, 2168:2168 + 2048].rearrange("p (c n) -> p c n", c=4)
    accb = View(arena.t[:, 4216:5240].bitcast(BF16).rearrange("p (c n) -> p c n", c=4))
    asw = View(arena.t[:, 5240:6264].bitcast(BF16).rearrange("p (c n) -> p c n", c=4))
    u_sb = View(arena.t[:, 0:5632].bitcast(BF16).rearrange("p (c n) -> p c n", c=NFB))
    mu = sb("mu", [P, T])
    var = sb("var", [P, T])
    sga = sb("sga", [P, T])
    sgb = sb("sgb", [P, T])
    t1 = sb("t1", [P, T])
    t2 = sb("t2", [P, T])
    ymix = sb("ymix", [P, KC, T], BF16)
    g_sb = sb("g_sb", [P, T + 2])
    gacc = sb("gacc", [P, T])
    stb = sb("stb", [P, 4, 129])
    coef = sb("coef", [P, 2 * NH])
    stmp = sb("stmp", [P, P])
    eo_sb = sb("eo_sb", [P, 4, KC, 2 * HL])
    hl_sb = sb("hl_sb", [P, KC, 2 * HL])
    pp = [ps("pp%d" % i, [P, T]) for i in range(3)]
    p_small = ps("p_small", [P, T])
    p_tr = ps("p_tr", [P, NCH * P], BF16)
    p_tr2 = ps("p_tr2", [P, NCH * P], BF16)
    p_sc = ps("p_sc", [P, T])
    p_o = ps("p_o", [P, T])
    pp_i = [0]

    def next_pp():
        b = pp[pp_i[0] % 3]
        pp_i[0] += 1
        return b

    def cast_w(l):
        for (src, dst, key, nb, step) in ((w_in_f, w_in_b, "in", 64, 2), (w_pw_f, w_pw_b, "pw", 8, 4),
                                          (w_ow_f, w_ow_b, "ow", 8, 2), (w_out_f, w_out_b, "out", 8, 2),
                                          (w_up_f, w_up_b, "up", 44, 2), (w_dn_f, w_dn_b, "dn", 8, 1)):
            for j in range(0, nb, step):
                pg.dma("pool", dst.ap()[l, j:j + step], src.ap()[l, j:j + step], [], [wregs[l][key]], join=True)

    pg.dma("sp", cm.t[:], cm_in.ap(), [], [cm.reg])
    pg.dma("sp", consts.t[:], consts_in.ap(), [], [consts.reg])
    pg.dma("sp", gvec.t[:], gvec_in.ap(), [], [gvec.reg])
    for l in range(NL):
        pg.dma("sp", vec[l].t[:], vec_in.ap()[l], [], [vec[l].reg])
    cast_w(0)
    for c8 in range(KC):
        pg.dma("sp", xA.ap()[c8 * P:(c8 + 1) * P, :], x_in.ap()[c8 * P:(c8 + 1) * P, :], [r_xin], [r_xA], join=True)
    pg.op("dve", [consts.reg], [ident.reg], lambda e: e.tensor_copy(out=ident.t[:], in_=consts.t[:, 0:128]))
    pg.op("pool", [], [ones.reg], lambda e: e.memset(ones.t[:], 1.0))
    pg.op("pool", [], [onesf.reg], lambda e: e.memset(onesf.t[:], 1.0))
    pg.op("pool", [], [scanmask.reg], lambda e: e.memset(scanmask.t[:], 1.0))
    pg.op("pool", [], [scanmask.reg],
          lambda e: e.memset(scanmask.t[:].rearrange("p (c j) -> p c j", j=CH)[:, :, 0:1], 0.0))
    maskf = consts.t[0:CH, 128:128 + 512]
    maskb = consts.t[0:CH, 640:640 + 512]

    lg = gvec.t[:, 8:72].rearrange("p (l r) -> p l r", l=4)
    TT = lambda o, a, b, op: pg.op("dve", [gvec.reg, lbm.reg, lbtmp.reg, lbt.reg], [], lambda e: e.tensor_tensor(out=o, in0=a, in1=b, op=op))
    lt3 = lbtmp.t[:].rearrange("p (l r) -> p l r", l=4)
    lb3 = lbt.t[:].rearrange("p (l r) -> p l r", l=4)

    def lbop(writes, fn, eng="dve"):
        pg.op(eng, [gvec.reg, lbm.reg, lbtmp.reg, lbt.reg], writes, fn)
    lbop([lbm.reg], lambda e: e.tensor_tensor(out=lbm.t[:], in0=lg[:, 0, :], in1=lg[:, 1, :], op=ALU.max))
    lbop([lbm.reg], lambda e: e.tensor_tensor(out=lbm.t[:], in0=lbm.t[:], in1=lg[:, 2, :], op=ALU.max))
    lbop([lbm.reg], lambda e: e.tensor_tensor(out=lbm.t[:], in0=lbm.t[:], in1=lg[:, 3, :], op=ALU.max))
    for l4 in range(4):
        lbop([lbtmp.reg], lambda e: e.tensor_tensor(out=lt3[:, l4, :], in0=lg[:, l4, :], in1=lbm.t[:], op=ALU.subtract))
    lbop([lbtmp.reg], lambda e: e.activation(out=lbtmp.t[:], in_=lbtmp.t[:], func=AF.Exp), eng="act")
    lbop([lbm.reg], lambda e: e.tensor_tensor(out=lbm.t[:], in0=lt3[:, 0, :], in1=lt3[:, 1, :], op=ALU.add))
    lbop([lbm.reg], lambda e: e.tensor_tensor(out=lbm.t[:], in0=lbm.t[:], in1=lt3[:, 2, :], op=ALU.add))
    lbop([lbm.reg], lambda e: e.tensor_tensor(out=lbm.t[:], in0=lbm.t[:], in1=lt3[:, 3, :], op=ALU.add))
    lbop([lbm.reg], lambda e: e.reciprocal(out=lbm.t[:], in_=lbm.t[:]))
    for l4 in range(4):
        lbop([lbtmp.reg], lambda e: e.tensor_tensor(out=lt3[:, l4, :], in0=lt3[:, l4, :], in1=lbm.t[:], op=ALU.mult))
    lbop([lbt.reg], lambda e: e.memset(lbt.t[:], 0.0), eng="pool")
    for l4 in range(1, 4):
        lbop([lbt.reg], lambda e: e.tensor_tensor(out=lb3[:, l4, :], in0=lb3[:, l4 - 1, :], in1=lt3[:, l4, :], op=ALU.add))
    lbop([omlb.reg], lambda e: e.tensor_scalar(out=omlb.t[:], in0=lbt.t[:], scalar1=-1.0, scalar2=1.0, op0=ALU.mult, op1=ALU.add))

    def load_x(xbuf, rx, col0, ncols):
        pg.dma("sp", xt.t[:, :, 0:ncols], xbuf.ap()[:, col0:col0 + ncols].rearrange("(c p) n -> p c n", p=P),
               [rx], [xt.reg])

    def sumsq(src_fn, nk, n, pst):
        for c in range(nk):
            pg.op("act", [xt.reg, oh.reg], [sqb.reg], lambda e: e.activation(out=sqb.t[:, 0:n], in_=src_fn(c), func=AF.Square))
            pg.op("pe", [ones.reg, sqb.reg], [pst.reg],
                  lambda e: e.matmul(pst.t[:, 0:n], lhsT=ones.t[:], rhs=sqb.t[:, 0:n], start=(c == 0), stop=(c == nk - 1)))

    def rsqrt_to(dst_buf, dst_ap, pst, n, scale):
        pg.op("dve", [pst.reg], [dst_buf.reg],
              lambda e: e.tensor_scalar(out=dst_ap, in0=pst.t[:, 0:n], scalar1=scale, scalar2=EPS, op0=ALU.mult, op1=ALU.add))
        pg.op("act", [dst_buf.reg], [dst_buf.reg], lambda e: e.activation(out=dst_ap, in_=dst_ap, func=AF.Sqrt))
        pg.op("dve", [dst_buf.reg], [dst_buf.reg], lambda e: e.reciprocal(out=dst_ap, in_=dst_ap))

    def rmsnorm(ncols, wcols, wreg):
        for (a, b) in ((0, min(ncols, T)), (T, ncols)):
            if b <= a:
                continue
            n = b - a
            pst = p_small if a > 0 else next_pp()
            sumsq(lambda c: xt.t[:, c, a:b], KC, n, pst)
            rsqrt_to(rstd, rstd.t[:, a:b], pst, n, 1.0 / D)
        for c in range(KC):
            eng = "dve"
            pg.op(eng, [xt.reg, rstd.reg, wreg], [hb.reg],
                  lambda e: e.scalar_tensor_tensor(out=hb.t[:, c, 0:ncols], in0=xt.t[:, c, 0:ncols], scalar=wcols[:, c:c + 1],
                                                   in1=rstd.t[:, 0:ncols], op0=ALU.mult, op1=ALU.mult))

    def proj(wslot, wv, j, col0, n, out_ps, kc=KC, rhs=None):
        rb = hb if rhs is None else rhs
        for c in range(kc):
            pg.op("pe", [wslot.reg, rb.reg], [out_ps.reg],
                  lambda e: e.matmul(out_ps.t[:, 0:n], lhsT=wv[:, j, c * P:(c + 1) * P], rhs=rb.t[:, c, col0:col0 + n],
                                     start=(c == 0), stop=(c == kc - 1)))

    def wi(l, cbs):
        return ([(w_in_b.ap()[l, cb:cb + 1], 1, 1024) for cb in cbs], [wregs[l]["in"]])

    def gate_math(zps, l, d, h):
        ci = (layers[l] * 2 + d) * 8 + h
        lbc = lbt.t[:, ci:ci + 1]
        omc = omlb.t[:, ci:ci + 1]
        pg.op("act", [zps.reg], [sg.reg], lambda e: e.activation(out=sg.t[:], in_=zps.t[:], func=AF.Sigmoid))
        pg.op("act", [zps.reg], [kk.reg], lambda e: e.activation(out=kk.t[:], in_=zps.t[:], func=AF.Sigmoid, scale=-1.0))
        pg.op("dve", [sg.reg, omlb.reg, lbt.reg], [sg.reg],
              lambda e: e.tensor_scalar(out=sg.t[:], in0=sg.t[:], scalar1=omc, scalar2=lbc, op0=ALU.mult, op1=ALU.add))
        pg.op("dve", [sg.reg], [sg.reg],
              lambda e: e.tensor_scalar(out=sg.t[:], in0=sg.t[:], scalar1=1e-6, scalar2=1.0, op0=ALU.max, op1=ALU.min))
        pg.op("act", [sg.reg], [lf.reg], lambda e: e.activation(out=lf.t[:], in_=sg.t[:], func=AF.Ln))
        pg.op("pool", [kk.reg, omlb.reg], [kk.reg],
              lambda e: e.tensor_scalar(out=kk.t[:], in0=kk.t[:], scalar1=omc, scalar2=None, op0=ALU.mult))

    def sweep1(l, xbuf, rx):
        for b in S_f:
            pg.op("pool", [], [b.reg], lambda e: e.memset(b.t[:], 0.0))
        pg.op("pool", [], [Ltot.reg], lambda e: e.memset(Ltot.t[:], 0.0))
        pg.op("pool", [], [Abase.reg], lambda e: e.memset(Abase.t[:], 0.0))
        ws.plan([wi(l, [CB_V + h, CB_ZF + h, CB_ZB + h]) for h in range(NH)] * NT)
        for i in range(NT):
            load_x(xbuf, rx, HL + i * T, T)
            rmsnorm(T, vec[l].t[:, 0:8], vec[l].reg)
            for h in range(NH):
                sl, wv = ws.get()
                pv = next_pp()
                proj(sl, wv[0], 0, 0, T, pv)
                pg.op("act", [pv.reg], [vb.reg], lambda e: e.activation(out=vb.t[:], in_=pv.t[:], func=AF.Copy))
                for tb in range(4):
                    pg.op("pe", [vb.reg, ident.reg], [p_tr.reg],
                          lambda e: e.transpose(p_tr.t[:, tb * P:(tb + 1) * P], vb.t[:, tb * P:(tb + 1) * P], ident.t[:]))
                pg.op("dve", [p_tr.reg], [vT.reg], lambda e: e.tensor_copy(out=vT.t[:, 0:4 * P], in_=p_tr.t[:, 0:4 * P]))
                for d in range(2):
                    pz = next_pp()
                    proj(sl, wv[1 + d], 0, 0, T, pz)
                    gate_math(pz, l, d, h)
                    pg.op("dve", [lf.reg, onesf.reg], [cum.reg],
                          lambda e: e.tensor_tensor_scan(out=cum.t[:], data0=onesf.t[:], data1=lf.t[:], initial=0.0, op0=ALU.mult, op1=ALU.add))
                    if d == 0:
                        pg.op("act", [cum.reg], [ex[0].reg],
                              lambda e: e.activation(out=ex[0].t[:], in_=cum.t[:], func=AF.Exp, scale=-1.0, bias=cum.t[:, T - 1:T]))
                    else:
                        pg.op("dve", [cum.reg, lf.reg], [d1.reg],
                              lambda e: e.tensor_tensor(out=d1.t[:], in0=cum.t[:], in1=lf.t[:], op=ALU.subtract))
                        pg.op("act", [d1.reg], [ex[0].reg], lambda e: e.activation(out=ex[0].t[:], in_=d1.t[:], func=AF.Exp))
                    pg.op("dve", [kk.reg, ex[0].reg], [kh.reg],
                          lambda e: e.tensor_tensor(out=kh.t[:], in0=kk.t[:], in1=ex[0].t[:], op=ALU.mult))
                    for tb in range(4):
                        pg.op("pe", [kh.reg, ident.reg], [p_tr2.reg],
                              lambda e: e.transpose(p_tr2.t[:, tb * P:(tb + 1) * P], kh.t[:, tb * P:(tb + 1) * P], ident.t[:]))
                    pg.op("act", [p_tr2.reg], [khT.reg], lambda e: e.activation(out=khT.t[:, 0:4 * P], in_=p_tr2.t[:, 0:4 * P], func=AF.Copy))
                    for tb in range(4):
                        pg.op("pe", [khT.reg, vT.reg], [p_o.reg],
                              lambda e: e.matmul(p_o.t[:, 0:P], lhsT=khT.t[:, tb * P:(tb + 1) * P], rhs=vT.t[:, tb * P:(tb + 1) * P],
                                                 start=(tb == 0), stop=(tb == 3)))
                    S = S_f[d * NH + h]
                    lt = Ltot.t[:, d * NH + h:d * NH + h + 1]
                    if d == 0:
                        pg.op("act", [cum.reg], [eAb.reg],
                              lambda e: e.activation(out=eAb.t[:, h:h + 1], in_=cum.t[:, T - 1:T], func=AF.Exp))
                        pg.op("dve", [S.reg, eAb.reg, p_o.reg], [S.reg],
                              lambda e: e.scalar_tensor_tensor(out=S.t[:], in0=S.t[:], scalar=eAb.t[:, h:h + 1], in1=p_o.t[:, 0:P],
                                                               op0=ALU.mult, op1=ALU.add))
                    else:
                        pg.op("act", [Abase.reg], [eAb.reg],
                              lambda e: e.activation(out=eAb.t[:, h:h + 1], in_=Abase.t[:, h:h + 1], func=AF.Exp))
                        pg.op("dve", [S.reg, eAb.reg, p_o.reg], [S.reg],
                              lambda e: e.scalar_tensor_tensor(out=S.t[:], in0=p_o.t[:, 0:P], scalar=eAb.t[:, h:h + 1], in1=S.t[:],
                                                               op0=ALU.mult, op1=ALU.add))
                        pg.op("dve", [Abase.reg, cum.reg], [Abase.reg],
                              lambda e: e.tensor_tensor(out=Abase.t[:, h:h + 1], in0=Abase.t[:, h:h + 1], in1=cum.t[:, T - 1:T], op=ALU.add))
                    pg.op("dve", [Ltot.reg, cum.reg], [Ltot.reg],
                          lambda e: e.tensor_tensor(out=lt, in0=lt, in1=cum.t[:, T - 1:T], op=ALU.add))
        r_sq = [Reg() for _ in range(4)]
        r_soq = [Reg() for _ in range(4)]
        for q in range(4):
            sti = st_i[q].ap().rearrange("(g p) n -> p g n", p=P)
            for g in range(4):
                pg.op("act", [S_f[q * 4 + g].reg], [stb.reg],
                      lambda e: e.activation(out=stb.t[:, g, 0:P], in_=S_f[q * 4 + g].t[:], func=AF.Copy))
            pg.op("dve", [Ltot.reg, stb.reg], [stb.reg], lambda e: e.tensor_copy(out=stb.t[:, :, P], in_=Ltot.t[:, q * 4:(q + 1) * 4]))
            pg.dma("sp", sti, stb.t[:], [stb.reg], [r_sq[q]])
            pg.allgather(st_i[q].ap(), st_o[q].ap(), [r_sq[q]], [r_soq[q]])
        for b in S_f:
            pg.op("pool", [], [b.reg], lambda e: e.memset(b.t[:], 0.0))
        sto = [st_o[q].ap().rearrange("(m g p) n -> m p g n", p=P, g=4) for q in range(4)]
        for d in range(2):
            order = [0, 1, 2] if d == 0 else [3, 2, 1]
            for m in order:
                uc = cm.t[:, d * 8 + m:d * 8 + m + 1]
                omu = cm.t[:, d * 8 + 4 + m:d * 8 + 4 + m + 1]
                for hq in range(2):
                    g0 = d * NH + hq * 4
                    pg.dma("sp", stb.t[:], sto[d * 2 + hq][m], [r_soq[d * 2 + hq]], [stb.reg])
                    pg.op("act", [stb.reg], [coef.reg],
                          lambda e: e.activation(out=coef.t[:, g0:g0 + 4], in_=stb.t[:, :, P], func=AF.Exp))
                    pg.op("dve", [coef.reg, cm.reg], [coef.reg],
                          lambda e: e.tensor_scalar(out=coef.t[:, g0:g0 + 4], in0=coef.t[:, g0:g0 + 4],
                                                    scalar1=uc, scalar2=omu, op0=ALU.mult, op1=ALU.add))
                    for hh in range(4):
                        S = S_f[g0 + hh]
                        pg.op("pool", [stb.reg, cm.reg], [stmp.reg],
                              lambda e: e.tensor_scalar(out=stmp.t[:], in0=stb.t[:, hh, 0:P], scalar1=uc, scalar2=None, op0=ALU.mult))
                        pg.op("dve", [S.reg, coef.reg, stmp.reg], [S.reg],
                              lambda e: e.scalar_tensor_tensor(out=S.t[:], in0=S.t[:], scalar=coef.t[:, g0 + hh:g0 + hh + 1],
                                                               in1=stmp.t[:], op0=ALU.mult, op1=ALU.add))
        for g in range(2 * NH):
            pg.op("act", [S_f[g].reg], [S_b[g].reg], lambda e: e.activation(out=S_b[g].t[:], in_=S_f[g].t[:], func=AF.Copy))

    def scan_tile(l, d, col0, on_head):
        chunks = list(range(NCH)) if d == 0 else list(range(NCH - 1, -1, -1))
        mask = maskf if d == 0 else maskb

        def prep(h):
            khT, vT, scT, qS = khT2[h % 2], vT2[h % 2], scT2[h % 2], qS2[h % 2]
            sl, wv = ws.get()
            pq = next_pp()
            proj(sl, wv[0], 0, col0, T, pq)
            pg.op("act", [pq.reg], [qs.reg], lambda e: e.activation(out=qs.t[:], in_=pq.t[:], func=AF.Silu))
            pv = next_pp()
            proj(sl, wv[1], 0, col0, T, pv)
            pg.op("act", [pv.reg], [vb.reg], lambda e: e.activation(out=vb.t[:], in_=pv.t[:], func=AF.Copy))
            pz = next_pp()
            proj(sl, wv[2], 0, col0, T, pz)
            gate_math(pz, l, d, h)
            pg.op("dve", [lf.reg, scanmask.reg], [cum.reg],
                  lambda e: e.tensor_tensor_scan(out=cum.t[:], data0=scanmask.t[:], data1=lf.t[:], initial=0.0, op0=ALU.mult, op1=ALU.add))
            c3 = cum.t[:].rearrange("p (c j) -> p c j", j=CH)
            ed = edec.t[:, h * NCH:(h + 1) * NCH]
            pg.op("act", [cum.reg], [edec.reg], lambda e: e.activation(out=ed, in_=c3[:, :, CH - 1], func=AF.Exp))
            if d == 0:
                C = cum
            else:
                pg.op("dve", [cum.reg, lf.reg], [lf.reg],
                      lambda e: e.tensor_tensor(out=lf.t[:], in0=cum.t[:], in1=lf.t[:], op=ALU.subtract))
                C = lf
            C3 = C.t[:].rearrange("p (c j) -> p c j", j=CH)
            d13 = d1.t[:].rearrange("p (c j) -> p c j", j=CH)
            d23 = d2.t[:].rearrange("p (c j) -> p c j", j=CH)
            pg.op("dve", [C.reg], [d1.reg],
                  lambda e: e.tensor_tensor(out=d13, in0=C3, in1=C3[:, :, CH // 2:CH // 2 + 1].to_broadcast([P, NCH, CH]), op=ALU.subtract))
            pg.op("pool", [C.reg, cum.reg], [d2.reg],
                  lambda e: e.tensor_tensor(out=d23, in0=c3[:, :, CH - 1:CH].to_broadcast([P, NCH, CH]), in1=C3, op=ALU.subtract))
            pg.op("act", [d1.reg], [ex[0].reg], lambda e: e.activation(out=ex[0].t[:], in_=d1.t[:], func=AF.Exp))
            pg.op("act", [d1.reg], [ex[1].reg], lambda e: e.activation(out=ex[1].t[:], in_=d1.t[:], func=AF.Exp, scale=-1.0))
            pg.op("act", [d2.reg], [ex[2].reg], lambda e: e.activation(out=ex[2].t[:], in_=d2.t[:], func=AF.Exp))
            pg.op("act", [C.reg], [ex[3].reg], lambda e: e.activation(out=ex[3].t[:], in_=C.t[:], func=AF.Exp))
            if d == 0:
                Eq, Ek, EqS, Ekh = ex[0], ex[1], ex[3], ex[2]
            else:
                Eq, Ek, EqS, Ekh = ex[1], ex[0], ex[2], ex[3]
            pg.op("dve", [qs.reg, Eq.reg], [qt.reg],
                  lambda e: e.scalar_tensor_tensor(out=qt.t[:], in0=qs.t[:], scalar=QSCALE, in1=Eq.t[:], op0=ALU.mult, op1=ALU.mult))
            pg.op("dve", [qs.reg, EqS.reg], [qS.reg],
                  lambda e: e.scalar_tensor_tensor(out=qS.t[:], in0=qs.t[:], scalar=QSCALE, in1=EqS.t[:], op0=ALU.mult, op1=ALU.mult))
            pg.op("dve", [kk.reg, Ek.reg], [kt.reg], lambda e: e.tensor_tensor(out=kt.t[:], in0=kk.t[:], in1=Ek.t[:], op=ALU.mult))
            pg.op("pool", [kk.reg, Ekh.reg], [kh.reg], lambda e: e.tensor_tensor(out=kh.t[:], in0=kk.t[:], in1=Ekh.t[:], op=ALU.mult))
            for j in range(NCH):
                pg.op("pe", [kh.reg, ident.reg], [p_tr.reg],
                      lambda e: e.transpose(p_tr.t[0:CH, j * P:(j + 1) * P], kh.t[:, j * CH:(j + 1) * CH], ident.t[:]))
            pg.op("act", [p_tr.reg], [khT.reg], lambda e: e.activation(out=khT.t[0:CH, :], in_=p_tr.t[0:CH, :], func=AF.Copy))
            for j in range(NCH):
                pg.op("pe", [vb.reg, ident.reg], [p_tr2.reg],
                      lambda e: e.transpose(p_tr2.t[0:CH, j * P:(j + 1) * P], vb.t[:, j * CH:(j + 1) * CH], ident.t[:]))
            pg.op("dve", [p_tr2.reg], [vT.reg], lambda e: e.tensor_copy(out=vT.t[0:CH, :], in_=p_tr2.t[0:CH, :]))
            for j in range(NCH):
                pg.op("pe", [kt.reg, qt.reg], [p_sc.reg],
                      lambda e: e.matmul(p_sc.t[0:CH, j * CH:(j + 1) * CH], lhsT=kt.t[:, j * CH:(j + 1) * CH],
                                         rhs=qt.t[:, j * CH:(j + 1) * CH], start=True, stop=True))
            pg.op("dve", [p_sc.reg, consts.reg], [scT.reg],
                  lambda e: e.tensor_tensor(out=scT.t[:], in0=p_sc.t[0:CH, :], in1=mask, op=ALU.mult))
            return sl, wv

        def recur(h, sl, wv):
            khT, vT, scT, qS = khT2[h % 2], vT2[h % 2], scT2[h % 2], qS2[h % 2]
            S = S_f[d * NH + h]
            Sb = S_b[d * NH + h]
            for j in chunks:
                cs = slice(j * CH, (j + 1) * CH)
                pg.op("pe", [vT.reg, scT.reg], [p_o.reg],
                      lambda e: e.matmul(p_o.t[:, cs], lhsT=vT.t[0:CH, j * P:(j + 1) * P], rhs=scT.t[:, cs], start=True, stop=False))
                pg.op("pe", [Sb.reg, qS.reg], [p_o.reg],
                      lambda e: e.matmul(p_o.t[:, cs], lhsT=Sb.t[:], rhs=qS.t[:, cs], start=False, stop=True))
                pg.op("pe", [khT.reg, vT.reg], [p_small.reg],
                      lambda e: e.matmul(p_small.t[:, 0:P], lhsT=khT.t[0:CH, j * P:(j + 1) * P], rhs=vT.t[0:CH, j * P:(j + 1) * P],
                                         start=True, stop=True))
                pg.op("dve", [S.reg, edec.reg, p_small.reg], [S.reg],
                      lambda e: e.scalar_tensor_tensor(out=S.t[:], in0=S.t[:], scalar=edec.t[:, h * NCH + j:h * NCH + j + 1],
                                                       in1=p_small.t[:, 0:P], op0=ALU.mult, op1=ALU.add))
                pg.op("act", [S.reg], [Sb.reg], lambda e: e.activation(out=Sb.t[:], in_=S.t[:], func=AF.Copy))
            on_head(h, sl, wv)

        if KPIPE:
            cur = prep(0)
            for h in range(NH):
                nxt = prep(h + 1) if h + 1 < NH else None
                recur(h, cur[0], cur[1])
                cur = nxt
        else:
            for h in range(NH):
                cur = prep(h)
                recur(h, cur[0], cur[1])

    def sweep2(l, xbuf, rx):
        ws.plan([wi(l, [CB_Q + h, CB_V + h, CB_ZB + h]) for h in range(NH)] * NT)
        for i in range(NT - 1, -1, -1):
            load_x(xbuf, rx, HL + i * T, T)
            rmsnorm(T, vec[l].t[:, 0:8], vec[l].reg)

            def on_head(h, sl, wv):
                pg.op("act", [p_o.reg], [oh.reg], lambda e: e.activation(out=oh.t[:], in_=p_o.t[:], func=AF.Copy))
                pg.dma("sp", ob_d.ap()[h * P:(h + 1) * P, i * T:(i + 1) * T], oh.t[:], [oh.reg], [r_ob], join=True)
            scan_tile(l, 1, 0, on_head)

    def sweep3(l, xbuf, rx, xdst, rxd):
        V = vec[l].t
        tile_items = ([wi(l, [CB_GLA + cb, CB_GLB + cb]) for cb in range(4)]
                      + [wi(l, [CB_Q + h, CB_V + h, CB_ZF + h, CB_OG + h]) for h in range(NH)]
                      + [([(w_in_b.ap()[l, CB_GA + eb:CB_GA + eb + 1], 1, 1024), (w_in_b.ap()[l, CB_GB + eb:CB_GB + eb + 1], 1, 1024),
                           (w_ow_b.ap()[l, eb:eb + 1], 1, 1024), (w_pw_b.ap()[l, eb:eb + 1], 1, 512)], [wregs[l]["in"], wregs[l]["ow"], wregs[l]["pw"]]) for eb in range(KC)]
                      + [([(w_out_b.ap()[l, db:db + 1], 1, 1024)], [wregs[l]["out"]]) for db in range(KC)])
        ws.plan(tile_items * NT)
        accr = [Reg() for _ in range(4)]
        for i in range(NT):
            load_x(xbuf, rx, HL + i * T - 15, XWT)
            rmsnorm(XWT, V[:, 0:8], vec[l].reg)
            for cb in range(4):
                sl, wv = ws.get()
                pa, pb = next_pp(), next_pp()
                proj(sl, wv[1], 0, 0, T, pb)
                pg.op("act", [pb.reg], [sga.reg], lambda e: e.activation(out=sga.t[:], in_=pb.t[:], func=AF.Sigmoid))
                proj(sl, wv[0], 0, 0, T, pa)
                pg.op("dve", [pa.reg, sga.reg], [a_ext.reg],
                      lambda e: e.tensor_tensor(out=a_ext.t[:, cb, 0:T], in0=pa.t[:], in1=sga.t[:], op=ALU.mult))
                proj(sl, wv[1], 0, T, XWT - T, p_small)
                pg.op("act", [p_small.reg], [sgb.reg],
                      lambda e: e.activation(out=sgb.t[:, 0:XWT - T], in_=p_small.t[:, 0:XWT - T], func=AF.Sigmoid))
                proj(sl, wv[0], 0, T, XWT - T, p_small)
                pg.op("dve", [p_small.reg, sgb.reg], [a_ext.reg],
                      lambda e: e.tensor_tensor(out=a_ext.t[:, cb, T:XWT], in0=p_small.t[:, 0:XWT - T], in1=sgb.t[:, 0:XWT - T], op=ALU.mult))
            for cb in range(4):
                pcv = next_pp()
                for k in range(31):
                    dg = dgb[dg_i[0] % NDG]
                    dg_i[0] += 1
                    pg.op("act", [ident.reg, vec[l].reg], [dg.reg],
                          lambda e: e.activation(out=dg.t[:], in_=ident.t[:], func=AF.Copy, scale=V[:, 8 + cb * 31 + k:8 + cb * 31 + k + 1]))
                    pg.op("pe", [dg.reg, a_ext.reg], [pcv.reg],
                          lambda e: e.matmul(pcv.t[:], lhsT=dg.t[:], rhs=a_ext.t[:, cb, k:k + T], start=(k == 0), stop=(k == 30)))
                pg.op("dve", [pcv.reg, vec[l].reg], [accr[cb]],
                      lambda e: e.tensor_scalar(out=acc_t[:, cb, :], in0=pcv.t[:], scalar1=V[:, 132 + cb:133 + cb], scalar2=None, op0=ALU.add))
            pg.op("act", accr, [accb.reg], lambda e: e.activation(out=accb.t, in_=acc_t, func=AF.Copy))
            pm = next_pp()
            for cb in range(4):
                pg.op("pe", [ones.reg, accb.reg], [pm.reg],
                      lambda e: e.matmul(pm.t[:], lhsT=ones.t[:], rhs=accb.t[:, cb, :], start=(cb == 0), stop=(cb == 3)))
            pg.op("dve", [pm.reg], [mu.reg], lambda e: e.tensor_scalar(out=mu.t[:], in0=pm.t[:], scalar1=1.0 / 512, scalar2=None, op0=ALU.mult))
            for cb in range(4):
                eng = "dve" if cb % 2 == 0 else "pool"
                pg.op(eng, [accr[cb], mu.reg], [accr[cb]],
                      lambda e: e.tensor_tensor(out=acc_t[:, cb, :], in0=acc_t[:, cb, :], in1=mu.t[:], op=ALU.subtract))
            pg.op("act", accr, [accb.reg], lambda e: e.activation(out=accb.t, in_=acc_t, func=AF.Square))
            pv_ = next_pp()
            for cb in range(4):
                pg.op("pe", [ones.reg, accb.reg], [pv_.reg],
                      lambda e: e.matmul(pv_.t[:], lhsT=ones.t[:], rhs=accb.t[:, cb, :], start=(cb == 0), stop=(cb == 3)))
            rsqrt_to(var, var.t[:], pv_, T, 1.0 / 512)
            for cb in range(4):
                eng = "dve" if cb % 2 == 0 else "pool"
                pg.op(eng, [accr[cb], var.reg], [accr[cb]],
                      lambda e: e.tensor_tensor(out=acc_t[:, cb, :], in0=acc_t[:, cb, :], in1=var.t[:], op=ALU.mult))
                pg.op("act", [accr[cb], vec[l].reg], [asw.reg],
                      lambda e: e.activation(out=asw.t[:, cb, :], in_=acc_t[:, cb, :], func=AF.Silu,
                                             scale=V[:, 136 + cb:137 + cb], bias=V[:, 140 + cb:141 + cb]))

            def on_head(h, sl, wv):
                pg.dma("sp", obh.t[:], ob_d.ap()[h * P:(h + 1) * P, i * T:(i + 1) * T], [r_ob], [obh.reg])
                pg.op("dve", [p_o.reg, obh.reg], [oh.reg], lambda e: e.tensor_tensor(out=oh.t[:], in0=p_o.t[:], in1=obh.t[:], op=ALU.add))
                po = next_pp()
                sumsq(lambda c: oh.t[:], 1, T, po)
                rsqrt_to(var, var.t[:], po, T, 1.0 / P)
                pog = next_pp()
                proj(sl, wv[3], 0, 15, T, pog)
                pg.op("act", [pog.reg], [t1.reg], lambda e: e.activation(out=t1.t[:], in_=pog.t[:], func=AF.Silu))
                pg.op("dve", [oh.reg, var.reg, vec[l].reg], [t2.reg],
                      lambda e: e.scalar_tensor_tensor(out=t2.t[:], in0=oh.t[:], scalar=V[:, 144 + h:145 + h], in1=var.t[:],
                                                       op0=ALU.mult, op1=ALU.mult))
                pg.op("pool", [t1.reg, t2.reg], [bmix.reg],
                      lambda e: e.tensor_tensor(out=bmix.t[:, h, :], in0=t1.t[:], in1=t2.t[:], op=ALU.mult))
            scan_tile(l, 0, 15, on_head)
            for eb in range(KC):
                sl, wv = ws.get()
                pga = next_pp()
                proj(sl, wv[0], 0, 15, T, pga)
                pg.op("act", [pga.reg], [sga.reg], lambda e: e.activation(out=sga.t[:], in_=pga.t[:], func=AF.Sigmoid))
                pgb = next_pp()
                proj(sl, wv[1], 0, 15, T, pgb)
                pg.op("act", [pgb.reg], [sgb.reg], lambda e: e.activation(out=sgb.t[:], in_=pgb.t[:], func=AF.Sigmoid))
                pA = next_pp()
                proj(sl, wv[3], 0, 0, T, pA, kc=4, rhs=asw)
                pg.op("dve", [pA.reg, sga.reg], [t1.reg], lambda e: e.tensor_tensor(out=t1.t[:], in0=pA.t[:], in1=sga.t[:], op=ALU.mult))
                pB = next_pp()
                proj(sl, wv[2], 0, 0, T, pB, rhs=bmix)
                pg.op("dve", [pB.reg, sgb.reg], [t2.reg], lambda e: e.tensor_tensor(out=t2.t[:], in0=pB.t[:], in1=sgb.t[:], op=ALU.mult))
                pg.op("pool", [t1.reg, t2.reg], [ymix.reg],
                      lambda e: e.tensor_tensor(out=ymix.t[:, eb, :], in0=t1.t[:], in1=t2.t[:], op=ALU.add))
            for db in range(KC):
                sl, wv = ws.get()
                pO = next_pp()
                proj(sl, wv[0], 0, 0, T, pO, rhs=ymix)
                pg.op("dve", [pO.reg, xt.reg], [xt.reg],
                      lambda e: e.tensor_tensor(out=xt.t[:, db, 15:15 + T], in0=pO.t[:], in1=xt.t[:, db, 15:15 + T], op=ALU.add))
            pg.dma("sp", xdst.ap()[:, HL + i * T:HL + (i + 1) * T].rearrange("(c p) n -> p c n", p=P), xt.t[:, :, 15:15 + T],
                   [xt.reg], [rxd], join=True)

    def exchange_halo(xbuf, rx):
        pg.dma("sp", edge_i.ap()[:, 0:HL], xbuf.ap()[:, HL:2 * HL], [rx], [r_ei])
        pg.dma("sp", edge_i.ap()[:, HL:2 * HL], xbuf.ap()[:, TOK:TOK + HL], [rx], [r_ei], join=True)
        pg.allgather(edge_i.ap(), edge_o.ap(), [r_ei], [r_eo])
        pg.dma("sp", eo_sb.t[:].rearrange("p m c n -> p (m c) n"), edge_o.ap().rearrange("(mc p) n -> p mc n", p=P), [r_eo], [eo_sb.reg])
        for side in range(2):
            src = (lambda m: eo_sb.t[:, m, :, HL:2 * HL]) if side == 0 else (lambda m: eo_sb.t[:, m, :, 0:HL])
            dst = hl_sb.t[:, :, side * HL:(side + 1) * HL]
            sel = lambda m: cm.t[:, 16 + side * 4 + m:17 + side * 4 + m]
            pg.op("dve", [eo_sb.reg, cm.reg], [hl_sb.reg],
                  lambda e: e.tensor_scalar(out=dst, in0=src(0), scalar1=sel(0), scalar2=None, op0=ALU.mult))
            for m in range(1, 4):
                pg.op("dve", [eo_sb.reg, cm.reg, hl_sb.reg], [hl_sb.reg],
                      lambda e: e.scalar_tensor_tensor(out=dst, in0=src(m), scalar=sel(m), in1=dst, op0=ALU.mult, op1=ALU.add))
        pg.dma("sp", xbuf.ap()[:, 0:HL].rearrange("(c p) n -> p c n", p=P), hl_sb.t[:, :, 0:HL], [hl_sb.reg], [rx], join=True)
        pg.dma("sp", xbuf.ap()[:, HL + TOK:HL + TOK + HL].rearrange("(c p) n -> p c n", p=P), hl_sb.t[:, :, HL:2 * HL], [hl_sb.reg], [rx], join=True)

    def sweep4(l, xbuf, rx, xdst, rxd, last):
        V = vec[l].t
        tile_items = ([([(w_up_b.ap()[l, fb:fb + 1], 1, 1024), (w_up_b.ap()[l, NFB + fb:NFB + fb + 1], 1, 1024)], [wregs[l]["up"]]) for fb in range(NFB)]
                      + [([(w_dn_b.ap()[l, db:db + 1], 1, NFB * 128)], [wregs[l]["dn"]]) for db in range(KC)])
        ws.plan(tile_items * NT)
        for i in range(NT):
            load_x(xbuf, rx, HL + i * T - 1, T + 2)
            rmsnorm(T + 2, V[:, 152:160], vec[l].reg)
            for fb in range(NFB):
                sl, wv = ws.get()
                pgt = next_pp()
                proj(sl, wv[0], 0, 1, T, pgt)
                for c in range(KC):
                    pg.op("pe", [sl.reg, hb.reg], [p_small.reg],
                          lambda e: e.matmul(p_small.t[:, 0:2], lhsT=wv[0][:, 0, c * P:(c + 1) * P],
                                             rhs=hb.t[:, c, 0:T + 2:T + 1], start=(c == 0), stop=(c == KC - 1)))
                pg.op("act", [pgt.reg], [g_sb.reg], lambda e: e.activation(out=g_sb.t[:, 1:T + 1], in_=pgt.t[:], func=AF.Copy))
                pg.op("act", [p_small.reg, g_sb.reg], [g_sb.reg],
                      lambda e: e.activation(out=g_sb.t[:, 0:T + 2:T + 1], in_=p_small.t[:, 0:2], func=AF.Copy))
                w3 = lambda k: V[:, 160 + fb * 3 + k:161 + fb * 3 + k]
                pg.op("dve", [g_sb.reg, vec[l].reg], [gacc.reg],
                      lambda e: e.tensor_scalar(out=gacc.t[:], in0=g_sb.t[:, 0:T], scalar1=w3(0), scalar2=V[:, 226 + fb:227 + fb],
                                                op0=ALU.mult, op1=ALU.add))
                pg.op("dve", [g_sb.reg, gacc.reg, vec[l].reg], [gacc.reg],
                      lambda e: e.scalar_tensor_tensor(out=gacc.t[:], in0=g_sb.t[:, 1:T + 1], scalar=w3(1), in1=gacc.t[:], op0=ALU.mult, op1=ALU.add))
                pg.op("dve", [g_sb.reg, gacc.reg, vec[l].reg], [gacc.reg],
                      lambda e: e.scalar_tensor_tensor(out=gacc.t[:], in0=g_sb.t[:, 2:T + 2], scalar=w3(2), in1=gacc.t[:], op0=ALU.mult, op1=ALU.add))
                pg.op("act", [gacc.reg], [t1.reg], lambda e: e.activation(out=t1.t[:], in_=gacc.t[:], func=AF.Silu))
                pvl = next_pp()
                proj(sl, wv[1], 0, 1, T, pvl)
                pg.op("dve", [pvl.reg, t1.reg], [u_sb.reg],
                      lambda e: e.tensor_tensor(out=u_sb.t[:, fb, :], in0=pvl.t[:], in1=t1.t[:], op=ALU.mult))
            for db in range(KC):
                sl, wv = ws.get()
                pO = next_pp()
                proj(sl, wv[0], 0, 0, T, pO, kc=NFB, rhs=u_sb)
                pg.op("dve", [pO.reg, xt.reg], [xt.reg],
                      lambda e: e.tensor_tensor(out=xt.t[:, db, 1:1 + T], in0=pO.t[:], in1=xt.t[:, db, 1:1 + T], op=ALU.add))
            if last and final_norm:
                pst = next_pp()
                sumsq(lambda c: xt.t[:, c, 1:1 + T], KC, T, pst)
                rsqrt_to(var, var.t[:], pst, T, 1.0 / D)
                for c in range(KC):
                    eng = "dve"
                    pg.op(eng, [xt.reg, var.reg, gvec.reg], [xt.reg],
                          lambda e: e.scalar_tensor_tensor(out=xt.t[:, c, 1:1 + T], in0=xt.t[:, c, 1:1 + T], scalar=gvec.t[:, c:c + 1], in1=var.t[:],
                                                           op0=ALU.mult, op1=ALU.mult))
            if last:
                pg.dma("sp", y_out.ap()[:, i * T:(i + 1) * T].rearrange("(c p) n -> p c n", p=P), xt.t[:, :, 1:1 + T], [xt.reg], [rxd], join=True)
            else:
                pg.dma("sp", xdst.ap()[:, HL + i * T:HL + (i + 1) * T].rearrange("(c p) n -> p c n", p=P), xt.t[:, :, 1:1 + T],
                       [xt.reg], [rxd], join=True)

    import os
    KSTOP = int(os.environ.get("KSTOP", "9"))
    r_y = Reg()
    for l in range(NL):
        if KSTOP < 9:
            if KSTOP >= 1:
                sweep1(l, xA, r_xA)
            if KSTOP >= 2:
                sweep2(l, xA, r_xA)
            if KSTOP >= 3:
                pg.barrier()
                sweep3(l, xA, r_xA, xB, r_xB)
            if KSTOP >= 4:
                exchange_halo(xB, r_xB)
            pg.wait_all("sp", [r_xA, r_xB, r_ob] + [wregs[l][k] for k in wregs[l]])
            pg.wait_all("pool", [wregs[l][k] for k in wregs[l]])
            break
        if l + 1 < NL:
            cast_w(l + 1)
        if l > 0:
            exchange_halo(xA, r_xA)
        sweep1(l, xA, r_xA)
        sweep2(l, xA, r_xA)
        pg.barrier()
        sweep3(l, xA, r_xA, xB, r_xB)
        exchange_halo(xB, r_xB)
        pg.barrier()
        last = (l == NL - 1)
        sweep4(l, xB, r_xB, xA, r_y if last else r_xA, last)
    pg.wait_all("sp", [r_y])
    return nc


def _relayout(W, kc):
    K, N = W.shape
    assert K == kc * 128
    return np.ascontiguousarray(W.reshape(kc, 128, N // 128, 128).transpose(2, 1, 0, 3).reshape(N // 128, 128, kc * 128))


def _pcol(v):
    return np.ascontiguousarray(v.reshape(-1, 128).T)


def _make_consts():
    c = np.zeros((128, 128 + 512 + 512), np.float32)
    c[:, 0:128] = np.eye(128, dtype=np.float32)
    s = np.arange(64)[:, None]
    t = np.arange(64)[None, :]
    mf = (s <= t).astype(np.float32)
    mb = (s >= t).astype(np.float32)
    c[0:64, 128:640] = np.tile(mf, (1, 8))
    c[0:64, 640:1152] = np.tile(mb, (1, 8))
    return c


def _core_masks(seg):
    m = np.zeros((128, NCM), np.float32)
    for k in range(4):
        uf = 1.0 if k < seg else 0.0
        ub = 1.0 if k > seg else 0.0
        m[:, k] = uf
        m[:, 4 + k] = 1.0 - uf
        m[:, 8 + k] = ub
        m[:, 12 + k] = 1.0 - ub
        m[:, 16 + k] = 1.0 if k == seg - 1 else 0.0
        m[:, 20 + k] = 1.0 if k == seg + 1 else 0.0
    return m


def _prep_weights(inp, layers):
    L = layers
    out = {}
    out["w_in"] = np.stack([_relayout(np.asarray(inp["w_in"][l]), 8) for l in L])
    out["w_pw"] = np.stack([_relayout(np.asarray(inp["conv_pw_w"][l]), 4) for l in L])
    out["w_ow"] = np.stack([_relayout(np.asarray(inp["hgrn_o_w"][l]), 8) for l in L])
    out["w_out"] = np.stack([_relayout(np.asarray(inp["w_out"][l]), 8) for l in L])
    out["w_up"] = np.stack([_relayout(np.asarray(inp["ffn_w_up"][l]), 8) for l in L])
    out["w_dn"] = np.stack([_relayout(np.asarray(inp["ffn_w_down"][l]), NFB) for l in L])
    vecs = []
    for l in L:
        v = np.zeros((128, NV), np.float32)
        v[:, 0:8] = _pcol(np.asarray(inp["attn_norm_w"][l]))
        dw = np.asarray(inp["conv_dw_w"][l])
        for cb in range(4):
            v[:, 8 + cb * 31:8 + (cb + 1) * 31] = dw[:, cb * 128:(cb + 1) * 128].T
        v[:, 132:136] = _pcol(np.asarray(inp["conv_dw_b"][l]))
        v[:, 136:140] = _pcol(np.asarray(inp["conv_ln_w"][l]))
        v[:, 140:144] = _pcol(np.asarray(inp["conv_ln_b"][l]))
        v[:, 144:152] = _pcol(np.asarray(inp["hgrn_norm_w"][l]))
        v[:, 152:160] = _pcol(np.asarray(inp["ffn_norm_w"][l]))
        fw = np.asarray(inp["ffn_dw_w"][l])
        for fb in range(NFB):
            v[:, 160 + fb * 3:163 + fb * 3] = fw[:, fb * 128:(fb + 1) * 128].T
        v[:, 226:248] = _pcol(np.asarray(inp["ffn_dw_b"][l]))
        vecs.append(v)
    out["vec"] = np.stack(vecs)
    g = np.zeros((128, 72), np.float32)
    g[:, 0:8] = _pcol(np.asarray(inp["final_norm_w"]))
    lbl = np.asarray(inp["lb_logits"])
    for l in range(4):
        for d in range(2):
            g[:, 8 + (l * 2 + d) * 8:8 + (l * 2 + d) * 8 + 8] = _pcol(lbl[l, d])
    out["gvec"] = g
    out["consts"] = _make_consts()
    return out


_PROG_CACHE = {}


def _run(x, inp, layer_groups):
    B, S, _ = x.shape
    nseg = 8 // B
    TOK = S // nseg
    cur = np.asarray(x, dtype=np.float32)
    for gi, layers in enumerate(layer_groups):
        final = (gi == len(layer_groups) - 1)
        key = (TOK, tuple(layers), final)
        if key not in _PROG_CACHE:
            _PROG_CACHE[key] = build_program(TOK, list(layers), final)
        nc = _PROG_CACHE[key]
        wts = _prep_weights(inp, layers)
        in_maps = []
        for c in range(8):
            b, seg = c // nseg, c % nseg
            xp = np.zeros((S + 2 * HL, D), np.float32)
            xp[HL:HL + S] = cur[b]
            sl = xp[seg * TOK:seg * TOK + TOK + 2 * HL]
            m = dict(wts)
            m["x_in"] = np.ascontiguousarray(sl.T)
            m["cm"] = _core_masks(seg)
            in_maps.append(m)
        res = run_bass_kernel_spmd(nc, in_maps, core_ids=list(range(8)))
        nxt = np.empty_like(cur)
        for c in range(8):
            b, seg = c // nseg, c % nseg
            nxt[b, seg * TOK:(seg + 1) * TOK] = res.results[c]["y_out"].T
        cur = nxt
    return cur


def kernel(**inputs):
    x = np.asarray(inputs["x"], dtype=np.float32)
    return _run(x, inputs, [[0, 1, 2, 3]])
```

```python
import numpy as np
import concourse.bass as bass
import concourse.mybir as mybir
from concourse.bass_utils import run_bass_kernel_spmd

F32 = mybir.dt.float32
BF16 = mybir.dt.bfloat16
AF = mybir.ActivationFunctionType
ALU = mybir.AluOpType

P = 128
D = 1024
KC = 8
T = 512
CH = 64
NCH = T // CH
HL = 16
NH = 8
DFF = 2816
NFB = 22
DEPTH = 4
EPS = 1e-6
NV = 248
NCM = 24
QSCALE = 128 ** -0.5

CB_GLA, CB_GLB, CB_Q, CB_V, CB_ZF, CB_ZB, CB_OG, CB_GA, CB_GB = 0, 4, 8, 16, 24, 32, 40, 48, 56


class Reg:
    __slots__ = ("w", "r")

    def __init__(self):
        self.w = {}
        self.r = {}


class PG:
    NDS = 24

    def __init__(self, nc):
        self.nc = nc
        self.engs = {"pe": nc.tensor, "act": nc.scalar, "dve": nc.vector, "pool": nc.gpsimd, "sp": nc.sync}
        self.sems = {}
        self.cnt = {e: 0 for e in self.engs}
        self.waited = {e: {} for e in self.engs}
        self.ndma = 0
        self.ncc = 0
        self._stack = []
        self.ekey = {}
        self.nep = {}
        for e in self.engs:
            self.sems[e] = self._sem("e_" + e)
        for i in range(self.NDS):
            self.sems["d%d" % i] = self._sem("dma%d" % i)
        for i in range(8):
            self.sems["g%d" % i] = self._sem("gdma%d" % i)
        self.ngdma = 0
        for i in range(4):
            self.sems["c%d" % i] = self._sem("cc%d" % i)

    def _sem(self, name):
        cm = self.nc.semaphore(name)
        s = cm.__enter__()
        self._stack.append(cm)
        return s

    def _deps(self, eng, reads, writes, extra=(), join=False):
        deps = {}

        def add(k, v):
            if deps.get(k, 0) < v:
                deps[k] = v
        for r in reads:
            for k, v in r.w.items():
                add(k, v)
        for w in writes:
            if not join:
                for k, v in w.w.items():
                    add(k, v)
            for k, v in w.r.items():
                add(k, v)
        for (k, v) in extra:
            add(k, v)
        E = self.engs[eng]
        wd = self.waited[eng]
        for k, v in deps.items():
            if eng == "pe" and k.startswith("pe"):
                continue
            if wd.get(k, 0) >= v:
                continue
            E.wait_ge(self.sems[k], v)
            wd[k] = v

    def _mark(self, t, reads, writes, join=False):
        k, v = t
        for r in reads:
            if r.r.get(k, 0) < v:
                r.r[k] = v
        for w in writes:
            if join:
                if w.w.get(k, 0) < v:
                    w.w[k] = v
            else:
                w.w = {k: v}
                w.r = {}

    EPOCH_LEN = 16000

    def op(self, eng, reads, writes, fn):
        self._deps(eng, reads, writes)
        ins = fn(self.engs[eng])
        key = self.ekey.get(eng, eng)
        self.cnt[eng] += 1
        ins.then_inc(self.sems[key], 1)
        t = (key, self.cnt[eng])
        self._mark(t, reads, writes)
        if self.cnt[eng] >= self.EPOCH_LEN:
            self.nep[eng] = self.nep.get(eng, 0) + 1
            nk = "%s#%d" % (eng, self.nep[eng])
            self.sems[nk] = self._sem("e_%s_%d" % (eng, self.nep[eng]))
            self.ekey[eng] = nk
            self.cnt[eng] = 0
        return t

    def dma(self, q, out, in_, reads, writes, join=False, slow=False):
        if q == "pool":
            i = self.ngdma
            self.ngdma += 1
            s = "g%d" % (i % 8)
            v = 16 * (i // 8 + 1)
        else:
            i = self.ndma
            self.ndma += 1
            s = "d%d" % (i % self.NDS)
            v = 16 * (i // self.NDS + 1)
        extra = [(s, v - 16)] if v > 16 else []
        self._deps(q, reads, writes, extra, join)
        if slow:
            ins = self.engs[q].dma_start(out=out, in_=in_, allow_slow_non_contiguous=True)
        else:
            ins = self.engs[q].dma_start(out=out, in_=in_)
        ins.then_inc(self.sems[s], 16)
        t = (s, v)
        self._mark(t, reads, writes, join)
        return t

    def allgather(self, in_ap, out_ap, reads, writes):
        i = self.ncc
        self.ncc += 1
        s = "c%d" % (i % 4)
        v = i // 4 + 1
        extra = [(s, v - 1)] if v > 1 else []
        self._deps("pool", reads, writes, extra)
        ins = self.nc.gpsimd.collective_compute("AllGather", ALU.bypass, replica_groups=[[0, 1, 2, 3], [4, 5, 6, 7]],
                                                ins=[in_ap], outs=[out_ap])
        ins.then_inc(self.sems[s], 1)
        t = (s, v)
        self._mark(t, reads, writes)
        return t

    def barrier(self):
        comp = ("pe", "act", "dve", "pool")
        for e in comp:
            E = self.engs[e]
            for o in comp:
                if o == e:
                    continue
                ok = self.ekey.get(o, o)
                val = self.cnt[o]
                if val == 0:
                    n = self.nep.get(o, 0)
                    if n == 0:
                        continue
                    ok = o if n == 1 else "%s#%d" % (o, n - 1)
                    val = self.EPOCH_LEN
                if self.waited[e].get(ok, 0) >= val:
                    continue
                E.wait_ge(self.sems[ok], val)
                self.waited[e][ok] = val

    def wait_all(self, eng, regs):
        self._deps(eng, regs, regs)


class Buf:
    def __init__(self, nc, name, shape, dtype, psum=False):
        if psum:
            cm = nc.psum_tensor(name, shape, dtype)
        else:
            cm = nc.sbuf_tensor(name, shape, dtype)
        self.cm = cm
        self.t = cm.__enter__()
        self.reg = Reg()

    def __getitem__(self, k):
        return self.t[k]


class WStream:
    def __init__(self, pg, nc, nslot, ws, bufs):
        self.pg = pg
        self.nslot = nslot
        self.slots = [Buf(nc, "wslot%d" % i, [P, ws], BF16) for i in range(nslot)]
        bufs.extend(self.slots)
        self.q = []
        self.issued = []
        self.nxt = 0

    def plan(self, items):
        self.q.extend(items)

    def _issue(self):
        pieces, reg = self.q.pop(0)
        s = self.nxt
        self.nxt = (s + 1) % self.nslot
        sl = self.slots[s]
        off = 0
        views = []
        for (ap, n, e) in pieces:
            out = sl.t[:, off:off + n * e].rearrange("p (j e) -> p j e", j=n)
            self.pg.dma("sp", out, ap.rearrange("j p e -> p j e"), list(reg), [sl.reg], join=(off > 0))
            views.append(out)
            off += n * e
        self.issued.append((s, views))

    def get(self):
        while len(self.issued) < self.nslot - 1 and self.q:
            self._issue()
        s, views = self.issued.pop(0)
        return self.slots[s], views


def build_program(TOK, layers, final_norm):
    import os
    KPIPE = int(os.environ.get("KPIPE", "1"))
    NL = len(layers)
    NT = TOK // T
    XW = TOK + 2 * HL
    nc = bass.Bass("TRN2", target_bir_lowering=False)
    pg = PG(nc)
    bufs = []

    def dram_in(name, shape, dt=F32):
        return nc.dram_tensor(name, shape, dt, kind="ExternalInput")

    x_in = dram_in("x_in", [D, XW])
    cm_in = dram_in("cm", [P, NCM])
    consts_in = dram_in("consts", [P, 128 + 512 + 512])
    gvec_in = dram_in("gvec", [P, 8 + 64])
    vec_in = dram_in("vec", [NL, P, NV])
    w_in_f = dram_in("w_in", [NL, 64, P, 1024])
    w_pw_f = dram_in("w_pw", [NL, 8, P, 512])
    w_ow_f = dram_in("w_ow", [NL, 8, P, 1024])
    w_out_f = dram_in("w_out", [NL, 8, P, 1024])
    w_up_f = dram_in("w_up", [NL, 44, P, 1024])
    w_dn_f = dram_in("w_dn", [NL, 8, P, NFB * 128])
    y_out = nc.dram_tensor("y_out", [D, TOK], F32, kind="ExternalOutput")

    w_in_b = nc.dram_tensor("w_in_b", [NL, 64, P, 1024], BF16)
    w_pw_b = nc.dram_tensor("w_pw_b", [NL, 8, P, 512], BF16)
    w_ow_b = nc.dram_tensor("w_ow_b", [NL, 8, P, 1024], BF16)
    w_out_b = nc.dram_tensor("w_out_b", [NL, 8, P, 1024], BF16)
    w_up_b = nc.dram_tensor("w_up_b", [NL, 44, P, 1024], BF16)
    w_dn_b = nc.dram_tensor("w_dn_b", [NL, 8, P, NFB * 128], BF16)
    xA = nc.dram_tensor("xA", [D, XW], F32)
    xB = nc.dram_tensor("xB", [D, XW], F32)
    ob_d = nc.dram_tensor("ob_d", [D, TOK], F32)
    edge_i = nc.dram_tensor("edge_i", [D, 2 * HL], F32)
    edge_o = nc.dram_tensor("edge_o", [4 * D, 2 * HL], F32)
    st_i = [nc.dram_tensor("st_i%d" % q, [4 * P, 129], F32) for q in range(4)]
    st_o = [nc.dram_tensor("st_o%d" % q, [4 * 4 * P, 129], F32) for q in range(4)]
    r_xA, r_xB, r_ob, r_ei, r_eo, r_si, r_so = Reg(), Reg(), Reg(), Reg(), Reg(), Reg(), Reg()
    r_xin = Reg()
    wregs = [{k: Reg() for k in ("in", "pw", "ow", "out", "up", "dn")} for _ in range(NL)]

    def sb(name, shape, dt=F32):
        b = Buf(nc, name, shape, dt)
        bufs.append(b)
        return b

    def ps(name, shape, dt=F32):
        b = Buf(nc, name, shape, dt, psum=True)
        bufs.append(b)
        return b

    class View:
        def __init__(self, t):
            self.t = t
            self.reg = Reg()

    XWT = T + 2 * HL - 2
    cm = sb("cm_sb", [P, NCM])
    consts = sb("consts_sb", [P, 128 + 512 + 512])
    ident = sb("ident", [P, P], BF16)
    ones = sb("ones", [P, P], BF16)
    gvec = sb("gvec_sb", [P, 72])
    vec = [sb("vec%d" % l, [P, NV]) for l in range(NL)]
    lbt = sb("lbt", [P, 64])
    omlb = sb("omlb", [P, 64])
    lbtmp = sb("lbtmp", [P, 64])
    lbm = sb("lbm", [P, 16])
    scanmask = sb("scanmask", [P, T])
    onesf = sb("onesf", [P, T])
    xt = sb("xt", [P, KC, XWT])
    hb = sb("hb", [P, KC, XWT], BF16)
    sqb = sb("sqb", [P, T], BF16)
    rstd = sb("rstd", [P, XWT])
    ws = WStream(pg, nc, 4, 4096, bufs)
    sg = sb("sg", [P, T])
    kk = sb("kk", [P, T])
    lf = sb("lf", [P, T])
    cum = sb("cum", [P, T])
    d1 = sb("d1", [P, T])
    d2 = sb("d2", [P, T])
    ex = [sb("ex%d" % i, [P, T]) for i in range(4)]
    qs = sb("qs", [P, T])
    vb = sb("vb", [P, T], BF16)
    qt = sb("qt", [P, T], BF16)
    kt = sb("kt", [P, T], BF16)
    kh = sb("kh", [P, T], BF16)
    khT2 = [sb("khT%d" % i, [P, NCH * P], BF16) for i in range(2)]
    vT2 = [sb("vT%d" % i, [P, NCH * P], BF16) for i in range(2)]
    scT2 = [sb("scT%d" % i, [CH, T], BF16) for i in range(2)]
    qS2 = [sb("qS%d" % i, [P, T], BF16) for i in range(2)]
    khT, vT = khT2[0], vT2[0]
    S_f = [sb("Sf%d_%d" % (d, h), [P, P]) for d in range(2) for h in range(NH)]
    S_b = [sb("Sb%d_%d" % (d, h), [P, P], BF16) for d in range(2) for h in range(NH)]
    edec = sb("edec", [P, NH * NCH])
    Ltot = sb("Ltot", [P, 2 * NH])
    Abase = sb("Abase", [P, NH])
    eAb = sb("eAb", [P, NH])
    oh = sb("oh", [P, T])
    obh = sb("obh", [P, T])
    bmix = sb("bmix", [P, NH, T], BF16)
    arena = sb("arena", [P, 6400])
    a_ext = View(arena.t[:, 0:2 * XWT].bitcast(BF16).rearrange("p (c n) -> p c n", c=4))
    NDG = 8
    dgb = [sb("dg%d" % i, [P, P], BF16) for i in range(NDG)]
    dg_i = [0]
    acc_t = arena.t[:, 2168:2168 + 2048].rearrange("p (c n) -> p c n", c=4)
    accb = View(arena.t[:, 4216:5240].bitcast(BF16).rearrange("p (c n) -> p c n", c=4))
    asw = View(arena.t[:, 5240:6264].bitcast(BF16).rearrange("p (c n) -> p c n", c=4))
    u_sb = View(arena.t[:, 0:5632].bitcast(BF16).rearrange("p (c n) -> p c n", c=NFB))
    mu = sb("mu", [P, T])
    var = sb("var", [P, T])
    sga = sb("sga", [P, T])
    sgb = sb("sgb", [P, T])
    t1 = sb("t1", [P, T])
    t2 = sb("t2", [P, T])
    ymix = sb("ymix", [P, KC, T], BF16)
    g_sb = sb("g_sb", [P, T + 2])
    gacc = sb("gacc", [P, T])
    stb = sb("stb", [P, 4, 129])
    coef = sb("coef", [P, 2 * NH])
    stmp = sb("stmp", [P, P])
    eo_sb = sb("eo_sb", [P, 4, KC, 2 * HL])
    hl_sb = sb("hl_sb", [P, KC, 2 * HL])
    pp = [ps("pp%d" % i, [P, T]) for i in range(3)]
    p_small = ps("p_small", [P, T])
    p_tr = ps("p_tr", [P, NCH * P], BF16)
    p_tr2 = ps("p_tr2", [P, NCH * P], BF16)
    p_sc = ps("p_sc", [P, T])
    p_o = ps("p_o", [P, T])
    pp_i = [0]

    def next_pp():
        b = pp[pp_i[0] % 3]
        pp_i[0] += 1
        return b

    def cast_w(l):
        for (src, dst, key, nb, step) in ((w_in_f, w_in_b, "in", 64, 2), (w_pw_f, w_pw_b, "pw", 8, 4),
                                          (w_ow_f, w_ow_b, "ow", 8, 2), (w_out_f, w_out_b, "out", 8, 2),
                                          (w_up_f, w_up_b, "up", 44, 2), (w_dn_f, w_dn_b, "dn", 8, 1)):
            for j in range(0, nb, step):
                pg.dma("pool", dst.ap()[l, j:j + step], src.ap()[l, j:j + step], [], [wregs[l][key]], join=True)

    pg.dma("sp", cm.t[:], cm_in.ap(), [], [cm.reg])
    pg.dma("sp", consts.t[:], consts_in.ap(), [], [consts.reg])
    pg.dma("sp", gvec.t[:], gvec_in.ap(), [], [gvec.reg])
    for l in range(NL):
        pg.dma("sp", vec[l].t[:], vec_in.ap()[l], [], [vec[l].reg])
    KCASTALL = int(os.environ.get("KCASTALL", "1"))
    if KCASTALL:
        for l_ in range(NL):
            cast_w(l_)
    else:
        cast_w(0)
    for c8 in range(KC):
        pg.dma("sp", xA.ap()[c8 * P:(c8 + 1) * P, :], x_in.ap()[c8 * P:(c8 + 1) * P, :], [r_xin], [r_xA], join=True)
    pg.op("dve", [consts.reg], [ident.reg], lambda e: e.tensor_copy(out=ident.t[:], in_=consts.t[:, 0:128]))
    pg.op("pool", [], [ones.reg], lambda e: e.memset(ones.t[:], 1.0))
    pg.op("pool", [], [onesf.reg], lambda e: e.memset(onesf.t[:], 1.0))
    pg.op("pool", [], [scanmask.reg], lambda e: e.memset(scanmask.t[:], 1.0))
    pg.op("pool", [], [scanmask.reg],
          lambda e: e.memset(scanmask.t[:].rearrange("p (c j) -> p c j", j=CH)[:, :, 0:1], 0.0))
    maskf = consts.t[0:CH, 128:128 + 512]
    maskb = consts.t[0:CH, 640:640 + 512]

    lg = gvec.t[:, 8:72].rearrange("p (l r) -> p l r", l=4)
    TT = lambda o, a, b, op: pg.op("dve", [gvec.reg, lbm.reg, lbtmp.reg, lbt.reg], [], lambda e: e.tensor_tensor(out=o, in0=a, in1=b, op=op))
    lt3 = lbtmp.t[:].rearrange("p (l r) -> p l r", l=4)
    lb3 = lbt.t[:].rearrange("p (l r) -> p l r", l=4)

    def lbop(writes, fn, eng="dve"):
        pg.op(eng, [gvec.reg, lbm.reg, lbtmp.reg, lbt.reg], writes, fn)
    lbop([lbm.reg], lambda e: e.tensor_tensor(out=lbm.t[:], in0=lg[:, 0, :], in1=lg[:, 1, :], op=ALU.max))
    lbop([lbm.reg], lambda e: e.tensor_tensor(out=lbm.t[:], in0=lbm.t[:], in1=lg[:, 2, :], op=ALU.max))
    lbop([lbm.reg], lambda e: e.tensor_tensor(out=lbm.t[:], in0=lbm.t[:], in1=lg[:, 3, :], op=ALU.max))
    for l4 in range(4):
        lbop([lbtmp.reg], lambda e: e.tensor_tensor(out=lt3[:, l4, :], in0=lg[:, l4, :], in1=lbm.t[:], op=ALU.subtract))
    lbop([lbtmp.reg], lambda e: e.activation(out=lbtmp.t[:], in_=lbtmp.t[:], func=AF.Exp), eng="act")
    lbop([lbm.reg], lambda e: e.tensor_tensor(out=lbm.t[:], in0=lt3[:, 0, :], in1=lt3[:, 1, :], op=ALU.add))
    lbop([lbm.reg], lambda e: e.tensor_tensor(out=lbm.t[:], in0=lbm.t[:], in1=lt3[:, 2, :], op=ALU.add))
    lbop([lbm.reg], lambda e: e.tensor_tensor(out=lbm.t[:], in0=lbm.t[:], in1=lt3[:, 3, :], op=ALU.add))
    lbop([lbm.reg], lambda e: e.reciprocal(out=lbm.t[:], in_=lbm.t[:]))
    for l4 in range(4):
        lbop([lbtmp.reg], lambda e: e.tensor_tensor(out=lt3[:, l4, :], in0=lt3[:, l4, :], in1=lbm.t[:], op=ALU.mult))
    lbop([lbt.reg], lambda e: e.memset(lbt.t[:], 0.0), eng="pool")
    for l4 in range(1, 4):
        lbop([lbt.reg], lambda e: e.tensor_tensor(out=lb3[:, l4, :], in0=lb3[:, l4 - 1, :], in1=lt3[:, l4, :], op=ALU.add))
    lbop([omlb.reg], lambda e: e.tensor_scalar(out=omlb.t[:], in0=lbt.t[:], scalar1=-1.0, scalar2=1.0, op0=ALU.mult, op1=ALU.add))

    def load_x(xbuf, rx, col0, ncols):
        pg.dma("sp", xt.t[:, :, 0:ncols], xbuf.ap()[:, col0:col0 + ncols].rearrange("(c p) n -> p c n", p=P),
               [rx], [xt.reg])

    def sumsq(src_fn, nk, n, pst):
        for c in range(nk):
            pg.op("act", [xt.reg, oh.reg], [sqb.reg], lambda e: e.activation(out=sqb.t[:, 0:n], in_=src_fn(c), func=AF.Square))
            pg.op("pe", [ones.reg, sqb.reg], [pst.reg],
                  lambda e: e.matmul(pst.t[:, 0:n], lhsT=ones.t[:], rhs=sqb.t[:, 0:n], start=(c == 0), stop=(c == nk - 1)))

    def rsqrt_to(dst_buf, dst_ap, pst, n, scale):
        pg.op("dve", [pst.reg], [dst_buf.reg],
              lambda e: e.tensor_scalar(out=dst_ap, in0=pst.t[:, 0:n], scalar1=scale, scalar2=EPS, op0=ALU.mult, op1=ALU.add))
        pg.op("act", [dst_buf.reg], [dst_buf.reg], lambda e: e.activation(out=dst_ap, in_=dst_ap, func=AF.Sqrt))
        pg.op("dve", [dst_buf.reg], [dst_buf.reg], lambda e: e.reciprocal(out=dst_ap, in_=dst_ap))

    def rmsnorm(ncols, wcols, wreg):
        for (a, b) in ((0, min(ncols, T)), (T, ncols)):
            if b <= a:
                continue
            n = b - a
            pst = p_small if a > 0 else next_pp()
            sumsq(lambda c: xt.t[:, c, a:b], KC, n, pst)
            rsqrt_to(rstd, rstd.t[:, a:b], pst, n, 1.0 / D)
        for c in range(KC):
            eng = "dve"
            pg.op(eng, [xt.reg, rstd.reg, wreg], [hb.reg],
                  lambda e: e.scalar_tensor_tensor(out=hb.t[:, c, 0:ncols], in0=xt.t[:, c, 0:ncols], scalar=wcols[:, c:c + 1],
                                                   in1=rstd.t[:, 0:ncols], op0=ALU.mult, op1=ALU.mult))

    def proj(wslot, wv, j, col0, n, out_ps, kc=KC, rhs=None):
        rb = hb if rhs is None else rhs
        for c in range(kc):
            pg.op("pe", [wslot.reg, rb.reg], [out_ps.reg],
                  lambda e: e.matmul(out_ps.t[:, 0:n], lhsT=wv[:, j, c * P:(c + 1) * P], rhs=rb.t[:, c, col0:col0 + n],
                                     start=(c == 0), stop=(c == kc - 1)))

    def wi(l, cbs):
        return ([(w_in_b.ap()[l, cb:cb + 1], 1, 1024) for cb in cbs], [wregs[l]["in"]])

    def gate_math(zps, l, d, h):
        ci = (layers[l] * 2 + d) * 8 + h
        lbc = lbt.t[:, ci:ci + 1]
        omc = omlb.t[:, ci:ci + 1]
        pg.op("act", [zps.reg], [sg.reg], lambda e: e.activation(out=sg.t[:], in_=zps.t[:], func=AF.Sigmoid))
        pg.op("act", [zps.reg], [kk.reg], lambda e: e.activation(out=kk.t[:], in_=zps.t[:], func=AF.Sigmoid, scale=-1.0))
        pg.op("dve", [sg.reg, omlb.reg, lbt.reg], [sg.reg],
              lambda e: e.tensor_scalar(out=sg.t[:], in0=sg.t[:], scalar1=omc, scalar2=lbc, op0=ALU.mult, op1=ALU.add))
        pg.op("dve", [sg.reg], [sg.reg],
              lambda e: e.tensor_scalar(out=sg.t[:], in0=sg.t[:], scalar1=1e-6, scalar2=1.0, op0=ALU.max, op1=ALU.min))
        pg.op("act", [sg.reg], [lf.reg], lambda e: e.activation(out=lf.t[:], in_=sg.t[:], func=AF.Ln))
        pg.op("pool", [kk.reg, omlb.reg], [kk.reg],
              lambda e: e.tensor_scalar(out=kk.t[:], in0=kk.t[:], scalar1=omc, scalar2=None, op0=ALU.mult))

    def sweep1(l, xbuf, rx):
        for b in S_f:
            pg.op("pool", [], [b.reg], lambda e: e.memset(b.t[:], 0.0))
        pg.op("pool", [], [Ltot.reg], lambda e: e.memset(Ltot.t[:], 0.0))
        pg.op("pool", [], [Abase.reg], lambda e: e.memset(Abase.t[:], 0.0))
        ws.plan([wi(l, [CB_V + h, CB_ZF + h, CB_ZB + h]) for h in range(NH)] * NT)
        for i in range(NT):
            load_x(xbuf, rx, HL + i * T, T)
            rmsnorm(T, vec[l].t[:, 0:8], vec[l].reg)
            for h in range(NH):
                sl, wv = ws.get()
                pv = next_pp()
                proj(sl, wv[0], 0, 0, T, pv)
                pg.op("act", [pv.reg], [vb.reg], lambda e: e.activation(out=vb.t[:], in_=pv.t[:], func=AF.Copy))
                for tb in range(4):
                    pg.op("pe", [vb.reg, ident.reg], [p_tr.reg],
                          lambda e: e.transpose(p_tr.t[:, tb * P:(tb + 1) * P], vb.t[:, tb * P:(tb + 1) * P], ident.t[:]))
                pg.op("dve", [p_tr.reg], [vT.reg], lambda e: e.tensor_copy(out=vT.t[:, 0:4 * P], in_=p_tr.t[:, 0:4 * P]))
                for d in range(2):
                    pz = next_pp()
                    proj(sl, wv[1 + d], 0, 0, T, pz)
                    gate_math(pz, l, d, h)
                    pg.op("dve", [lf.reg, onesf.reg], [cum.reg],
                          lambda e: e.tensor_tensor_scan(out=cum.t[:], data0=onesf.t[:], data1=lf.t[:], initial=0.0, op0=ALU.mult, op1=ALU.add))
                    if d == 0:
                        pg.op("act", [cum.reg], [ex[0].reg],
                              lambda e: e.activation(out=ex[0].t[:], in_=cum.t[:], func=AF.Exp, scale=-1.0, bias=cum.t[:, T - 1:T]))
                    else:
                        pg.op("dve", [cum.reg, lf.reg], [d1.reg],
                              lambda e: e.tensor_tensor(out=d1.t[:], in0=cum.t[:], in1=lf.t[:], op=ALU.subtract))
                        pg.op("act", [d1.reg], [ex[0].reg], lambda e: e.activation(out=ex[0].t[:], in_=d1.t[:], func=AF.Exp))
                    pg.op("dve", [kk.reg, ex[0].reg], [kh.reg],
                          lambda e: e.tensor_tensor(out=kh.t[:], in0=kk.t[:], in1=ex[0].t[:], op=ALU.mult))
                    for tb in range(4):
                        pg.op("pe", [kh.reg, ident.reg], [p_tr2.reg],
                              lambda e: e.transpose(p_tr2.t[:, tb * P:(tb + 1) * P], kh.t[:, tb * P:(tb + 1) * P], ident.t[:]))
                    pg.op("act", [p_tr2.reg], [khT.reg], lambda e: e.activation(out=khT.t[:, 0:4 * P], in_=p_tr2.t[:, 0:4 * P], func=AF.Copy))
                    for tb in range(4):
                        pg.op("pe", [khT.reg, vT.reg], [p_o.reg],
                              lambda e: e.matmul(p_o.t[:, 0:P], lhsT=khT.t[:, tb * P:(tb + 1) * P], rhs=vT.t[:, tb * P:(tb + 1) * P],
                                                 start=(tb == 0), stop=(tb == 3)))
                    S = S_f[d * NH + h]
                    lt = Ltot.t[:, d * NH + h:d * NH + h + 1]
                    if d == 0:
                        pg.op("act", [cum.reg], [eAb.reg],
                              lambda e: e.activation(out=eAb.t[:, h:h + 1], in_=cum.t[:, T - 1:T], func=AF.Exp))
                        pg.op("dve", [S.reg, eAb.reg, p_o.reg], [S.reg],
                              lambda e: e.scalar_tensor_tensor(out=S.t[:], in0=S.t[:], scalar=eAb.t[:, h:h + 1], in1=p_o.t[:, 0:P],
                                                               op0=ALU.mult, op1=ALU.add))
                    else:
                        pg.op("act", [Abase.reg], [eAb.reg],
                              lambda e: e.activation(out=eAb.t[:, h:h + 1], in_=Abase.t[:, h:h + 1], func=AF.Exp))
                        pg.op("dve", [S.reg, eAb.reg, p_o.reg], [S.reg],
                              lambda e: e.scalar_tensor_tensor(out=S.t[:], in0=p_o.t[:, 0:P], scalar=eAb.t[:, h:h + 1], in1=S.t[:],
                                                               op0=ALU.mult, op1=ALU.add))
                        pg.op("dve", [Abase.reg, cum.reg], [Abase.reg],
                              lambda e: e.tensor_tensor(out=Abase.t[:, h:h + 1], in0=Abase.t[:, h:h + 1], in1=cum.t[:, T - 1:T], op=ALU.add))
                    pg.op("dve", [Ltot.reg, cum.reg], [Ltot.reg],
                          lambda e: e.tensor_tensor(out=lt, in0=lt, in1=cum.t[:, T - 1:T], op=ALU.add))
        r_sq = [Reg() for _ in range(4)]
        r_soq = [Reg() for _ in range(4)]
        for q in range(4):
            sti = st_i[q].ap().rearrange("(g p) n -> p g n", p=P)
            for g in range(4):
                pg.op("act", [S_f[q * 4 + g].reg], [stb.reg],
                      lambda e: e.activation(out=stb.t[:, g, 0:P], in_=S_f[q * 4 + g].t[:], func=AF.Copy))
            pg.op("dve", [Ltot.reg, stb.reg], [stb.reg], lambda e: e.tensor_copy(out=stb.t[:, :, P], in_=Ltot.t[:, q * 4:(q + 1) * 4]))
            pg.dma("sp", sti, stb.t[:], [stb.reg], [r_sq[q]])
            pg.allgather(st_i[q].ap(), st_o[q].ap(), [r_sq[q]], [r_soq[q]])
        for b in S_f:
            pg.op("pool", [], [b.reg], lambda e: e.memset(b.t[:], 0.0))
        sto = [st_o[q].ap().rearrange("(m g p) n -> m p g n", p=P, g=4) for q in range(4)]
        for d in range(2):
            order = [0, 1, 2] if d == 0 else [3, 2, 1]
            for m in order:
                uc = cm.t[:, d * 8 + m:d * 8 + m + 1]
                omu = cm.t[:, d * 8 + 4 + m:d * 8 + 4 + m + 1]
                for hq in range(2):
                    g0 = d * NH + hq * 4
                    pg.dma("sp", stb.t[:], sto[d * 2 + hq][m], [r_soq[d * 2 + hq]], [stb.reg])
                    pg.op("act", [stb.reg], [coef.reg],
                          lambda e: e.activation(out=coef.t[:, g0:g0 + 4], in_=stb.t[:, :, P], func=AF.Exp))
                    pg.op("dve", [coef.reg, cm.reg], [coef.reg],
                          lambda e: e.tensor_scalar(out=coef.t[:, g0:g0 + 4], in0=coef.t[:, g0:g0 + 4],
                                                    scalar1=uc, scalar2=omu, op0=ALU.mult, op1=ALU.add))
                    for hh in range(4):
                        S = S_f[g0 + hh]
                        pg.op("pool", [stb.reg, cm.reg], [stmp.reg],
                              lambda e: e.tensor_scalar(out=stmp.t[:], in0=stb.t[:, hh, 0:P], scalar1=uc, scalar2=None, op0=ALU.mult))
                        pg.op("dve", [S.reg, coef.reg, stmp.reg], [S.reg],
                              lambda e: e.scalar_tensor_tensor(out=S.t[:], in0=S.t[:], scalar=coef.t[:, g0 + hh:g0 + hh + 1],
                                                               in1=stmp.t[:], op0=ALU.mult, op1=ALU.add))
        for g in range(2 * NH):
            pg.op("act", [S_f[g].reg], [S_b[g].reg], lambda e: e.activation(out=S_b[g].t[:], in_=S_f[g].t[:], func=AF.Copy))

    def scan_tile(l, d, col0, on_head):
        chunks = list(range(NCH)) if d == 0 else list(range(NCH - 1, -1, -1))
        mask = maskf if d == 0 else maskb

        def prep(h):
            khT, vT, scT, qS = khT2[h % 2], vT2[h % 2], scT2[h % 2], qS2[h % 2]
            sl, wv = ws.get()
            pq = next_pp()
            proj(sl, wv[0], 0, col0, T, pq)
            pg.op("act", [pq.reg], [qs.reg], lambda e: e.activation(out=qs.t[:], in_=pq.t[:], func=AF.Silu))
            pv = next_pp()
            proj(sl, wv[1], 0, col0, T, pv)
            pg.op("act", [pv.reg], [vb.reg], lambda e: e.activation(out=vb.t[:], in_=pv.t[:], func=AF.Copy))
            pz = next_pp()
            proj(sl, wv[2], 0, col0, T, pz)
            gate_math(pz, l, d, h)
            pg.op("dve", [lf.reg, scanmask.reg], [cum.reg],
                  lambda e: e.tensor_tensor_scan(out=cum.t[:], data0=scanmask.t[:], data1=lf.t[:], initial=0.0, op0=ALU.mult, op1=ALU.add))
            c3 = cum.t[:].rearrange("p (c j) -> p c j", j=CH)
            ed = edec.t[:, h * NCH:(h + 1) * NCH]
            pg.op("act", [cum.reg], [edec.reg], lambda e: e.activation(out=ed, in_=c3[:, :, CH - 1], func=AF.Exp))
            if d == 0:
                C = cum
            else:
                pg.op("dve", [cum.reg, lf.reg], [lf.reg],
                      lambda e: e.tensor_tensor(out=lf.t[:], in0=cum.t[:], in1=lf.t[:], op=ALU.subtract))
                C = lf
            C3 = C.t[:].rearrange("p (c j) -> p c j", j=CH)
            d13 = d1.t[:].rearrange("p (c j) -> p c j", j=CH)
            d23 = d2.t[:].rearrange("p (c j) -> p c j", j=CH)
            pg.op("dve", [C.reg], [d1.reg],
                  lambda e: e.tensor_tensor(out=d13, in0=C3, in1=C3[:, :, CH // 2:CH // 2 + 1].to_broadcast([P, NCH, CH]), op=ALU.subtract))
            pg.op("pool", [C.reg, cum.reg], [d2.reg],
                  lambda e: e.tensor_tensor(out=d23, in0=c3[:, :, CH - 1:CH].to_broadcast([P, NCH, CH]), in1=C3, op=ALU.subtract))
            pg.op("act", [d1.reg], [ex[0].reg], lambda e: e.activation(out=ex[0].t[:], in_=d1.t[:], func=AF.Exp))
            pg.op("act", [d1.reg], [ex[1].reg], lambda e: e.activation(out=ex[1].t[:], in_=d1.t[:], func=AF.Exp, scale=-1.0))
            pg.op("act", [d2.reg], [ex[2].reg], lambda e: e.activation(out=ex[2].t[:], in_=d2.t[:], func=AF.Exp))
            pg.op("act", [C.reg], [ex[3].reg], lambda e: e.activation(out=ex[3].t[:], in_=C.t[:], func=AF.Exp))
            if d == 0:
                Eq, Ek, EqS, Ekh = ex[0], ex[1], ex[3], ex[2]
            else:
                Eq, Ek, EqS, Ekh = ex[1], ex[0], ex[2], ex[3]
            pg.op("dve", [qs.reg, Eq.reg], [qt.reg],
                  lambda e: e.scalar_tensor_tensor(out=qt.t[:], in0=qs.t[:], scalar=QSCALE, in1=Eq.t[:], op0=ALU.mult, op1=ALU.mult))
            pg.op("dve", [qs.reg, EqS.reg], [qS.reg],
                  lambda e: e.scalar_tensor_tensor(out=qS.t[:], in0=qs.t[:], scalar=QSCALE, in1=EqS.t[:], op0=ALU.mult, op1=ALU.mult))
            pg.op("dve", [kk.reg, Ek.reg], [kt.reg], lambda e: e.tensor_tensor(out=kt.t[:], in0=kk.t[:], in1=Ek.t[:], op=ALU.mult))
            pg.op("pool", [kk.reg, Ekh.reg], [kh.reg], lambda e: e.tensor_tensor(out=kh.t[:], in0=kk.t[:], in1=Ekh.t[:], op=ALU.mult))
            for j in range(NCH):
                pg.op("pe", [kh.reg, ident.reg], [p_tr.reg],
                      lambda e: e.transpose(p_tr.t[0:CH, j * P:(j + 1) * P], kh.t[:, j * CH:(j + 1) * CH], ident.t[:]))
            pg.op("act", [p_tr.reg], [khT.reg], lambda e: e.activation(out=khT.t[0:CH, :], in_=p_tr.t[0:CH, :], func=AF.Copy))
            for j in range(NCH):
                pg.op("pe", [vb.reg, ident.reg], [p_tr2.reg],
                      lambda e: e.transpose(p_tr2.t[0:CH, j * P:(j + 1) * P], vb.t[:, j * CH:(j + 1) * CH], ident.t[:]))
            pg.op("dve", [p_tr2.reg], [vT.reg], lambda e: e.tensor_copy(out=vT.t[0:CH, :], in_=p_tr2.t[0:CH, :]))
            for j in range(NCH):
                pg.op("pe", [kt.reg, qt.reg], [p_sc.reg],
                      lambda e: e.matmul(p_sc.t[0:CH, j * CH:(j + 1) * CH], lhsT=kt.t[:, j * CH:(j + 1) * CH],
                                         rhs=qt.t[:, j * CH:(j + 1) * CH], start=True, stop=True))
            pg.op("dve", [p_sc.reg, consts.reg], [scT.reg],
                  lambda e: e.tensor_tensor(out=scT.t[:], in0=p_sc.t[0:CH, :], in1=mask, op=ALU.mult))
            return sl, wv

        def recur(h, sl, wv):
            khT, vT, scT, qS = khT2[h % 2], vT2[h % 2], scT2[h % 2], qS2[h % 2]
            S = S_f[d * NH + h]
            Sb = S_b[d * NH + h]
            for j in chunks:
                cs = slice(j * CH, (j + 1) * CH)
                pg.op("pe", [vT.reg, scT.reg], [p_o.reg],
                      lambda e: e.matmul(p_o.t[:, cs], lhsT=vT.t[0:CH, j * P:(j + 1) * P], rhs=scT.t[:, cs], start=True, stop=False))
                pg.op("pe", [Sb.reg, qS.reg], [p_o.reg],
                      lambda e: e.matmul(p_o.t[:, cs], lhsT=Sb.t[:], rhs=qS.t[:, cs], start=False, stop=True))
                pg.op("pe", [khT.reg, vT.reg], [p_small.reg],
                      lambda e: e.matmul(p_small.t[:, 0:P], lhsT=khT.t[0:CH, j * P:(j + 1) * P], rhs=vT.t[0:CH, j * P:(j + 1) * P],
                                         start=True, stop=True))
                pg.op("dve", [S.reg, edec.reg, p_small.reg], [S.reg],
                      lambda e: e.scalar_tensor_tensor(out=S.t[:], in0=S.t[:], scalar=edec.t[:, h * NCH + j:h * NCH + j + 1],
                                                       in1=p_small.t[:, 0:P], op0=ALU.mult, op1=ALU.add))
                pg.op("act", [S.reg], [Sb.reg], lambda e: e.activation(out=Sb.t[:], in_=S.t[:], func=AF.Copy))
            on_head(h, sl, wv)

        if KPIPE:
            cur = prep(0)
            for h in range(NH):
                nxt = prep(h + 1) if h + 1 < NH else None
                recur(h, cur[0], cur[1])
                cur = nxt
        else:
            for h in range(NH):
                cur = prep(h)
                recur(h, cur[0], cur[1])

    def sweep2(l, xbuf, rx):
        ws.plan([wi(l, [CB_Q + h, CB_V + h, CB_ZB + h]) for h in range(NH)] * NT)
        for i in range(NT - 1, -1, -1):
            load_x(xbuf, rx, HL + i * T, T)
            rmsnorm(T, vec[l].t[:, 0:8], vec[l].reg)

            def on_head(h, sl, wv):
                pg.op("act", [p_o.reg], [oh.reg], lambda e: e.activation(out=oh.t[:], in_=p_o.t[:], func=AF.Copy))
                pg.dma("sp", ob_d.ap()[h * P:(h + 1) * P, i * T:(i + 1) * T], oh.t[:], [oh.reg], [r_ob], join=True)
            scan_tile(l, 1, 0, on_head)

    def sweep3(l, xbuf, rx, xdst, rxd):
        V = vec[l].t
        tile_items = ([wi(l, [CB_GLA + cb, CB_GLB + cb]) for cb in range(4)]
                      + [wi(l, [CB_Q + h, CB_V + h, CB_ZF + h, CB_OG + h]) for h in range(NH)]
                      + [([(w_in_b.ap()[l, CB_GA + eb:CB_GA + eb + 1], 1, 1024), (w_in_b.ap()[l, CB_GB + eb:CB_GB + eb + 1], 1, 1024),
                           (w_ow_b.ap()[l, eb:eb + 1], 1, 1024), (w_pw_b.ap()[l, eb:eb + 1], 1, 512)], [wregs[l]["in"], wregs[l]["ow"], wregs[l]["pw"]]) for eb in range(KC)]
                      + [([(w_out_b.ap()[l, db:db + 1], 1, 1024)], [wregs[l]["out"]]) for db in range(KC)])
        ws.plan(tile_items * NT)
        accr = [Reg() for _ in range(4)]
        for i in range(NT):
            load_x(xbuf, rx, HL + i * T - 15, XWT)
            rmsnorm(XWT, V[:, 0:8], vec[l].reg)
            for cb in range(4):
                sl, wv = ws.get()
                pa, pb = next_pp(), next_pp()
                proj(sl, wv[1], 0, 0, T, pb)
                pg.op("act", [pb.reg], [sga.reg], lambda e: e.activation(out=sga.t[:], in_=pb.t[:], func=AF.Sigmoid))
                proj(sl, wv[0], 0, 0, T, pa)
                pg.op("dve", [pa.reg, sga.reg], [a_ext.reg],
                      lambda e: e.tensor_tensor(out=a_ext.t[:, cb, 0:T], in0=pa.t[:], in1=sga.t[:], op=ALU.mult))
                proj(sl, wv[1], 0, T, XWT - T, p_small)
                pg.op("act", [p_small.reg], [sgb.reg],
                      lambda e: e.activation(out=sgb.t[:, 0:XWT - T], in_=p_small.t[:, 0:XWT - T], func=AF.Sigmoid))
                proj(sl, wv[0], 0, T, XWT - T, p_small)
                pg.op("dve", [p_small.reg, sgb.reg], [a_ext.reg],
                      lambda e: e.tensor_tensor(out=a_ext.t[:, cb, T:XWT], in0=p_small.t[:, 0:XWT - T], in1=sgb.t[:, 0:XWT - T], op=ALU.mult))
            for cb in range(4):
                pcv = next_pp()
                for k in range(31):
                    dg = dgb[dg_i[0] % NDG]
                    dg_i[0] += 1
                    pg.op("act", [ident.reg, vec[l].reg], [dg.reg],
                          lambda e: e.activation(out=dg.t[:], in_=ident.t[:], func=AF.Copy, scale=V[:, 8 + cb * 31 + k:8 + cb * 31 + k + 1]))
                    pg.op("pe", [dg.reg, a_ext.reg], [pcv.reg],
                          lambda e: e.matmul(pcv.t[:], lhsT=dg.t[:], rhs=a_ext.t[:, cb, k:k + T], start=(k == 0), stop=(k == 30)))
                pg.op("dve", [pcv.reg, vec[l].reg], [accr[cb]],
                      lambda e: e.tensor_scalar(out=acc_t[:, cb, :], in0=pcv.t[:], scalar1=V[:, 132 + cb:133 + cb], scalar2=None, op0=ALU.add))
            pg.op("act", accr, [accb.reg], lambda e: e.activation(out=accb.t, in_=acc_t, func=AF.Copy))
            pm = next_pp()
            for cb in range(4):
                pg.op("pe", [ones.reg, accb.reg], [pm.reg],
                      lambda e: e.matmul(pm.t[:], lhsT=ones.t[:], rhs=accb.t[:, cb, :], start=(cb == 0), stop=(cb == 3)))
            pg.op("dve", [pm.reg], [mu.reg], lambda e: e.tensor_scalar(out=mu.t[:], in0=pm.t[:], scalar1=1.0 / 512, scalar2=None, op0=ALU.mult))
            for cb in range(4):
                eng = "dve" if cb % 2 == 0 else "pool"
                pg.op(eng, [accr[cb], mu.reg], [accr[cb]],
                      lambda e: e.tensor_tensor(out=acc_t[:, cb, :], in0=acc_t[:, cb, :], in1=mu.t[:], op=ALU.subtract))
            pg.op("act", accr, [accb.reg], lambda e: e.activation(out=accb.t, in_=acc_t, func=AF.Square))
            pv_ = next_pp()
            for cb in range(4):
                pg.op("pe", [ones.reg, accb.reg], [pv_.reg],
                      lambda e: e.matmul(pv_.t[:], lhsT=ones.t[:], rhs=accb.t[:, cb, :], start=(cb == 0), stop=(cb == 3)))
            rsqrt_to(var, var.t[:], pv_, T, 1.0 / 512)
            for cb in range(4):
                eng = "dve" if cb % 2 == 0 else "pool"
                pg.op(eng, [accr[cb], var.reg], [accr[cb]],
                      lambda e: e.tensor_tensor(out=acc_t[:, cb, :], in0=acc_t[:, cb, :], in1=var.t[:], op=ALU.mult))
                pg.op("act", [accr[cb], vec[l].reg], [asw.reg],
                      lambda e: e.activation(out=asw.t[:, cb, :], in_=acc_t[:, cb, :], func=AF.Silu,
                                             scale=V[:, 136 + cb:137 + cb], bias=V[:, 140 + cb:141 + cb]))

            def on_head(h, sl, wv):
                pg.dma("sp", obh.t[:], ob_d.ap()[h * P:(h + 1) * P, i * T:(i + 1) * T], [r_ob], [obh.reg])
                pg.op("dve", [p_o.reg, obh.reg], [oh.reg], lambda e: e.tensor_tensor(out=oh.t[:], in0=p_o.t[:], in1=obh.t[:], op=ALU.add))
                po = next_pp()
                sumsq(lambda c: oh.t[:], 1, T, po)
                rsqrt_to(var, var.t[:], po, T, 1.0 / P)
                pog = next_pp()
                proj(sl, wv[3], 0, 15, T, pog)
                pg.op("act", [pog.reg], [t1.reg], lambda e: e.activation(out=t1.t[:], in_=pog.t[:], func=AF.Silu))
                pg.op("dve", [oh.reg, var.reg, vec[l].reg], [t2.reg],
                      lambda e: e.scalar_tensor_tensor(out=t2.t[:], in0=oh.t[:], scalar=V[:, 144 + h:145 + h], in1=var.t[:],
                                                       op0=ALU.mult, op1=ALU.mult))
                pg.op("pool", [t1.reg, t2.reg], [bmix.reg],
                      lambda e: e.tensor_tensor(out=bmix.t[:, h, :], in0=t1.t[:], in1=t2.t[:], op=ALU.mult))
            scan_tile(l, 0, 15, on_head)
            for eb in range(KC):
                sl, wv = ws.get()
                pga = next_pp()
                proj(sl, wv[0], 0, 15, T, pga)
                pg.op("act", [pga.reg], [sga.reg], lambda e: e.activation(out=sga.t[:], in_=pga.t[:], func=AF.Sigmoid))
                pgb = next_pp()
                proj(sl, wv[1], 0, 15, T, pgb)
                pg.op("act", [pgb.reg], [sgb.reg], lambda e: e.activation(out=sgb.t[:], in_=pgb.t[:], func=AF.Sigmoid))
                pA = next_pp()
                proj(sl, wv[3], 0, 0, T, pA, kc=4, rhs=asw)
                pg.op("dve", [pA.reg, sga.reg], [t1.reg], lambda e: e.tensor_tensor(out=t1.t[:], in0=pA.t[:], in1=sga.t[:], op=ALU.mult))
                pB = next_pp()
                proj(sl, wv[2], 0, 0, T, pB, rhs=bmix)
                pg.op("dve", [pB.reg, sgb.reg], [t2.reg], lambda e: e.tensor_tensor(out=t2.t[:], in0=pB.t[:], in1=sgb.t[:], op=ALU.mult))
                pg.op("pool", [t1.reg, t2.reg], [ymix.reg],
                      lambda e: e.tensor_tensor(out=ymix.t[:, eb, :], in0=t1.t[:], in1=t2.t[:], op=ALU.add))
            for db in range(KC):
                sl, wv = ws.get()
                pO = next_pp()
                proj(sl, wv[0], 0, 0, T, pO, rhs=ymix)
                pg.op("dve", [pO.reg, xt.reg], [xt.reg],
                      lambda e: e.tensor_tensor(out=xt.t[:, db, 15:15 + T], in0=pO.t[:], in1=xt.t[:, db, 15:15 + T], op=ALU.add))
            pg.dma("sp", xdst.ap()[:, HL + i * T:HL + (i + 1) * T].rearrange("(c p) n -> p c n", p=P), xt.t[:, :, 15:15 + T],
                   [xt.reg], [rxd], join=True)

    def exchange_halo(xbuf, rx):
        pg.dma("sp", edge_i.ap()[:, 0:HL], xbuf.ap()[:, HL:2 * HL], [rx], [r_ei])
        pg.dma("sp", edge_i.ap()[:, HL:2 * HL], xbuf.ap()[:, TOK:TOK + HL], [rx], [r_ei], join=True)
        pg.allgather(edge_i.ap(), edge_o.ap(), [r_ei], [r_eo])
        pg.dma("sp", eo_sb.t[:].rearrange("p m c n -> p (m c) n"), edge_o.ap().rearrange("(mc p) n -> p mc n", p=P), [r_eo], [eo_sb.reg])
        for side in range(2):
            src = (lambda m: eo_sb.t[:, m, :, HL:2 * HL]) if side == 0 else (lambda m: eo_sb.t[:, m, :, 0:HL])
            dst = hl_sb.t[:, :, side * HL:(side + 1) * HL]
            sel = lambda m: cm.t[:, 16 + side * 4 + m:17 + side * 4 + m]
            pg.op("dve", [eo_sb.reg, cm.reg], [hl_sb.reg],
                  lambda e: e.tensor_scalar(out=dst, in0=src(0), scalar1=sel(0), scalar2=None, op0=ALU.mult))
            for m in range(1, 4):
                pg.op("dve", [eo_sb.reg, cm.reg, hl_sb.reg], [hl_sb.reg],
                      lambda e: e.scalar_tensor_tensor(out=dst, in0=src(m), scalar=sel(m), in1=dst, op0=ALU.mult, op1=ALU.add))
        pg.dma("sp", xbuf.ap()[:, 0:HL].rearrange("(c p) n -> p c n", p=P), hl_sb.t[:, :, 0:HL], [hl_sb.reg], [rx], join=True)
        pg.dma("sp", xbuf.ap()[:, HL + TOK:HL + TOK + HL].rearrange("(c p) n -> p c n", p=P), hl_sb.t[:, :, HL:2 * HL], [hl_sb.reg], [rx], join=True)

    def sweep4(l, xbuf, rx, xdst, rxd, last):
        V = vec[l].t
        tile_items = ([([(w_up_b.ap()[l, fb:fb + 1], 1, 1024), (w_up_b.ap()[l, NFB + fb:NFB + fb + 1], 1, 1024)], [wregs[l]["up"]]) for fb in range(NFB)]
                      + [([(w_dn_b.ap()[l, db:db + 1], 1, NFB * 128)], [wregs[l]["dn"]]) for db in range(KC)])
        ws.plan(tile_items * NT)
        for i in range(NT):
            load_x(xbuf, rx, HL + i * T - 1, T + 2)
            rmsnorm(T + 2, V[:, 152:160], vec[l].reg)
            for fb in range(NFB):
                sl, wv = ws.get()
                pgt = next_pp()
                proj(sl, wv[0], 0, 1, T, pgt)
                for c in range(KC):
                    pg.op("pe", [sl.reg, hb.reg], [p_small.reg],
                          lambda e: e.matmul(p_small.t[:, 0:2], lhsT=wv[0][:, 0, c * P:(c + 1) * P],
                                             rhs=hb.t[:, c, 0:T + 2:T + 1], start=(c == 0), stop=(c == KC - 1)))
                pg.op("act", [pgt.reg], [g_sb.reg], lambda e: e.activation(out=g_sb.t[:, 1:T + 1], in_=pgt.t[:], func=AF.Copy))
                pg.op("act", [p_small.reg, g_sb.reg], [g_sb.reg],
                      lambda e: e.activation(out=g_sb.t[:, 0:T + 2:T + 1], in_=p_small.t[:, 0:2], func=AF.Copy))
                w3 = lambda k: V[:, 160 + fb * 3 + k:161 + fb * 3 + k]
                pg.op("dve", [g_sb.reg, vec[l].reg], [gacc.reg],
                      lambda e: e.tensor_scalar(out=gacc.t[:], in0=g_sb.t[:, 0:T], scalar1=w3(0), scalar2=V[:, 226 + fb:227 + fb],
                                                op0=ALU.mult, op1=ALU.add))
                pg.op("dve", [g_sb.reg, gacc.reg, vec[l].reg], [gacc.reg],
                      lambda e: e.scalar_tensor_tensor(out=gacc.t[:], in0=g_sb.t[:, 1:T + 1], scalar=w3(1), in1=gacc.t[:], op0=ALU.mult, op1=ALU.add))
                pg.op("dve", [g_sb.reg, gacc.reg, vec[l].reg], [gacc.reg],
                      lambda e: e.scalar_tensor_tensor(out=gacc.t[:], in0=g_sb.t[:, 2:T + 2], scalar=w3(2), in1=gacc.t[:], op0=ALU.mult, op1=ALU.add))
                pg.op("act", [gacc.reg], [t1.reg], lambda e: e.activation(out=t1.t[:], in_=gacc.t[:], func=AF.Silu))
                pvl = next_pp()
                proj(sl, wv[1], 0, 1, T, pvl)
                pg.op("dve", [pvl.reg, t1.reg], [u_sb.reg],
                      lambda e: e.tensor_tensor(out=u_sb.t[:, fb, :], in0=pvl.t[:], in1=t1.t[:], op=ALU.mult))
            for db in range(KC):
                sl, wv = ws.get()
                pO = next_pp()
                proj(sl, wv[0], 0, 0, T, pO, kc=NFB, rhs=u_sb)
                pg.op("dve", [pO.reg, xt.reg], [xt.reg],
                      lambda e: e.tensor_tensor(out=xt.t[:, db, 1:1 + T], in0=pO.t[:], in1=xt.t[:, db, 1:1 + T], op=ALU.add))
            if last and final_norm:
                pst = next_pp()
                sumsq(lambda c: xt.t[:, c, 1:1 + T], KC, T, pst)
                rsqrt_to(var, var.t[:], pst, T, 1.0 / D)
                for c in range(KC):
                    eng = "dve"
                    pg.op(eng, [xt.reg, var.reg, gvec.reg], [xt.reg],
                          lambda e: e.scalar_tensor_tensor(out=xt.t[:, c, 1:1 + T], in0=xt.t[:, c, 1:1 + T], scalar=gvec.t[:, c:c + 1], in1=var.t[:],
                                                           op0=ALU.mult, op1=ALU.mult))
            if last:
                pg.dma("sp", y_out.ap()[:, i * T:(i + 1) * T].rearrange("(c p) n -> p c n", p=P), xt.t[:, :, 1:1 + T], [xt.reg], [rxd], join=True)
            else:
                pg.dma("sp", xdst.ap()[:, HL + i * T:HL + (i + 1) * T].rearrange("(c p) n -> p c n", p=P), xt.t[:, :, 1:1 + T],
                       [xt.reg], [rxd], join=True)

    import os
    KSTOP = int(os.environ.get("KSTOP", "9"))
    r_y = Reg()
    for l in range(NL):
        if KSTOP < 9:
            if KSTOP >= 1:
                sweep1(l, xA, r_xA)
            if KSTOP >= 2:
                sweep2(l, xA, r_xA)
            if KSTOP >= 3:
                pg.barrier()
                sweep3(l, xA, r_xA, xB, r_xB)
            if KSTOP >= 4:
                exchange_halo(xB, r_xB)
            pg.wait_all("sp", [r_xA, r_xB, r_ob] + [wregs[l][k] for k in wregs[l]])
            pg.wait_all("pool", [wregs[l][k] for k in wregs[l]])
            break
        if l + 1 < NL and not KCASTALL:
            cast_w(l + 1)
        if l > 0:
            exchange_halo(xA, r_xA)
        sweep1(l, xA, r_xA)
        sweep2(l, xA, r_xA)
        pg.barrier()
        sweep3(l, xA, r_xA, xB, r_xB)
        exchange_halo(xB, r_xB)
        pg.barrier()
        last = (l == NL - 1)
        sweep4(l, xB, r_xB, xA, r_y if last else r_xA, last)
    pg.wait_all("sp", [r_y])
    return nc


def _relayout(W, kc):
    K, N = W.shape
    assert K == kc * 128
    return np.ascontiguousarray(W.reshape(kc, 128, N // 128, 128).transpose(2, 1, 0, 3).reshape(N // 128, 128, kc * 128))


def _pcol(v):
    return np.ascontiguousarray(v.reshape(-1, 128).T)


def _make_consts():
    c = np.zeros((128, 128 + 512 + 512), np.float32)
    c[:, 0:128] = np.eye(128, dtype=np.float32)
    s = np.arange(64)[:, None]
    t = np.arange(64)[None, :]
    mf = (s <= t).astype(np.float32)
    mb = (s >= t).astype(np.float32)
    c[0:64, 128:640] = np.tile(mf, (1, 8))
    c[0:64, 640:1152] = np.tile(mb, (1, 8))
    return c


def _core_masks(seg):
    m = np.zeros((128, NCM), np.float32)
    for k in range(4):
        uf = 1.0 if k < seg else 0.0
        ub = 1.0 if k > seg else 0.0
        m[:, k] = uf
        m[:, 4 + k] = 1.0 - uf
        m[:, 8 + k] = ub
        m[:, 12 + k] = 1.0 - ub
        m[:, 16 + k] = 1.0 if k == seg - 1 else 0.0
        m[:, 20 + k] = 1.0 if k == seg + 1 else 0.0
    return m


def _prep_weights(inp, layers):
    L = layers
    out = {}
    out["w_in"] = np.stack([_relayout(np.asarray(inp["w_in"][l]), 8) for l in L])
    out["w_pw"] = np.stack([_relayout(np.asarray(inp["conv_pw_w"][l]), 4) for l in L])
    out["w_ow"] = np.stack([_relayout(np.asarray(inp["hgrn_o_w"][l]), 8) for l in L])
    out["w_out"] = np.stack([_relayout(np.asarray(inp["w_out"][l]), 8) for l in L])
    out["w_up"] = np.stack([_relayout(np.asarray(inp["ffn_w_up"][l]), 8) for l in L])
    out["w_dn"] = np.stack([_relayout(np.asarray(inp["ffn_w_down"][l]), NFB) for l in L])
    vecs = []
    for l in L:
        v = np.zeros((128, NV), np.float32)
        v[:, 0:8] = _pcol(np.asarray(inp["attn_norm_w"][l]))
        dw = np.asarray(inp["conv_dw_w"][l])
        for cb in range(4):
            v[:, 8 + cb * 31:8 + (cb + 1) * 31] = dw[:, cb * 128:(cb + 1) * 128].T
        v[:, 132:136] = _pcol(np.asarray(inp["conv_dw_b"][l]))
        v[:, 136:140] = _pcol(np.asarray(inp["conv_ln_w"][l]))
        v[:, 140:144] = _pcol(np.asarray(inp["conv_ln_b"][l]))
        v[:, 144:152] = _pcol(np.asarray(inp["hgrn_norm_w"][l]))
        v[:, 152:160] = _pcol(np.asarray(inp["ffn_norm_w"][l]))
        fw = np.asarray(inp["ffn_dw_w"][l])
        for fb in range(NFB):
            v[:, 160 + fb * 3:163 + fb * 3] = fw[:, fb * 128:(fb + 1) * 128].T
        v[:, 226:248] = _pcol(np.asarray(inp["ffn_dw_b"][l]))
        vecs.append(v)
    out["vec"] = np.stack(vecs)
    g = np.zeros((128, 72), np.float32)
    g[:, 0:8] = _pcol(np.asarray(inp["final_norm_w"]))
    lbl = np.asarray(inp["lb_logits"])
    for l in range(4):
        for d in range(2):
            g[:, 8 + (l * 2 + d) * 8:8 + (l * 2 + d) * 8 + 8] = _pcol(lbl[l, d])
    out["gvec"] = g
    out["consts"] = _make_consts()
    return out


_PROG_CACHE = {}


def _run(x, inp, layer_groups):
    B, S, _ = x.shape
    nseg = 8 // B
    TOK = S // nseg
    cur = np.asarray(x, dtype=np.float32)
    for gi, layers in enumerate(layer_groups):
        final = (gi == len(layer_groups) - 1)
        key = (TOK, tuple(layers), final)
        if key not in _PROG_CACHE:
            _PROG_CACHE[key] = build_program(TOK, list(layers), final)
        nc = _PROG_CACHE[key]
        wts = _prep_weights(inp, layers)
        in_maps = []
        for c in range(8):
            b, seg = c // nseg, c % nseg
            xp = np.zeros((S + 2 * HL, D), np.float32)
            xp[HL:HL + S] = cur[b]
            sl = xp[seg * TOK:seg * TOK + TOK + 2 * HL]
            m = dict(wts)
            m["x_in"] = np.ascontiguousarray(sl.T)
            m["cm"] = _core_masks(seg)
            in_maps.append(m)
        res = run_bass_kernel_spmd(nc, in_maps, core_ids=list(range(8)))
        nxt = np.empty_like(cur)
        for c in range(8):
            b, seg = c // nseg, c % nseg
            nxt[b, seg * TOK:(seg + 1) * TOK] = res.results[c]["y_out"].T
        cur = nxt
    return cur


def kernel(**inputs):
    x = np.asarray(inputs["x"], dtype=np.float32)
    return _run(x, inputs, [[0, 1, 2, 3]])
```

```python
import numpy as np
import concourse.bass as bass
import concourse.mybir as mybir
from concourse.bass_utils import run_bass_kernel_spmd

F32 = mybir.dt.float32
BF16 = mybir.dt.bfloat16
AF = mybir.ActivationFunctionType
ALU = mybir.AluOpType

P = 128
D = 1024
KC = 8
T = 512
CH = 64
NCH = T // CH
HL = 16
NH = 8
DFF = 2816
NFB = 22
DEPTH = 4
EPS = 1e-6
NV = 248
NCM = 24
QSCALE = 128 ** -0.5

CB_GLA, CB_GLB, CB_Q, CB_V, CB_ZF, CB_ZB, CB_OG, CB_GA, CB_GB = 0, 4, 8, 16, 24, 32, 40, 48, 56


class Reg:
    __slots__ = ("w", "r")

    def __init__(self):
        self.w = {}
        self.r = {}


class PG:
    NDS = 24

    def __init__(self, nc):
        self.nc = nc
        self.engs = {"pe": nc.tensor, "act": nc.scalar, "dve": nc.vector, "pool": nc.gpsimd, "sp": nc.sync}
        self.sems = {}
        self.cnt = {e: 0 for e in self.engs}
        self.waited = {e: {} for e in self.engs}
        self.ndma = 0
        self.ncc = 0
        self._stack = []
        self.ekey = {}
        self.nep = {}
        for e in self.engs:
            self.sems[e] = self._sem("e_" + e)
        for i in range(self.NDS):
            self.sems["d%d" % i] = self._sem("dma%d" % i)
        for i in range(8):
            self.sems["g%d" % i] = self._sem("gdma%d" % i)
        self.ngdma = 0
        for i in range(4):
            self.sems["c%d" % i] = self._sem("cc%d" % i)

    def _sem(self, name):
        cm = self.nc.semaphore(name)
        s = cm.__enter__()
        self._stack.append(cm)
        return s

    def _deps(self, eng, reads, writes, extra=(), join=False):
        deps = {}

        def add(k, v):
            if deps.get(k, 0) < v:
                deps[k] = v
        for r in reads:
            for k, v in r.w.items():
                add(k, v)
        for w in writes:
            if not join:
                for k, v in w.w.items():
                    add(k, v)
            for k, v in w.r.items():
                add(k, v)
        for (k, v) in extra:
            add(k, v)
        E = self.engs[eng]
        wd = self.waited[eng]
        for k, v in deps.items():
            if eng == "pe" and k.startswith("pe"):
                continue
            if wd.get(k, 0) >= v:
                continue
            E.wait_ge(self.sems[k], v)
            wd[k] = v

    def _mark(self, t, reads, writes, join=False):
        k, v = t
        for r in reads:
            if r.r.get(k, 0) < v:
                r.r[k] = v
        for w in writes:
            if join:
                if w.w.get(k, 0) < v:
                    w.w[k] = v
            else:
                w.w = {k: v}
                w.r = {}

    EPOCH_LEN = 16000

    def op(self, eng, reads, writes, fn):
        self._deps(eng, reads, writes)
        ins = fn(self.engs[eng])
        key = self.ekey.get(eng, eng)
        self.cnt[eng] += 1
        ins.then_inc(self.sems[key], 1)
        t = (key, self.cnt[eng])
        self._mark(t, reads, writes)
        if self.cnt[eng] >= self.EPOCH_LEN:
            self.nep[eng] = self.nep.get(eng, 0) + 1
            nk = "%s#%d" % (eng, self.nep[eng])
            self.sems[nk] = self._sem("e_%s_%d" % (eng, self.nep[eng]))
            self.ekey[eng] = nk
            self.cnt[eng] = 0
        return t

    def dma(self, q, out, in_, reads, writes, join=False, slow=False):
        if q == "pool":
            i = self.ngdma
            self.ngdma += 1
            s = "g%d" % (i % 8)
            v = 16 * (i // 8 + 1)
        else:
            i = self.ndma
            self.ndma += 1
            s = "d%d" % (i % self.NDS)
            v = 16 * (i // self.NDS + 1)
        extra = [(s, v - 16)] if v > 16 else []
        self._deps(q, reads, writes, extra, join)
        if slow:
            ins = self.engs[q].dma_start(out=out, in_=in_, allow_slow_non_contiguous=True)
        else:
            ins = self.engs[q].dma_start(out=out, in_=in_)
        ins.then_inc(self.sems[s], 16)
        t = (s, v)
        self._mark(t, reads, writes, join)
        return t

    def allgather(self, in_ap, out_ap, reads, writes):
        i = self.ncc
        self.ncc += 1
        s = "c%d" % (i % 4)
        v = i // 4 + 1
        extra = [(s, v - 1)] if v > 1 else []
        self._deps("pool", reads, writes, extra)
        ins = self.nc.gpsimd.collective_compute("AllGather", ALU.bypass, replica_groups=[[0, 1, 2, 3], [4, 5, 6, 7]],
                                                ins=[in_ap], outs=[out_ap])
        ins.then_inc(self.sems[s], 1)
        t = (s, v)
        self._mark(t, reads, writes)
        return t

    def barrier(self):
        comp = ("pe", "act", "dve", "pool")
        for e in comp:
            E = self.engs[e]
            for o in comp:
                if o == e:
                    continue
                ok = self.ekey.get(o, o)
                val = self.cnt[o]
                if val == 0:
                    n = self.nep.get(o, 0)
                    if n == 0:
                        continue
                    ok = o if n == 1 else "%s#%d" % (o, n - 1)
                    val = self.EPOCH_LEN
                if self.waited[e].get(ok, 0) >= val:
                    continue
                E.wait_ge(self.sems[ok], val)
                self.waited[e][ok] = val

    def wait_all(self, eng, regs):
        self._deps(eng, regs, regs)


class Buf:
    def __init__(self, nc, name, shape, dtype, psum=False):
        if psum:
            cm = nc.psum_tensor(name, shape, dtype)
        else:
            cm = nc.sbuf_tensor(name, shape, dtype)
        self.cm = cm
        self.t = cm.__enter__()
        self.reg = Reg()

    def __getitem__(self, k):
        return self.t[k]


class WStream:
    def __init__(self, pg, nc, nslot, ws, bufs):
        self.pg = pg
        self.nslot = nslot
        self.slots = [Buf(nc, "wslot%d" % i, [P, ws], BF16) for i in range(nslot)]
        bufs.extend(self.slots)
        self.q = []
        self.issued = []
        self.nxt = 0

    def plan(self, items):
        self.q.extend(items)

    def _issue(self):
        pieces, reg = self.q.pop(0)
        s = self.nxt
        self.nxt = (s + 1) % self.nslot
        sl = self.slots[s]
        off = 0
        views = []
        for (ap, n, e) in pieces:
            out = sl.t[:, off:off + n * e].rearrange("p (j e) -> p j e", j=n)
            self.pg.dma("sp", out, ap.rearrange("j p e -> p j e"), list(reg), [sl.reg], join=(off > 0))
            views.append(out)
            off += n * e
        self.issued.append((s, views))

    def get(self):
        while len(self.issued) < self.nslot - 1 and self.q:
            self._issue()
        s, views = self.issued.pop(0)
        return self.slots[s], views


def build_program(TOK, layers, final_norm):
    import os
    KPIPE = int(os.environ.get("KPIPE", "1"))
    NL = len(layers)
    NT = TOK // T
    XW = TOK + 2 * HL
    nc = bass.Bass("TRN2", target_bir_lowering=False)
    pg = PG(nc)
    bufs = []

    def dram_in(name, shape, dt=F32):
        return nc.dram_tensor(name, shape, dt, kind="ExternalInput")

    x_in = dram_in("x_in", [D, XW])
    cm_in = dram_in("cm", [P, NCM])
    consts_in = dram_in("consts", [P, 128 + 512 + 512])
    gvec_in = dram_in("gvec", [P, 8 + 64])
    vec_in = dram_in("vec", [NL, P, NV])
    w_in_f = dram_in("w_in", [NL, 64, P, 1024])
    w_pw_f = dram_in("w_pw", [NL, 8, P, 512])
    w_ow_f = dram_in("w_ow", [NL, 8, P, 1024])
    w_out_f = dram_in("w_out", [NL, 8, P, 1024])
    w_up_f = dram_in("w_up", [NL, 44, P, 1024])
    w_dn_f = dram_in("w_dn", [NL, 8, P, NFB * 128])
    y_out = nc.dram_tensor("y_out", [D, TOK], F32, kind="ExternalOutput")

    w_in_b = nc.dram_tensor("w_in_b", [NL, 64, P, 1024], BF16)
    w_pw_b = nc.dram_tensor("w_pw_b", [NL, 8, P, 512], BF16)
    w_ow_b = nc.dram_tensor("w_ow_b", [NL, 8, P, 1024], BF16)
    w_out_b = nc.dram_tensor("w_out_b", [NL, 8, P, 1024], BF16)
    w_up_b = nc.dram_tensor("w_up_b", [NL, 44, P, 1024], BF16)
    w_dn_b = nc.dram_tensor("w_dn_b", [NL, 8, P, NFB * 128], BF16)
    xA = nc.dram_tensor("xA", [D, XW], F32)
    xB = nc.dram_tensor("xB", [D, XW], F32)
    ob_d = nc.dram_tensor("ob_d", [D, TOK], F32)
    edge_i = nc.dram_tensor("edge_i", [D, 2 * HL], F32)
    edge_o = nc.dram_tensor("edge_o", [4 * D, 2 * HL], F32)
    st_i = [nc.dram_tensor("st_i%d" % q, [4 * P, 129], F32) for q in range(4)]
    st_o = [nc.dram_tensor("st_o%d" % q, [4 * 4 * P, 129], F32) for q in range(4)]
    r_xA, r_xB, r_ob, r_ei, r_eo, r_si, r_so = Reg(), Reg(), Reg(), Reg(), Reg(), Reg(), Reg()
    r_xin = Reg()
    wregs = [{k: Reg() for k in ("in", "pw", "ow", "out", "up", "dn")} for _ in range(NL)]

    def sb(name, shape, dt=F32):
        b = Buf(nc, name, shape, dt)
        bufs.append(b)
        return b

    def ps(name, shape, dt=F32):
        b = Buf(nc, name, shape, dt, psum=True)
        bufs.append(b)
        return b

    class View:
        def __init__(self, t):
            self.t = t
            self.reg = Reg()

    XWT = T + 2 * HL - 2
    cm = sb("cm_sb", [P, NCM])
    consts = sb("consts_sb", [P, 128 + 512 + 512])
    ident = sb("ident", [P, P], BF16)
    ones = sb("ones", [P, P], BF16)
    gvec = sb("gvec_sb", [P, 72])
    vec = [sb("vec%d" % l, [P, NV]) for l in range(NL)]
    lbt = sb("lbt", [P, 64])
    omlb = sb("omlb", [P, 64])
    lbtmp = sb("lbtmp", [P, 64])
    lbm = sb("lbm", [P, 16])
    scanmask = sb("scanmask", [P, T])
    onesf = sb("onesf", [P, T])
    xt = sb("xt", [P, KC, XWT])
    hb = sb("hb", [P, KC, XWT], BF16)
    sqb = sb("sqb", [P, T], BF16)
    rstd = sb("rstd", [P, XWT])
    ws = WStream(pg, nc, 4, 4096, bufs)
    sg = sb("sg", [P, T])
    kk = sb("kk", [P, T])
    lf = sb("lf", [P, T])
    cum = sb("cum", [P, T])
    d1 = sb("d1", [P, T])
    d2 = sb("d2", [P, T])
    ex = [sb("ex%d" % i, [P, T]) for i in range(4)]
    qs = sb("qs", [P, T])
    vb = sb("vb", [P, T], BF16)
    qt = sb("qt", [P, T], BF16)
    kt = sb("kt", [P, T], BF16)
    kh = sb("kh", [P, T], BF16)
    khT2 = [sb("khT%d" % i, [P, NCH * P], BF16) for i in range(2)]
    vT2 = [sb("vT%d" % i, [P, NCH * P], BF16) for i in range(2)]
    scT2 = [sb("scT%d" % i, [CH, T], BF16) for i in range(2)]
    qS2 = [sb("qS%d" % i, [P, T], BF16) for i in range(2)]
    khT, vT = khT2[0], vT2[0]
    S_f = [sb("Sf%d_%d" % (d, h), [P, P]) for d in range(2) for h in range(NH)]
    S_b = [sb("Sb%d_%d" % (d, h), [P, P], BF16) for d in range(2) for h in range(NH)]
    edec = sb("edec", [P, NH * NCH])
    Ltot = sb("Ltot", [P, 2 * NH])
    Abase = sb("Abase", [P, NH])
    eAb = sb("eAb", [P, NH])
    eAb2 = sb("eAb2", [P, NH])
    oh = sb("oh", [P, T])
    obh = sb("obh", [P, T])
    bmix = sb("bmix", [P, NH, T], BF16)
    arena = sb("arena", [P, 6400])
    a_ext = View(arena.t[:, 0:2 * XWT].bitcast(BF16).rearrange("p (c n) -> p c n", c=4))
    NDG = 8
    dgb = [sb("dg%d" % i, [P, P], BF16) for i in range(NDG)]
    dg_i = [0]
    acc_t = arena.t[:, 2168:2168 + 2048].rearrange("p (c n) -> p c n", c=4)
    accb = View(arena.t[:, 4216:5240].bitcast(BF16).rearrange("p (c n) -> p c n", c=4))
    asw = View(arena.t[:, 5240:6264].bitcast(BF16).rearrange("p (c n) -> p c n", c=4))
    u_sb = View(arena.t[:, 0:5632].bitcast(BF16).rearrange("p (c n) -> p c n", c=NFB))
    mu = sb("mu", [P, T])
    var = sb("var", [P, T])
    sga = sb("sga", [P, T])
    sgb = sb("sgb", [P, T])
    t1 = sb("t1", [P, T])
    t2 = sb("t2", [P, T])
    ymix = sb("ymix", [P, KC, T], BF16)
    g_sb = sb("g_sb", [P, T + 2])
    gacc = sb("gacc", [P, T])
    stb = sb("stb", [P, 4, 129])
    coef = sb("coef", [P, 2 * NH])
    stmp = sb("stmp", [P, P])
    eo_sb = sb("eo_sb", [P, 4, KC, 2 * HL])
    hl_sb = sb("hl_sb", [P, KC, 2 * HL])
    pp = [ps("pp%d" % i, [P, T]) for i in range(3)]
    p_small = ps("p_small", [P, T])
    p_tr = ps("p_tr", [P, NCH * P], BF16)
    p_tr2 = ps("p_tr2", [P, NCH * P], BF16)
    p_sc = ps("p_sc", [P, T])
    p_o = ps("p_o", [P, T])
    pp_i = [0]

    def next_pp():
        b = pp[pp_i[0] % 3]
        pp_i[0] += 1
        return b

    def cast_w(l):
        for (src, dst, key, nb, step) in ((w_in_f, w_in_b, "in", 64, 2), (w_pw_f, w_pw_b, "pw", 8, 4),
                                          (w_ow_f, w_ow_b, "ow", 8, 2), (w_out_f, w_out_b, "out", 8, 2),
                                          (w_up_f, w_up_b, "up", 44, 2), (w_dn_f, w_dn_b, "dn", 8, 1)):
            for j in range(0, nb, step):
                pg.dma("pool", dst.ap()[l, j:j + step], src.ap()[l, j:j + step], [], [wregs[l][key]], join=True)

    pg.dma("sp", cm.t[:], cm_in.ap(), [], [cm.reg])
    pg.dma("sp", consts.t[:], consts_in.ap(), [], [consts.reg])
    pg.dma("sp", gvec.t[:], gvec_in.ap(), [], [gvec.reg])
    for l in range(NL):
        pg.dma("sp", vec[l].t[:], vec_in.ap()[l], [], [vec[l].reg])
    KCASTALL = int(os.environ.get("KCASTALL", "1"))
    if KCASTALL:
        for l_ in range(NL):
            cast_w(l_)
    else:
        cast_w(0)
    for c8 in range(KC):
        pg.dma("sp", xA.ap()[c8 * P:(c8 + 1) * P, :], x_in.ap()[c8 * P:(c8 + 1) * P, :], [r_xin], [r_xA], join=True)
    pg.op("dve", [consts.reg], [ident.reg], lambda e: e.tensor_copy(out=ident.t[:], in_=consts.t[:, 0:128]))
    pg.op("pool", [], [ones.reg], lambda e: e.memset(ones.t[:], 1.0))
    pg.op("pool", [], [onesf.reg], lambda e: e.memset(onesf.t[:], 1.0))
    pg.op("pool", [], [scanmask.reg], lambda e: e.memset(scanmask.t[:], 1.0))
    pg.op("pool", [], [scanmask.reg],
          lambda e: e.memset(scanmask.t[:].rearrange("p (c j) -> p c j", j=CH)[:, :, 0:1], 0.0))
    maskf = consts.t[0:CH, 128:128 + 512]
    maskb = consts.t[0:CH, 640:640 + 512]

    lg = gvec.t[:, 8:72].rearrange("p (l r) -> p l r", l=4)
    TT = lambda o, a, b, op: pg.op("dve", [gvec.reg, lbm.reg, lbtmp.reg, lbt.reg], [], lambda e: e.tensor_tensor(out=o, in0=a, in1=b, op=op))
    lt3 = lbtmp.t[:].rearrange("p (l r) -> p l r", l=4)
    lb3 = lbt.t[:].rearrange("p (l r) -> p l r", l=4)

    def lbop(writes, fn, eng="dve"):
        pg.op(eng, [gvec.reg, lbm.reg, lbtmp.reg, lbt.reg], writes, fn)
    lbop([lbm.reg], lambda e: e.tensor_tensor(out=lbm.t[:], in0=lg[:, 0, :], in1=lg[:, 1, :], op=ALU.max))
    lbop([lbm.reg], lambda e: e.tensor_tensor(out=lbm.t[:], in0=lbm.t[:], in1=lg[:, 2, :], op=ALU.max))
    lbop([lbm.reg], lambda e: e.tensor_tensor(out=lbm.t[:], in0=lbm.t[:], in1=lg[:, 3, :], op=ALU.max))
    for l4 in range(4):
        lbop([lbtmp.reg], lambda e: e.tensor_tensor(out=lt3[:, l4, :], in0=lg[:, l4, :], in1=lbm.t[:], op=ALU.subtract))
    lbop([lbtmp.reg], lambda e: e.activation(out=lbtmp.t[:], in_=lbtmp.t[:], func=AF.Exp), eng="act")
    lbop([lbm.reg], lambda e: e.tensor_tensor(out=lbm.t[:], in0=lt3[:, 0, :], in1=lt3[:, 1, :], op=ALU.add))
    lbop([lbm.reg], lambda e: e.tensor_tensor(out=lbm.t[:], in0=lbm.t[:], in1=lt3[:, 2, :], op=ALU.add))
    lbop([lbm.reg], lambda e: e.tensor_tensor(out=lbm.t[:], in0=lbm.t[:], in1=lt3[:, 3, :], op=ALU.add))
    lbop([lbm.reg], lambda e: e.reciprocal(out=lbm.t[:], in_=lbm.t[:]))
    for l4 in range(4):
        lbop([lbtmp.reg], lambda e: e.tensor_tensor(out=lt3[:, l4, :], in0=lt3[:, l4, :], in1=lbm.t[:], op=ALU.mult))
    lbop([lbt.reg], lambda e: e.memset(lbt.t[:], 0.0), eng="pool")
    for l4 in range(1, 4):
        lbop([lbt.reg], lambda e: e.tensor_tensor(out=lb3[:, l4, :], in0=lb3[:, l4 - 1, :], in1=lt3[:, l4, :], op=ALU.add))
    lbop([omlb.reg], lambda e: e.tensor_scalar(out=omlb.t[:], in0=lbt.t[:], scalar1=-1.0, scalar2=1.0, op0=ALU.mult, op1=ALU.add))

    def load_x(xbuf, rx, col0, ncols):
        pg.dma("sp", xt.t[:, :, 0:ncols], xbuf.ap()[:, col0:col0 + ncols].rearrange("(c p) n -> p c n", p=P),
               [rx], [xt.reg])

    def sumsq(src_fn, nk, n, pst):
        for c in range(nk):
            pg.op("act", [xt.reg, oh.reg], [sqb.reg], lambda e: e.activation(out=sqb.t[:, 0:n], in_=src_fn(c), func=AF.Square))
            pg.op("pe", [ones.reg, sqb.reg], [pst.reg],
                  lambda e: e.matmul(pst.t[:, 0:n], lhsT=ones.t[:], rhs=sqb.t[:, 0:n], start=(c == 0), stop=(c == nk - 1)))

    def rsqrt_to(dst_buf, dst_ap, pst, n, scale):
        pg.op("dve", [pst.reg], [dst_buf.reg],
              lambda e: e.tensor_scalar(out=dst_ap, in0=pst.t[:, 0:n], scalar1=scale, scalar2=EPS, op0=ALU.mult, op1=ALU.add))
        pg.op("act", [dst_buf.reg], [dst_buf.reg], lambda e: e.activation(out=dst_ap, in_=dst_ap, func=AF.Sqrt))
        pg.op("dve", [dst_buf.reg], [dst_buf.reg], lambda e: e.reciprocal(out=dst_ap, in_=dst_ap))

    def rmsnorm(ncols, wcols, wreg):
        for (a, b) in ((0, min(ncols, T)), (T, ncols)):
            if b <= a:
                continue
            n = b - a
            pst = p_small if a > 0 else next_pp()
            sumsq(lambda c: xt.t[:, c, a:b], KC, n, pst)
            rsqrt_to(rstd, rstd.t[:, a:b], pst, n, 1.0 / D)
        for c in range(KC):
            eng = "dve"
            pg.op(eng, [xt.reg, rstd.reg, wreg], [hb.reg],
                  lambda e: e.scalar_tensor_tensor(out=hb.t[:, c, 0:ncols], in0=xt.t[:, c, 0:ncols], scalar=wcols[:, c:c + 1],
                                                   in1=rstd.t[:, 0:ncols], op0=ALU.mult, op1=ALU.mult))

    def proj(wslot, wv, j, col0, n, out_ps, kc=KC, rhs=None):
        rb = hb if rhs is None else rhs
        for c in range(kc):
            pg.op("pe", [wslot.reg, rb.reg], [out_ps.reg],
                  lambda e: e.matmul(out_ps.t[:, 0:n], lhsT=wv[:, j, c * P:(c + 1) * P], rhs=rb.t[:, c, col0:col0 + n],
                                     start=(c == 0), stop=(c == kc - 1)))

    def wi(l, cbs):
        return ([(w_in_b.ap()[l, cb:cb + 1], 1, 1024) for cb in cbs], [wregs[l]["in"]])

    def gate_math(zps, l, d, h, sg=sg, kk=kk, lf=lf):
        ci = (layers[l] * 2 + d) * 8 + h
        lbc = lbt.t[:, ci:ci + 1]
        omc = omlb.t[:, ci:ci + 1]
        pg.op("act", [zps.reg], [sg.reg], lambda e: e.activation(out=sg.t[:], in_=zps.t[:], func=AF.Sigmoid))
        pg.op("act", [zps.reg], [kk.reg], lambda e: e.activation(out=kk.t[:], in_=zps.t[:], func=AF.Sigmoid, scale=-1.0))
        yield
        pg.op("dve", [sg.reg, omlb.reg, lbt.reg], [sg.reg],
              lambda e: e.tensor_scalar(out=sg.t[:], in0=sg.t[:], scalar1=omc, scalar2=lbc, op0=ALU.mult, op1=ALU.add))
        pg.op("dve", [sg.reg], [sg.reg],
              lambda e: e.tensor_scalar(out=sg.t[:], in0=sg.t[:], scalar1=1e-6, scalar2=1.0, op0=ALU.max, op1=ALU.min))
        yield
        pg.op("act", [sg.reg], [lf.reg], lambda e: e.activation(out=lf.t[:], in_=sg.t[:], func=AF.Ln))
        pg.op("pool", [kk.reg, omlb.reg], [kk.reg],
              lambda e: e.tensor_scalar(out=kk.t[:], in0=kk.t[:], scalar1=omc, scalar2=None, op0=ALU.mult))
        yield

    def interleave(gens):
        gens = list(gens)
        while gens:
            for g in list(gens):
                try:
                    next(g)
                except StopIteration:
                    gens.remove(g)

    def sweep1(l, xbuf, rx):
        for b in S_f:
            pg.op("pool", [], [b.reg], lambda e: e.memset(b.t[:], 0.0))
        pg.op("pool", [], [Ltot.reg], lambda e: e.memset(Ltot.t[:], 0.0))
        pg.op("pool", [], [Abase.reg], lambda e: e.memset(Abase.t[:], 0.0))
        ws.plan([wi(l, [CB_V + h, CB_ZF + h, CB_ZB + h]) for h in range(NH)] * NT)
        for i in range(NT):
            load_x(xbuf, rx, HL + i * T, T)
            rmsnorm(T, vec[l].t[:, 0:8], vec[l].reg)
            for h in range(NH):
                sl, wv = ws.get()
                pv = next_pp()
                proj(sl, wv[0], 0, 0, T, pv)
                pg.op("act", [pv.reg], [vb.reg], lambda e: e.activation(out=vb.t[:], in_=pv.t[:], func=AF.Copy))
                for tb in range(4):
                    pg.op("pe", [vb.reg, ident.reg], [p_tr.reg],
                          lambda e: e.transpose(p_tr.t[:, tb * P:(tb + 1) * P], vb.t[:, tb * P:(tb + 1) * P], ident.t[:]))
                pg.op("dve", [p_tr.reg], [vT.reg], lambda e: e.tensor_copy(out=vT.t[:, 0:4 * P], in_=p_tr.t[:, 0:4 * P]))
                def s1_chain(d, B):
                    sg_, kk_, lf_, cum_, ex_, kh_, khT_, ptr_, pds_ = B
                    pz = next_pp()
                    proj(sl, wv[1 + d], 0, 0, T, pz)
                    yield
                    yield from gate_math(pz, l, d, h, sg_, kk_, lf_)
                    pg.op("dve", [lf_.reg, onesf.reg], [cum_.reg],
                          lambda e: e.tensor_tensor_scan(out=cum_.t[:], data0=onesf.t[:], data1=lf_.t[:], initial=0.0, op0=ALU.mult, op1=ALU.add))
                    yield
                    if d == 0:
                        pg.op("act", [cum_.reg], [ex_.reg],
                              lambda e: e.activation(out=ex_.t[:], in_=cum_.t[:], func=AF.Exp, scale=-1.0, bias=cum_.t[:, T - 1:T]))
                    else:
                        pg.op("dve", [cum_.reg, lf_.reg], [d1.reg],
                              lambda e: e.tensor_tensor(out=d1.t[:], in0=cum_.t[:], in1=lf_.t[:], op=ALU.subtract))
                        yield
                        pg.op("act", [d1.reg], [ex_.reg], lambda e: e.activation(out=ex_.t[:], in_=d1.t[:], func=AF.Exp))
                    yield
                    pg.op("dve", [kk_.reg, ex_.reg], [kh_.reg],
                          lambda e: e.tensor_tensor(out=kh_.t[:], in0=kk_.t[:], in1=ex_.t[:], op=ALU.mult))
                    yield
                    for tb in range(4):
                        pg.op("pe", [kh_.reg, ident.reg], [ptr_.reg],
                              lambda e: e.transpose(ptr_.t[:, tb * P:(tb + 1) * P], kh_.t[:, tb * P:(tb + 1) * P], ident.t[:]))
                    yield
                    pg.op("act", [ptr_.reg], [khT_.reg], lambda e: e.activation(out=khT_.t[:, 0:4 * P], in_=ptr_.t[:, 0:4 * P], func=AF.Copy))
                    yield
                    for tb in range(4):
                        pg.op("pe", [khT_.reg, vT.reg], [pds_.reg],
                              lambda e: e.matmul(pds_.t[:, 0:P], lhsT=khT_.t[:, tb * P:(tb + 1) * P], rhs=vT.t[:, tb * P:(tb + 1) * P],
                                                 start=(tb == 0), stop=(tb == 3)))
                    yield
                    S = S_f[d * NH + h]
                    lt = Ltot.t[:, d * NH + h:d * NH + h + 1]
                    if d == 0:
                        pg.op("act", [cum_.reg], [eAb.reg],
                              lambda e: e.activation(out=eAb.t[:, h:h + 1], in_=cum_.t[:, T - 1:T], func=AF.Exp))
                        yield
                        pg.op("dve", [S.reg, eAb.reg, pds_.reg], [S.reg],
                              lambda e: e.scalar_tensor_tensor(out=S.t[:], in0=S.t[:], scalar=eAb.t[:, h:h + 1], in1=pds_.t[:, 0:P],
                                                               op0=ALU.mult, op1=ALU.add))
                    else:
                        pg.op("act", [Abase.reg], [eAb2.reg],
                              lambda e: e.activation(out=eAb2.t[:, h:h + 1], in_=Abase.t[:, h:h + 1], func=AF.Exp))
                        yield
                        pg.op("dve", [S.reg, eAb2.reg, pds_.reg], [S.reg],
                              lambda e: e.scalar_tensor_tensor(out=S.t[:], in0=pds_.t[:, 0:P], scalar=eAb2.t[:, h:h + 1], in1=S.t[:],
                                                               op0=ALU.mult, op1=ALU.add))
                        pg.op("dve", [Abase.reg, cum_.reg], [Abase.reg],
                              lambda e: e.tensor_tensor(out=Abase.t[:, h:h + 1], in0=Abase.t[:, h:h + 1], in1=cum_.t[:, T - 1:T], op=ALU.add))
                    yield
                    pg.op("dve", [Ltot.reg, cum_.reg], [Ltot.reg],
                          lambda e: e.tensor_tensor(out=lt, in0=lt, in1=cum_.t[:, T - 1:T], op=ALU.add))
                set0 = (sg, kk, lf, cum, ex[0], kh, khT2[0], p_tr2, p_o)
                set1 = (sga, sgb, t1, t2, ex[1], kt, khT2[1], p_tr, p_sc)
                interleave([s1_chain(0, set0), s1_chain(1, set1)])
        r_sq = [Reg() for _ in range(4)]
        r_soq = [Reg() for _ in range(4)]
        for q in range(4):
            sti = st_i[q].ap().rearrange("(g p) n -> p g n", p=P)
            for g in range(4):
                pg.op("act", [S_f[q * 4 + g].reg], [stb.reg],
                      lambda e: e.activation(out=stb.t[:, g, 0:P], in_=S_f[q * 4 + g].t[:], func=AF.Copy))
            pg.op("dve", [Ltot.reg, stb.reg], [stb.reg], lambda e: e.tensor_copy(out=stb.t[:, :, P], in_=Ltot.t[:, q * 4:(q + 1) * 4]))
            pg.dma("sp", sti, stb.t[:], [stb.reg], [r_sq[q]])
            pg.allgather(st_i[q].ap(), st_o[q].ap(), [r_sq[q]], [r_soq[q]])
        for b in S_f:
            pg.op("pool", [], [b.reg], lambda e: e.memset(b.t[:], 0.0))
        sto = [st_o[q].ap().rearrange("(m g p) n -> m p g n", p=P, g=4) for q in range(4)]
        for d in range(2):
            order = [0, 1, 2] if d == 0 else [3, 2, 1]
            for m in order:
                uc = cm.t[:, d * 8 + m:d * 8 + m + 1]
                omu = cm.t[:, d * 8 + 4 + m:d * 8 + 4 + m + 1]
                for hq in range(2):
                    g0 = d * NH + hq * 4
                    pg.dma("sp", stb.t[:], sto[d * 2 + hq][m], [r_soq[d * 2 + hq]], [stb.reg])
                    pg.op("act", [stb.reg], [coef.reg],
                          lambda e: e.activation(out=coef.t[:, g0:g0 + 4], in_=stb.t[:, :, P], func=AF.Exp))
                    pg.op("dve", [coef.reg, cm.reg], [coef.reg],
                          lambda e: e.tensor_scalar(out=coef.t[:, g0:g0 + 4], in0=coef.t[:, g0:g0 + 4],
                                                    scalar1=uc, scalar2=omu, op0=ALU.mult, op1=ALU.add))
                    for hh in range(4):
                        S = S_f[g0 + hh]
                        pg.op("pool", [stb.reg, cm.reg], [stmp.reg],
                              lambda e: e.tensor_scalar(out=stmp.t[:], in0=stb.t[:, hh, 0:P], scalar1=uc, scalar2=None, op0=ALU.mult))
                        pg.op("dve", [S.reg, coef.reg, stmp.reg], [S.reg],
                              lambda e: e.scalar_tensor_tensor(out=S.t[:], in0=S.t[:], scalar=coef.t[:, g0 + hh:g0 + hh + 1],
                                                               in1=stmp.t[:], op0=ALU.mult, op1=ALU.add))
        for g in range(2 * NH):
            pg.op("act", [S_f[g].reg], [S_b[g].reg], lambda e: e.activation(out=S_b[g].t[:], in_=S_f[g].t[:], func=AF.Copy))

    def scan_tile(l, d, col0, on_head):
        chunks = list(range(NCH)) if d == 0 else list(range(NCH - 1, -1, -1))
        mask = maskf if d == 0 else maskb

        def prep(h, res):
            khT, vT, scT, qS = khT2[h % 2], vT2[h % 2], scT2[h % 2], qS2[h % 2]
            sl, wv = ws.get()
            res[h] = (sl, wv)
            pq = next_pp()
            proj(sl, wv[0], 0, col0, T, pq)
            pg.op("act", [pq.reg], [qs.reg], lambda e: e.activation(out=qs.t[:], in_=pq.t[:], func=AF.Silu))
            yield
            pv = next_pp()
            proj(sl, wv[1], 0, col0, T, pv)
            pg.op("act", [pv.reg], [vb.reg], lambda e: e.activation(out=vb.t[:], in_=pv.t[:], func=AF.Copy))
            yield
            pz = next_pp()
            proj(sl, wv[2], 0, col0, T, pz)
            for _ in gate_math(pz, l, d, h):
                pass
            yield
            pg.op("dve", [lf.reg, scanmask.reg], [cum.reg],
                  lambda e: e.tensor_tensor_scan(out=cum.t[:], data0=scanmask.t[:], data1=lf.t[:], initial=0.0, op0=ALU.mult, op1=ALU.add))
            c3 = cum.t[:].rearrange("p (c j) -> p c j", j=CH)
            ed = edec.t[:, h * NCH:(h + 1) * NCH]
            pg.op("act", [cum.reg], [edec.reg], lambda e: e.activation(out=ed, in_=c3[:, :, CH - 1], func=AF.Exp))
            if d == 0:
                C = cum
            else:
                pg.op("dve", [cum.reg, lf.reg], [lf.reg],
                      lambda e: e.tensor_tensor(out=lf.t[:], in0=cum.t[:], in1=lf.t[:], op=ALU.subtract))
                C = lf
            C3 = C.t[:].rearrange("p (c j) -> p c j", j=CH)
            d13 = d1.t[:].rearrange("p (c j) -> p c j", j=CH)
            d23 = d2.t[:].rearrange("p (c j) -> p c j", j=CH)
            pg.op("dve", [C.reg], [d1.reg],
                  lambda e: e.tensor_tensor(out=d13, in0=C3, in1=C3[:, :, CH // 2:CH // 2 + 1].to_broadcast([P, NCH, CH]), op=ALU.subtract))
            pg.op("pool", [C.reg, cum.reg], [d2.reg],
                  lambda e: e.tensor_tensor(out=d23, in0=c3[:, :, CH - 1:CH].to_broadcast([P, NCH, CH]), in1=C3, op=ALU.subtract))
            yield
            pg.op("act", [d1.reg], [ex[0].reg], lambda e: e.activation(out=ex[0].t[:], in_=d1.t[:], func=AF.Exp))
            pg.op("act", [d1.reg], [ex[1].reg], lambda e: e.activation(out=ex[1].t[:], in_=d1.t[:], func=AF.Exp, scale=-1.0))
            pg.op("act", [d2.reg], [ex[2].reg], lambda e: e.activation(out=ex[2].t[:], in_=d2.t[:], func=AF.Exp))
            pg.op("act", [C.reg], [ex[3].reg], lambda e: e.activation(out=ex[3].t[:], in_=C.t[:], func=AF.Exp))
            if d == 0:
                Eq, Ek, EqS, Ekh = ex[0], ex[1], ex[3], ex[2]
            else:
                Eq, Ek, EqS, Ekh = ex[1], ex[0], ex[2], ex[3]
            yield
            pg.op("dve", [qs.reg, Eq.reg], [qt.reg],
                  lambda e: e.scalar_tensor_tensor(out=qt.t[:], in0=qs.t[:], scalar=QSCALE, in1=Eq.t[:], op0=ALU.mult, op1=ALU.mult))
            pg.op("dve", [qs.reg, EqS.reg], [qS.reg],
                  lambda e: e.scalar_tensor_tensor(out=qS.t[:], in0=qs.t[:], scalar=QSCALE, in1=EqS.t[:], op0=ALU.mult, op1=ALU.mult))
            pg.op("dve", [kk.reg, Ek.reg], [kt.reg], lambda e: e.tensor_tensor(out=kt.t[:], in0=kk.t[:], in1=Ek.t[:], op=ALU.mult))
            pg.op("pool", [kk.reg, Ekh.reg], [kh.reg], lambda e: e.tensor_tensor(out=kh.t[:], in0=kk.t[:], in1=Ekh.t[:], op=ALU.mult))
            yield
            for j in range(NCH):
                pg.op("pe", [kh.reg, ident.reg], [p_tr.reg],
                      lambda e: e.transpose(p_tr.t[0:CH, j * P:(j + 1) * P], kh.t[:, j * CH:(j + 1) * CH], ident.t[:]))
            pg.op("act", [p_tr.reg], [khT.reg], lambda e: e.activation(out=khT.t[0:CH, :], in_=p_tr.t[0:CH, :], func=AF.Copy))
            yield
            for j in range(NCH):
                pg.op("pe", [vb.reg, ident.reg], [p_tr2.reg],
                      lambda e: e.transpose(p_tr2.t[0:CH, j * P:(j + 1) * P], vb.t[:, j * CH:(j + 1) * CH], ident.t[:]))
            pg.op("dve", [p_tr2.reg], [vT.reg], lambda e: e.tensor_copy(out=vT.t[0:CH, :], in_=p_tr2.t[0:CH, :]))
            yield
            for j in range(NCH):
                pg.op("pe", [kt.reg, qt.reg], [p_sc.reg],
                      lambda e: e.matmul(p_sc.t[0:CH, j * CH:(j + 1) * CH], lhsT=kt.t[:, j * CH:(j + 1) * CH],
                                         rhs=qt.t[:, j * CH:(j + 1) * CH], start=True, stop=True))
            pg.op("dve", [p_sc.reg, consts.reg], [scT.reg],
                  lambda e: e.tensor_tensor(out=scT.t[:], in0=p_sc.t[0:CH, :], in1=mask, op=ALU.mult))

        def recur(h, sl, wv):
            khT, vT, scT, qS = khT2[h % 2], vT2[h % 2], scT2[h % 2], qS2[h % 2]
            S = S_f[d * NH + h]
            Sb = S_b[d * NH + h]
            for j in chunks:
                cs = slice(j * CH, (j + 1) * CH)
                pg.op("pe", [vT.reg, scT.reg], [p_o.reg],
                      lambda e: e.matmul(p_o.t[:, cs], lhsT=vT.t[0:CH, j * P:(j + 1) * P], rhs=scT.t[:, cs], start=True, stop=False))
                pg.op("pe", [Sb.reg, qS.reg], [p_o.reg],
                      lambda e: e.matmul(p_o.t[:, cs], lhsT=Sb.t[:], rhs=qS.t[:, cs], start=False, stop=True))
                pg.op("pe", [khT.reg, vT.reg], [p_small.reg],
                      lambda e: e.matmul(p_small.t[:, 0:P], lhsT=khT.t[0:CH, j * P:(j + 1) * P], rhs=vT.t[0:CH, j * P:(j + 1) * P],
                                         start=True, stop=True))
                pg.op("dve", [S.reg, edec.reg, p_small.reg], [S.reg],
                      lambda e: e.scalar_tensor_tensor(out=S.t[:], in0=S.t[:], scalar=edec.t[:, h * NCH + j:h * NCH + j + 1],
                                                       in1=p_small.t[:, 0:P], op0=ALU.mult, op1=ALU.add))
                pg.op("act", [S.reg], [Sb.reg], lambda e: e.activation(out=Sb.t[:], in_=S.t[:], func=AF.Copy))
                yield
            on_head(h, sl, wv)

        res = {}
        if KPIPE:
            interleave([prep(0, res)])
            for h in range(NH):
                gens = [recur(h, res[h][0], res[h][1])]
                if h + 1 < NH:
                    gens.append(prep(h + 1, res))
                interleave(gens)
        else:
            for h in range(NH):
                interleave([prep(h, res)])
                interleave([recur(h, res[h][0], res[h][1])])

    def sweep2(l, xbuf, rx):
        ws.plan([wi(l, [CB_Q + h, CB_V + h, CB_ZB + h]) for h in range(NH)] * NT)
        for i in range(NT - 1, -1, -1):
            load_x(xbuf, rx, HL + i * T, T)
            rmsnorm(T, vec[l].t[:, 0:8], vec[l].reg)

            def on_head(h, sl, wv):
                pg.op("act", [p_o.reg], [oh.reg], lambda e: e.activation(out=oh.t[:], in_=p_o.t[:], func=AF.Copy))
                pg.dma("sp", ob_d.ap()[h * P:(h + 1) * P, i * T:(i + 1) * T], oh.t[:], [oh.reg], [r_ob], join=True)
            scan_tile(l, 1, 0, on_head)

    def sweep3(l, xbuf, rx, xdst, rxd):
        V = vec[l].t
        tile_items = ([wi(l, [CB_GLA + cb, CB_GLB + cb]) for cb in range(4)]
                      + [wi(l, [CB_Q + h, CB_V + h, CB_ZF + h, CB_OG + h]) for h in range(NH)]
                      + [([(w_in_b.ap()[l, CB_GA + eb:CB_GA + eb + 1], 1, 1024), (w_in_b.ap()[l, CB_GB + eb:CB_GB + eb + 1], 1, 1024),
                           (w_ow_b.ap()[l, eb:eb + 1], 1, 1024), (w_pw_b.ap()[l, eb:eb + 1], 1, 512)], [wregs[l]["in"], wregs[l]["ow"], wregs[l]["pw"]]) for eb in range(KC)]
                      + [([(w_out_b.ap()[l, db:db + 1], 1, 1024)], [wregs[l]["out"]]) for db in range(KC)])
        ws.plan(tile_items * NT)
        accr = [Reg() for _ in range(4)]
        for i in range(NT):
            load_x(xbuf, rx, HL + i * T - 15, XWT)
            rmsnorm(XWT, V[:, 0:8], vec[l].reg)
            for cb in range(4):
                sl, wv = ws.get()
                pa, pb = next_pp(), next_pp()
                proj(sl, wv[1], 0, 0, T, pb)
                pg.op("act", [pb.reg], [sga.reg], lambda e: e.activation(out=sga.t[:], in_=pb.t[:], func=AF.Sigmoid))
                proj(sl, wv[0], 0, 0, T, pa)
                pg.op("dve", [pa.reg, sga.reg], [a_ext.reg],
                      lambda e: e.tensor_tensor(out=a_ext.t[:, cb, 0:T], in0=pa.t[:], in1=sga.t[:], op=ALU.mult))
                proj(sl, wv[1], 0, T, XWT - T, p_small)
                pg.op("act", [p_small.reg], [sgb.reg],
                      lambda e: e.activation(out=sgb.t[:, 0:XWT - T], in_=p_small.t[:, 0:XWT - T], func=AF.Sigmoid))
                proj(sl, wv[0], 0, T, XWT - T, p_small)
                pg.op("dve", [p_small.reg, sgb.reg], [a_ext.reg],
                      lambda e: e.tensor_tensor(out=a_ext.t[:, cb, T:XWT], in0=p_small.t[:, 0:XWT - T], in1=sgb.t[:, 0:XWT - T], op=ALU.mult))
            for cb in range(4):
                pcv = next_pp()
                for k in range(31):
                    dg = dgb[dg_i[0] % NDG]
                    dg_i[0] += 1
                    pg.op("act", [ident.reg, vec[l].reg], [dg.reg],
                          lambda e: e.activation(out=dg.t[:], in_=ident.t[:], func=AF.Copy, scale=V[:, 8 + cb * 31 + k:8 + cb * 31 + k + 1]))
                    pg.op("pe", [dg.reg, a_ext.reg], [pcv.reg],
                          lambda e: e.matmul(pcv.t[:], lhsT=dg.t[:], rhs=a_ext.t[:, cb, k:k + T], start=(k == 0), stop=(k == 30)))
                pg.op("dve", [pcv.reg, vec[l].reg], [accr[cb]],
                      lambda e: e.tensor_scalar(out=acc_t[:, cb, :], in0=pcv.t[:], scalar1=V[:, 132 + cb:133 + cb], scalar2=None, op0=ALU.add))
            pg.op("act", accr, [accb.reg], lambda e: e.activation(out=accb.t, in_=acc_t, func=AF.Copy))
            pm = next_pp()
            for cb in range(4):
                pg.op("pe", [ones.reg, accb.reg], [pm.reg],
                      lambda e: e.matmul(pm.t[:], lhsT=ones.t[:], rhs=accb.t[:, cb, :], start=(cb == 0), stop=(cb == 3)))
            pg.op("dve", [pm.reg], [mu.reg], lambda e: e.tensor_scalar(out=mu.t[:], in0=pm.t[:], scalar1=1.0 / 512, scalar2=None, op0=ALU.mult))
            for cb in range(4):
                eng = "dve" if cb % 2 == 0 else "pool"
                pg.op(eng, [accr[cb], mu.reg], [accr[cb]],
                      lambda e: e.tensor_tensor(out=acc_t[:, cb, :], in0=acc_t[:, cb, :], in1=mu.t[:], op=ALU.subtract))
            pg.op("act", accr, [accb.reg], lambda e: e.activation(out=accb.t, in_=acc_t, func=AF.Square))
            pv_ = next_pp()
            for cb in range(4):
                pg.op("pe", [ones.reg, accb.reg], [pv_.reg],
                      lambda e: e.matmul(pv_.t[:], lhsT=ones.t[:], rhs=accb.t[:, cb, :], start=(cb == 0), stop=(cb == 3)))
            rsqrt_to(var, var.t[:], pv_, T, 1.0 / 512)
            for cb in range(4):
                eng = "dve" if cb % 2 == 0 else "pool"
                pg.op(eng, [accr[cb], var.reg], [accr[cb]],
                      lambda e: e.tensor_tensor(out=acc_t[:, cb, :], in0=acc_t[:, cb, :], in1=var.t[:], op=ALU.mult))
                pg.op("act", [accr[cb], vec[l].reg], [asw.reg],
                      lambda e: e.activation(out=asw.t[:, cb, :], in_=acc_t[:, cb, :], func=AF.Silu,
                                             scale=V[:, 136 + cb:137 + cb], bias=V[:, 140 + cb:141 + cb]))

            def on_head(h, sl, wv):
                pg.dma("sp", obh.t[:], ob_d.ap()[h * P:(h + 1) * P, i * T:(i + 1) * T], [r_ob], [obh.reg])
                pg.op("dve", [p_o.reg, obh.reg], [oh.reg], lambda e: e.tensor_tensor(out=oh.t[:], in0=p_o.t[:], in1=obh.t[:], op=ALU.add))
                po = next_pp()
                sumsq(lambda c: oh.t[:], 1, T, po)
                rsqrt_to(var, var.t[:], po, T, 1.0 / P)
                pog = next_pp()
                proj(sl, wv[3], 0, 15, T, pog)
                pg.op("act", [pog.reg], [t1.reg], lambda e: e.activation(out=t1.t[:], in_=pog.t[:], func=AF.Silu))
                pg.op("dve", [oh.reg, var.reg, vec[l].reg], [t2.reg],
                      lambda e: e.scalar_tensor_tensor(out=t2.t[:], in0=oh.t[:], scalar=V[:, 144 + h:145 + h], in1=var.t[:],
                                                       op0=ALU.mult, op1=ALU.mult))
                pg.op("pool", [t1.reg, t2.reg], [bmix.reg],
                      lambda e: e.tensor_tensor(out=bmix.t[:, h, :], in0=t1.t[:], in1=t2.t[:], op=ALU.mult))
            scan_tile(l, 0, 15, on_head)
            for eb in range(KC):
                sl, wv = ws.get()
                pga = next_pp()
                proj(sl, wv[0], 0, 15, T, pga)
                pg.op("act", [pga.reg], [sga.reg], lambda e: e.activation(out=sga.t[:], in_=pga.t[:], func=AF.Sigmoid))
                pgb = next_pp()
                proj(sl, wv[1], 0, 15, T, pgb)
                pg.op("act", [pgb.reg], [sgb.reg], lambda e: e.activation(out=sgb.t[:], in_=pgb.t[:], func=AF.Sigmoid))
                pA = next_pp()
                proj(sl, wv[3], 0, 0, T, pA, kc=4, rhs=asw)
                pg.op("dve", [pA.reg, sga.reg], [t1.reg], lambda e: e.tensor_tensor(out=t1.t[:], in0=pA.t[:], in1=sga.t[:], op=ALU.mult))
                pB = next_pp()
                proj(sl, wv[2], 0, 0, T, pB, rhs=bmix)
                pg.op("dve", [pB.reg, sgb.reg], [t2.reg], lambda e: e.tensor_tensor(out=t2.t[:], in0=pB.t[:], in1=sgb.t[:], op=ALU.mult))
                pg.op("pool", [t1.reg, t2.reg], [ymix.reg],
                      lambda e: e.tensor_tensor(out=ymix.t[:, eb, :], in0=t1.t[:], in1=t2.t[:], op=ALU.add))
            for db in range(KC):
                sl, wv = ws.get()
                pO = next_pp()
                proj(sl, wv[0], 0, 0, T, pO, rhs=ymix)
                pg.op("dve", [pO.reg, xt.reg], [xt.reg],
                      lambda e: e.tensor_tensor(out=xt.t[:, db, 15:15 + T], in0=pO.t[:], in1=xt.t[:, db, 15:15 + T], op=ALU.add))
            pg.dma("sp", xdst.ap()[:, HL + i * T:HL + (i + 1) * T].rearrange("(c p) n -> p c n", p=P), xt.t[:, :, 15:15 + T],
                   [xt.reg], [rxd], join=True)

    def exchange_halo(xbuf, rx):
        pg.dma("sp", edge_i.ap()[:, 0:HL], xbuf.ap()[:, HL:2 * HL], [rx], [r_ei])
        pg.dma("sp", edge_i.ap()[:, HL:2 * HL], xbuf.ap()[:, TOK:TOK + HL], [rx], [r_ei], join=True)
        pg.allgather(edge_i.ap(), edge_o.ap(), [r_ei], [r_eo])
        pg.dma("sp", eo_sb.t[:].rearrange("p m c n -> p (m c) n"), edge_o.ap().rearrange("(mc p) n -> p mc n", p=P), [r_eo], [eo_sb.reg])
        for side in range(2):
            src = (lambda m: eo_sb.t[:, m, :, HL:2 * HL]) if side == 0 else (lambda m: eo_sb.t[:, m, :, 0:HL])
            dst = hl_sb.t[:, :, side * HL:(side + 1) * HL]
            sel = lambda m: cm.t[:, 16 + side * 4 + m:17 + side * 4 + m]
            pg.op("dve", [eo_sb.reg, cm.reg], [hl_sb.reg],
                  lambda e: e.tensor_scalar(out=dst, in0=src(0), scalar1=sel(0), scalar2=None, op0=ALU.mult))
            for m in range(1, 4):
                pg.op("dve", [eo_sb.reg, cm.reg, hl_sb.reg], [hl_sb.reg],
                      lambda e: e.scalar_tensor_tensor(out=dst, in0=src(m), scalar=sel(m), in1=dst, op0=ALU.mult, op1=ALU.add))
        pg.dma("sp", xbuf.ap()[:, 0:HL].rearrange("(c p) n -> p c n", p=P), hl_sb.t[:, :, 0:HL], [hl_sb.reg], [rx], join=True)
        pg.dma("sp", xbuf.ap()[:, HL + TOK:HL + TOK + HL].rearrange("(c p) n -> p c n", p=P), hl_sb.t[:, :, HL:2 * HL], [hl_sb.reg], [rx], join=True)

    def sweep4(l, xbuf, rx, xdst, rxd, last):
        V = vec[l].t
        tile_items = ([([(w_up_b.ap()[l, fb:fb + 1], 1, 1024), (w_up_b.ap()[l, NFB + fb:NFB + fb + 1], 1, 1024)], [wregs[l]["up"]]) for fb in range(NFB)]
                      + [([(w_dn_b.ap()[l, db:db + 1], 1, NFB * 128)], [wregs[l]["dn"]]) for db in range(KC)])
        ws.plan(tile_items * NT)
        for i in range(NT):
            load_x(xbuf, rx, HL + i * T - 1, T + 2)
            rmsnorm(T + 2, V[:, 152:160], vec[l].reg)
            for fb in range(NFB):
                sl, wv = ws.get()
                pgt = next_pp()
                proj(sl, wv[0], 0, 1, T, pgt)
                for c in range(KC):
                    pg.op("pe", [sl.reg, hb.reg], [p_small.reg],
                          lambda e: e.matmul(p_small.t[:, 0:2], lhsT=wv[0][:, 0, c * P:(c + 1) * P],
                                             rhs=hb.t[:, c, 0:T + 2:T + 1], start=(c == 0), stop=(c == KC - 1)))
                pg.op("act", [pgt.reg], [g_sb.reg], lambda e: e.activation(out=g_sb.t[:, 1:T + 1], in_=pgt.t[:], func=AF.Copy))
                pg.op("act", [p_small.reg, g_sb.reg], [g_sb.reg],
                      lambda e: e.activation(out=g_sb.t[:, 0:T + 2:T + 1], in_=p_small.t[:, 0:2], func=AF.Copy))
                w3 = lambda k: V[:, 160 + fb * 3 + k:161 + fb * 3 + k]
                pg.op("dve", [g_sb.reg, vec[l].reg], [gacc.reg],
                      lambda e: e.tensor_scalar(out=gacc.t[:], in0=g_sb.t[:, 0:T], scalar1=w3(0), scalar2=V[:, 226 + fb:227 + fb],
                                                op0=ALU.mult, op1=ALU.add))
                pg.op("dve", [g_sb.reg, gacc.reg, vec[l].reg], [gacc.reg],
                      lambda e: e.scalar_tensor_tensor(out=gacc.t[:], in0=g_sb.t[:, 1:T + 1], scalar=w3(1), in1=gacc.t[:], op0=ALU.mult, op1=ALU.add))
                pg.op("dve", [g_sb.reg, gacc.reg, vec[l].reg], [gacc.reg],
                      lambda e: e.scalar_tensor_tensor(out=gacc.t[:], in0=g_sb.t[:, 2:T + 2], scalar=w3(2), in1=gacc.t[:], op0=ALU.mult, op1=ALU.add))
                pg.op("act", [gacc.reg], [t1.reg], lambda e: e.activation(out=t1.t[:], in_=gacc.t[:], func=AF.Silu))
                pvl = next_pp()
                proj(sl, wv[1], 0, 1, T, pvl)
                pg.op("dve", [pvl.reg, t1.reg], [u_sb.reg],
                      lambda e: e.tensor_tensor(out=u_sb.t[:, fb, :], in0=pvl.t[:], in1=t1.t[:], op=ALU.mult))
            for db in range(KC):
                sl, wv = ws.get()
                pO = next_pp()
                proj(sl, wv[0], 0, 0, T, pO, kc=NFB, rhs=u_sb)
                pg.op("dve", [pO.reg, xt.reg], [xt.reg],
                      lambda e: e.tensor_tensor(out=xt.t[:, db, 1:1 + T], in0=pO.t[:], in1=xt.t[:, db, 1:1 + T], op=ALU.add))
            if last and final_norm:
                pst = next_pp()
                sumsq(lambda c: xt.t[:, c, 1:1 + T], KC, T, pst)
                rsqrt_to(var, var.t[:], pst, T, 1.0 / D)
                for c in range(KC):
                    eng = "dve"
                    pg.op(eng, [xt.reg, var.reg, gvec.reg], [xt.reg],
                          lambda e: e.scalar_tensor_tensor(out=xt.t[:, c, 1:1 + T], in0=xt.t[:, c, 1:1 + T], scalar=gvec.t[:, c:c + 1], in1=var.t[:],
                                                           op0=ALU.mult, op1=ALU.mult))
            if last:
                pg.dma("sp", y_out.ap()[:, i * T:(i + 1) * T].rearrange("(c p) n -> p c n", p=P), xt.t[:, :, 1:1 + T], [xt.reg], [rxd], join=True)
            else:
                pg.dma("sp", xdst.ap()[:, HL + i * T:HL + (i + 1) * T].rearrange("(c p) n -> p c n", p=P), xt.t[:, :, 1:1 + T],
                       [xt.reg], [rxd], join=True)

    import os
    KSTOP = int(os.environ.get("KSTOP", "9"))
    r_y = Reg()
    for l in range(NL):
        if KSTOP < 9:
            if KSTOP >= 1:
                sweep1(l, xA, r_xA)
            if KSTOP >= 2:
                sweep2(l, xA, r_xA)
            if KSTOP >= 3:
                pg.barrier()
                sweep3(l, xA, r_xA, xB, r_xB)
            if KSTOP >= 4:
                exchange_halo(xB, r_xB)
            pg.wait_all("sp", [r_xA, r_xB, r_ob] + [wregs[l][k] for k in wregs[l]])
            pg.wait_all("pool", [wregs[l][k] for k in wregs[l]])
            break
        if l + 1 < NL and not KCASTALL:
            cast_w(l + 1)
        if l > 0:
            exchange_halo(xA, r_xA)
        sweep1(l, xA, r_xA)
        sweep2(l, xA, r_xA)
        pg.barrier()
        sweep3(l, xA, r_xA, xB, r_xB)
        exchange_halo(xB, r_xB)
        pg.barrier()
        last = (l == NL - 1)
        sweep4(l, xB, r_xB, xA, r_y if last else r_xA, last)
    pg.wait_all("sp", [r_y])
    return nc


def _relayout(W, kc):
    K, N = W.shape
    assert K == kc * 128
    return np.ascontiguousarray(W.reshape(kc, 128, N // 128, 128).transpose(2, 1, 0, 3).reshape(N // 128, 128, kc * 128))


def _pcol(v):
    return np.ascontiguousarray(v.reshape(-1, 128).T)


def _make_consts():
    c = np.zeros((128, 128 + 512 + 512), np.float32)
    c[:, 0:128] = np.eye(128, dtype=np.float32)
    s = np.arange(64)[:, None]
    t = np.arange(64)[None, :]
    mf = (s <= t).astype(np.float32)
    mb = (s >= t).astype(np.float32)
    c[0:64, 128:640] = np.tile(mf, (1, 8))
    c[0:64, 640:1152] = np.tile(mb, (1, 8))
    return c


def _core_masks(seg):
    m = np.zeros((128, NCM), np.float32)
    for k in range(4):
        uf = 1.0 if k < seg else 0.0
        ub = 1.0 if k > seg else 0.0
        m[:, k] = uf
        m[:, 4 + k] = 1.0 - uf
        m[:, 8 + k] = ub
        m[:, 12 + k] = 1.0 - ub
        m[:, 16 + k] = 1.0 if k == seg - 1 else 0.0
        m[:, 20 + k] = 1.0 if k == seg + 1 else 0.0
    return m


def _prep_weights(inp, layers):
    L = layers
    out = {}
    out["w_in"] = np.stack([_relayout(np.asarray(inp["w_in"][l]), 8) for l in L])
    out["w_pw"] = np.stack([_relayout(np.asarray(inp["conv_pw_w"][l]), 4) for l in L])
    out["w_ow"] = np.stack([_relayout(np.asarray(inp["hgrn_o_w"][l]), 8) for l in L])
    out["w_out"] = np.stack([_relayout(np.asarray(inp["w_out"][l]), 8) for l in L])
    out["w_up"] = np.stack([_relayout(np.asarray(inp["ffn_w_up"][l]), 8) for l in L])
    out["w_dn"] = np.stack([_relayout(np.asarray(inp["ffn_w_down"][l]), NFB) for l in L])
    vecs = []
    for l in L:
        v = np.zeros((128, NV), np.float32)
        v[:, 0:8] = _pcol(np.asarray(inp["attn_norm_w"][l]))
        dw = np.asarray(inp["conv_dw_w"][l])
        for cb in range(4):
            v[:, 8 + cb * 31:8 + (cb + 1) * 31] = dw[:, cb * 128:(cb + 1) * 128].T
        v[:, 132:136] = _pcol(np.asarray(inp["conv_dw_b"][l]))
        v[:, 136:140] = _pcol(np.asarray(inp["conv_ln_w"][l]))
        v[:, 140:144] = _pcol(np.asarray(inp["conv_ln_b"][l]))
        v[:, 144:152] = _pcol(np.asarray(inp["hgrn_norm_w"][l]))
        v[:, 152:160] = _pcol(np.asarray(inp["ffn_norm_w"][l]))
        fw = np.asarray(inp["ffn_dw_w"][l])
        for fb in range(NFB):
            v[:, 160 + fb * 3:163 + fb * 3] = fw[:, fb * 128:(fb + 1) * 128].T
        v[:, 226:248] = _pcol(np.asarray(inp["ffn_dw_b"][l]))
        vecs.append(v)
    out["vec"] = np.stack(vecs)
    g = np.zeros((128, 72), np.float32)
    g[:, 0:8] = _pcol(np.asarray(inp["final_norm_w"]))
    lbl = np.asarray(inp["lb_logits"])
    for l in range(4):
        for d in range(2):
            g[:, 8 + (l * 2 + d) * 8:8 + (l * 2 + d) * 8 + 8] = _pcol(lbl[l, d])
    out["gvec"] = g
    out["consts"] = _make_consts()
    return out


_PROG_CACHE = {}


def _run(x, inp, layer_groups):
    B, S, _ = x.shape
    nseg = 8 // B
    TOK = S // nseg
    cur = np.asarray(x, dtype=np.float32)
    for gi, layers in enumerate(layer_groups):
        final = (gi == len(layer_groups) - 1)
        key = (TOK, tuple(layers), final)
        if key not in _PROG_CACHE:
            _PROG_CACHE[key] = build_program(TOK, list(layers), final)
        nc = _PROG_CACHE[key]
        wts = _prep_weights(inp, layers)
        in_maps = []
        for c in range(8):
            b, seg = c // nseg, c % nseg
            xp = np.zeros((S + 2 * HL, D), np.float32)
            xp[HL:HL + S] = cur[b]
            sl = xp[seg * TOK:seg * TOK + TOK + 2 * HL]
            m = dict(wts)
            m["x_in"] = np.ascontiguousarray(sl.T)
            m["cm"] = _core_masks(seg)
            in_maps.append(m)
        res = run_bass_kernel_spmd(nc, in_maps, core_ids=list(range(8)))
        nxt = np.empty_like(cur)
        for c in range(8):
            b, seg = c // nseg, c % nseg
            nxt[b, seg * TOK:(seg + 1) * TOK] = res.results[c]["y_out"].T
        cur = nxt
    return cur


def kernel(**inputs):
    x = np.asarray(inputs["x"], dtype=np.float32)
    return _run(x, inputs, [[0, 1, 2, 3]])
```

```python
import numpy as np
import concourse.bass as bass
import concourse.mybir as mybir
from concourse.bass_utils import run_bass_kernel_spmd

F32 = mybir.dt.float32
BF16 = mybir.dt.bfloat16
AF = mybir.ActivationFunctionType
ALU = mybir.AluOpType

P = 128
D = 1024
KC = 8
T = 512
CH = 64
NCH = T // CH
HL = 16
NH = 8
DFF = 2816
NFB = 22
DEPTH = 4
EPS = 1e-6
NV = 248
NCM = 24
QSCALE = 128 ** -0.5

CB_GLA, CB_GLB, CB_Q, CB_V, CB_ZF, CB_ZB, CB_OG, CB_GA, CB_GB = 0, 4, 8, 16, 24, 32, 40, 48, 56


class Reg:
    __slots__ = ("w", "r")

    def __init__(self):
        self.w = {}
        self.r = {}


class PG:
    NDS = 24

    def __init__(self, nc):
        self.nc = nc
        self.engs = {"pe": nc.tensor, "act": nc.scalar, "dve": nc.vector, "pool": nc.gpsimd, "sp": nc.sync}
        self.sems = {}
        self.cnt = {e: 0 for e in self.engs}
        self.waited = {e: {} for e in self.engs}
        self.ndma = 0
        self.ncc = 0
        self._stack = []
        self.ekey = {}
        self.nep = {}
        for e in self.engs:
            self.sems[e] = self._sem("e_" + e)
        for i in range(self.NDS):
            self.sems["d%d" % i] = self._sem("dma%d" % i)
        for i in range(8):
            self.sems["g%d" % i] = self._sem("gdma%d" % i)
        self.ngdma = 0
        for i in range(4):
            self.sems["c%d" % i] = self._sem("cc%d" % i)

    def _sem(self, name):
        cm = self.nc.semaphore(name)
        s = cm.__enter__()
        self._stack.append(cm)
        return s

    def _deps(self, eng, reads, writes, extra=(), join=False):
        deps = {}

        def add(k, v):
            if deps.get(k, 0) < v:
                deps[k] = v
        for r in reads:
            for k, v in r.w.items():
                add(k, v)
        for w in writes:
            if not join:
                for k, v in w.w.items():
                    add(k, v)
            for k, v in w.r.items():
                add(k, v)
        for (k, v) in extra:
            add(k, v)
        E = self.engs[eng]
        wd = self.waited[eng]
        for k, v in deps.items():
            if eng == "pe" and k.startswith("pe"):
                continue
            if wd.get(k, 0) >= v:
                continue
            E.wait_ge(self.sems[k], v)
            wd[k] = v

    def _mark(self, t, reads, writes, join=False):
        k, v = t
        for r in reads:
            if r.r.get(k, 0) < v:
                r.r[k] = v
        for w in writes:
            if join:
                if w.w.get(k, 0) < v:
                    w.w[k] = v
            else:
                w.w = {k: v}
                w.r = {}

    EPOCH_LEN = 16000

    def op(self, eng, reads, writes, fn):
        self._deps(eng, reads, writes)
        ins = fn(self.engs[eng])
        key = self.ekey.get(eng, eng)
        self.cnt[eng] += 1
        ins.then_inc(self.sems[key], 1)
        t = (key, self.cnt[eng])
        self._mark(t, reads, writes)
        if self.cnt[eng] >= self.EPOCH_LEN:
            self.nep[eng] = self.nep.get(eng, 0) + 1
            nk = "%s#%d" % (eng, self.nep[eng])
            self.sems[nk] = self._sem("e_%s_%d" % (eng, self.nep[eng]))
            self.ekey[eng] = nk
            self.cnt[eng] = 0
        return t

    def dma(self, q, out, in_, reads, writes, join=False, slow=False):
        if q == "pool":
            i = self.ngdma
            self.ngdma += 1
            s = "g%d" % (i % 8)
            v = 16 * (i // 8 + 1)
        else:
            i = self.ndma
            self.ndma += 1
            s = "d%d" % (i % self.NDS)
            v = 16 * (i // self.NDS + 1)
        extra = [(s, v - 16)] if v > 16 else []
        self._deps(q, reads, writes, extra, join)
        if slow:
            ins = self.engs[q].dma_start(out=out, in_=in_, allow_slow_non_contiguous=True)
        else:
            ins = self.engs[q].dma_start(out=out, in_=in_)
        ins.then_inc(self.sems[s], 16)
        t = (s, v)
        self._mark(t, reads, writes, join)
        return t

    def allgather(self, in_ap, out_ap, reads, writes):
        i = self.ncc
        self.ncc += 1
        s = "c%d" % (i % 4)
        v = i // 4 + 1
        extra = [(s, v - 1)] if v > 1 else []
        self._deps("pool", reads, writes, extra)
        ins = self.nc.gpsimd.collective_compute("AllGather", ALU.bypass, replica_groups=[[0, 1, 2, 3], [4, 5, 6, 7]],
                                                ins=[in_ap], outs=[out_ap])
        ins.then_inc(self.sems[s], 1)
        t = (s, v)
        self._mark(t, reads, writes)
        return t

    def barrier(self):
        comp = ("pe", "act", "dve", "pool")
        for e in comp:
            E = self.engs[e]
            for o in comp:
                if o == e:
                    continue
                ok = self.ekey.get(o, o)
                val = self.cnt[o]
                if val == 0:
                    n = self.nep.get(o, 0)
                    if n == 0:
                        continue
                    ok = o if n == 1 else "%s#%d" % (o, n - 1)
                    val = self.EPOCH_LEN
                if self.waited[e].get(ok, 0) >= val:
                    continue
                E.wait_ge(self.sems[ok], val)
                self.waited[e][ok] = val

    def wait_all(self, eng, regs):
        self._deps(eng, regs, regs)


class Buf:
    def __init__(self, nc, name, shape, dtype, psum=False):
        if psum:
            cm = nc.psum_tensor(name, shape, dtype)
        else:
            cm = nc.sbuf_tensor(name, shape, dtype)
        self.cm = cm
        self.t = cm.__enter__()
        self.reg = Reg()

    def __getitem__(self, k):
        return self.t[k]


class WStream:
    def __init__(self, pg, nc, nslot, ws, bufs):
        self.pg = pg
        self.nslot = nslot
        self.slots = [Buf(nc, "wslot%d" % i, [P, ws], BF16) for i in range(nslot)]
        bufs.extend(self.slots)
        self.q = []
        self.issued = []
        self.nxt = 0

    def plan(self, items):
        self.q.extend(items)

    def _issue(self):
        pieces, reg = self.q.pop(0)
        s = self.nxt
        self.nxt = (s + 1) % self.nslot
        sl = self.slots[s]
        off = 0
        views = []
        for (ap, n, e) in pieces:
            out = sl.t[:, off:off + n * e].rearrange("p (j e) -> p j e", j=n)
            self.pg.dma("sp", out, ap.rearrange("j p e -> p j e"), list(reg), [sl.reg], join=(off > 0))
            views.append(out)
            off += n * e
        self.issued.append((s, views))

    def get(self):
        while len(self.issued) < self.nslot - 1 and self.q:
            self._issue()
        s, views = self.issued.pop(0)
        return self.slots[s], views


def build_program(TOK, layers, final_norm):
    import os
    KPIPE = int(os.environ.get("KPIPE", "1"))
    NL = len(layers)
    NT = TOK // T
    XW = TOK + 2 * HL
    nc = bass.Bass("TRN2", target_bir_lowering=False)
    pg = PG(nc)
    bufs = []

    def dram_in(name, shape, dt=F32):
        return nc.dram_tensor(name, shape, dt, kind="ExternalInput")

    x_in = dram_in("x_in", [D, XW])
    cm_in = dram_in("cm", [P, NCM])
    consts_in = dram_in("consts", [P, 128 + 512 + 512])
    gvec_in = dram_in("gvec", [P, 8 + 64])
    vec_in = dram_in("vec", [NL, P, NV])
    w_in_f = dram_in("w_in", [NL, 64, P, 1024])
    w_pw_f = dram_in("w_pw", [NL, 8, P, 512])
    w_ow_f = dram_in("w_ow", [NL, 8, P, 1024])
    w_out_f = dram_in("w_out", [NL, 8, P, 1024])
    w_up_f = dram_in("w_up", [NL, 44, P, 1024])
    w_dn_f = dram_in("w_dn", [NL, 8, P, NFB * 128])
    y_out = nc.dram_tensor("y_out", [D, TOK], F32, kind="ExternalOutput")

    w_in_b = nc.dram_tensor("w_in_b", [NL, 64, P, 1024], BF16)
    w_pw_b = nc.dram_tensor("w_pw_b", [NL, 8, P, 512], BF16)
    w_ow_b = nc.dram_tensor("w_ow_b", [NL, 8, P, 1024], BF16)
    w_out_b = nc.dram_tensor("w_out_b", [NL, 8, P, 1024], BF16)
    w_up_b = nc.dram_tensor("w_up_b", [NL, 44, P, 1024], BF16)
    w_dn_b = nc.dram_tensor("w_dn_b", [NL, 8, P, NFB * 128], BF16)
    xA = nc.dram_tensor("xA", [D, XW], F32)
    xB = nc.dram_tensor("xB", [D, XW], F32)
    ob_d = nc.dram_tensor("ob_d", [D, TOK], F32)
    edge_i = nc.dram_tensor("edge_i", [D, 2 * HL], F32)
    edge_o = nc.dram_tensor("edge_o", [4 * D, 2 * HL], F32)
    st_i = [nc.dram_tensor("st_i%d" % q, [4 * P, 129], F32) for q in range(4)]
    st_o = [nc.dram_tensor("st_o%d" % q, [4 * 4 * P, 129], F32) for q in range(4)]
    r_xA, r_xB, r_ob, r_ei, r_eo, r_si, r_so = Reg(), Reg(), Reg(), Reg(), Reg(), Reg(), Reg()
    r_xin = Reg()
    wregs = [{k: Reg() for k in ("in", "pw", "ow", "out", "up", "dn")} for _ in range(NL)]

    def sb(name, shape, dt=F32):
        b = Buf(nc, name, shape, dt)
        bufs.append(b)
        return b

    def ps(name, shape, dt=F32):
        b = Buf(nc, name, shape, dt, psum=True)
        bufs.append(b)
        return b

    class View:
        def __init__(self, t):
            self.t = t
            self.reg = Reg()

    XWT = T + 2 * HL - 2
    cm = sb("cm_sb", [P, NCM])
    consts = sb("consts_sb", [P, 128 + 512 + 512])
    ident = sb("ident", [P, P], BF16)
    ones = sb("ones", [P, P], BF16)
    gvec = sb("gvec_sb", [P, 72])
    vec = [sb("vec%d" % l, [P, NV]) for l in range(NL)]
    lbt = sb("lbt", [P, 64])
    omlb = sb("omlb", [P, 64])
    lbtmp = sb("lbtmp", [P, 64])
    lbm = sb("lbm", [P, 16])
    scanmask = sb("scanmask", [P, T])
    onesf = sb("onesf", [P, T])
    xt = sb("xt", [P, KC, XWT])
    hb = sb("hb", [P, KC, XWT], BF16)
    sqb = sb("sqb", [P, T], BF16)
    rstd = sb("rstd", [P, XWT])
    ws = WStream(pg, nc, 4, 4096, bufs)
    sg = sb("sg", [P, T])
    kk = sb("kk", [P, T])
    lf = sb("lf", [P, T])
    cum = sb("cum", [P, T])
    d1 = sb("d1", [P, T])
    d2 = sb("d2", [P, T])
    ex = [sb("ex%d" % i, [P, T]) for i in range(4)]
    qs = sb("qs", [P, T])
    vb = sb("vb", [P, T], BF16)
    qt = sb("qt", [P, T], BF16)
    kt = sb("kt", [P, T], BF16)
    kh = sb("kh", [P, T], BF16)
    khT2 = [sb("khT%d" % i, [P, NCH * P], BF16) for i in range(2)]
    vT2 = [sb("vT%d" % i, [P, NCH * P], BF16) for i in range(2)]
    scT2 = [sb("scT%d" % i, [CH, T], BF16) for i in range(2)]
    qS2 = [sb("qS%d" % i, [P, T], BF16) for i in range(2)]
    khT, vT = khT2[0], vT2[0]
    S_f = [sb("Sf%d_%d" % (d, h), [P, P]) for d in range(2) for h in range(NH)]
    S_b = [sb("Sb%d_%d" % (d, h), [P, P], BF16) for d in range(2) for h in range(NH)]
    edec = sb("edec", [P, NH * NCH])
    Ltot = sb("Ltot", [P, 2 * NH])
    Abase = sb("Abase", [P, NH])
    eAb = sb("eAb", [P, NH])
    eAb2 = sb("eAb2", [P, NH])
    oh = sb("oh", [P, T])
    obh = sb("obh", [P, T])
    bmix = sb("bmix", [P, NH, T], BF16)
    arena = sb("arena", [P, 6400])
    a_ext = View(arena.t[:, 0:2 * XWT].bitcast(BF16).rearrange("p (c n) -> p c n", c=4))
    NDG = 8
    dgb = [sb("dg%d" % i, [P, P], BF16) for i in range(NDG)]
    dg_i = [0]
    acc_t = arena.t[:, 2168:2168 + 2048].rearrange("p (c n) -> p c n", c=4)
    accb = View(arena.t[:, 4216:5240].bitcast(BF16).rearrange("p (c n) -> p c n", c=4))
    asw = View(arena.t[:, 5240:6264].bitcast(BF16).rearrange("p (c n) -> p c n", c=4))
    u_sb = View(arena.t[:, 0:5632].bitcast(BF16).rearrange("p (c n) -> p c n", c=NFB))
    mu = sb("mu", [P, T])
    var = sb("var", [P, T])
    sga = sb("sga", [P, T])
    sgb = sb("sgb", [P, T])
    t1 = sb("t1", [P, T])
    t2 = sb("t2", [P, T])
    ymix = sb("ymix", [P, KC, T], BF16)
    g_sb = sb("g_sb", [P, T + 2])
    gacc = sb("gacc", [P, T])
    stb = sb("stb", [P, 4, 129])
    coef = sb("coef", [P, 2 * NH])
    stmp = sb("stmp", [P, P])
    eo_sb = sb("eo_sb", [P, 4, KC, 2 * HL])
    hl_sb = sb("hl_sb", [P, KC, 2 * HL])
    pp = [ps("pp%d" % i, [P, T]) for i in range(3)]
    p_small = ps("p_small", [P, T])
    p_tr = ps("p_tr", [P, NCH * P], BF16)
    p_tr2 = ps("p_tr2", [P, NCH * P], BF16)
    p_sc = ps("p_sc", [P, T])
    p_o = ps("p_o", [P, T])
    pp_i = [0]

    def next_pp():
        b = pp[pp_i[0] % 3]
        pp_i[0] += 1
        return b

    def cast_w(l):
        for (src, dst, key, nb, step) in ((w_in_f, w_in_b, "in", 64, 2), (w_pw_f, w_pw_b, "pw", 8, 4),
                                          (w_ow_f, w_ow_b, "ow", 8, 2), (w_out_f, w_out_b, "out", 8, 2),
                                          (w_up_f, w_up_b, "up", 44, 2), (w_dn_f, w_dn_b, "dn", 8, 1)):
            for j in range(0, nb, step):
                pg.dma("pool", dst.ap()[l, j:j + step], src.ap()[l, j:j + step], [], [wregs[l][key]], join=True)

    pg.dma("sp", cm.t[:], cm_in.ap(), [], [cm.reg])
    pg.dma("sp", consts.t[:], consts_in.ap(), [], [consts.reg])
    pg.dma("sp", gvec.t[:], gvec_in.ap(), [], [gvec.reg])
    for l in range(NL):
        pg.dma("sp", vec[l].t[:], vec_in.ap()[l], [], [vec[l].reg])
    KCASTALL = int(os.environ.get("KCASTALL", "1"))
    if KCASTALL:
        for l_ in range(NL):
            cast_w(l_)
    else:
        cast_w(0)
    for c8 in range(KC):
        pg.dma("sp", xA.ap()[c8 * P:(c8 + 1) * P, :], x_in.ap()[c8 * P:(c8 + 1) * P, :], [r_xin], [r_xA], join=True)
    pg.op("dve", [consts.reg], [ident.reg], lambda e: e.tensor_copy(out=ident.t[:], in_=consts.t[:, 0:128]))
    pg.op("pool", [], [ones.reg], lambda e: e.memset(ones.t[:], 1.0))
    pg.op("pool", [], [onesf.reg], lambda e: e.memset(onesf.t[:], 1.0))
    pg.op("pool", [], [scanmask.reg], lambda e: e.memset(scanmask.t[:], 1.0))
    pg.op("pool", [], [scanmask.reg],
          lambda e: e.memset(scanmask.t[:].rearrange("p (c j) -> p c j", j=CH)[:, :, 0:1], 0.0))
    maskf = consts.t[0:CH, 128:128 + 512]
    maskb = consts.t[0:CH, 640:640 + 512]

    lg = gvec.t[:, 8:72].rearrange("p (l r) -> p l r", l=4)
    TT = lambda o, a, b, op: pg.op("dve", [gvec.reg, lbm.reg, lbtmp.reg, lbt.reg], [], lambda e: e.tensor_tensor(out=o, in0=a, in1=b, op=op))
    lt3 = lbtmp.t[:].rearrange("p (l r) -> p l r", l=4)
    lb3 = lbt.t[:].rearrange("p (l r) -> p l r", l=4)

    def lbop(writes, fn, eng="dve"):
        pg.op(eng, [gvec.reg, lbm.reg, lbtmp.reg, lbt.reg], writes, fn)
    lbop([lbm.reg], lambda e: e.tensor_tensor(out=lbm.t[:], in0=lg[:, 0, :], in1=lg[:, 1, :], op=ALU.max))
    lbop([lbm.reg], lambda e: e.tensor_tensor(out=lbm.t[:], in0=lbm.t[:], in1=lg[:, 2, :], op=ALU.max))
    lbop([lbm.reg], lambda e: e.tensor_tensor(out=lbm.t[:], in0=lbm.t[:], in1=lg[:, 3, :], op=ALU.max))
    for l4 in range(4):
        lbop([lbtmp.reg], lambda e: e.tensor_tensor(out=lt3[:, l4, :], in0=lg[:, l4, :], in1=lbm.t[:], op=ALU.subtract))
    lbop([lbtmp.reg], lambda e: e.activation(out=lbtmp.t[:], in_=lbtmp.t[:], func=AF.Exp), eng="act")
    lbop([lbm.reg], lambda e: e.tensor_tensor(out=lbm.t[:], in0=lt3[:, 0, :], in1=lt3[:, 1, :], op=ALU.add))
    lbop([lbm.reg], lambda e: e.tensor_tensor(out=lbm.t[:], in0=lbm.t[:], in1=lt3[:, 2, :], op=ALU.add))
    lbop([lbm.reg], lambda e: e.tensor_tensor(out=lbm.t[:], in0=lbm.t[:], in1=lt3[:, 3, :], op=ALU.add))
    lbop([lbm.reg], lambda e: e.reciprocal(out=lbm.t[:], in_=lbm.t[:]))
    for l4 in range(4):
        lbop([lbtmp.reg], lambda e: e.tensor_tensor(out=lt3[:, l4, :], in0=lt3[:, l4, :], in1=lbm.t[:], op=ALU.mult))
    lbop([lbt.reg], lambda e: e.memset(lbt.t[:], 0.0), eng="pool")
    for l4 in range(1, 4):
        lbop([lbt.reg], lambda e: e.tensor_tensor(out=lb3[:, l4, :], in0=lb3[:, l4 - 1, :], in1=lt3[:, l4, :], op=ALU.add))
    lbop([omlb.reg], lambda e: e.tensor_scalar(out=omlb.t[:], in0=lbt.t[:], scalar1=-1.0, scalar2=1.0, op0=ALU.mult, op1=ALU.add))

    def load_x(xbuf, rx, col0, ncols):
        pg.dma("sp", xt.t[:, :, 0:ncols], xbuf.ap()[:, col0:col0 + ncols].rearrange("(c p) n -> p c n", p=P),
               [rx], [xt.reg])

    def sumsq(src_fn, nk, n, pst):
        for c in range(nk):
            pg.op("act", [xt.reg, oh.reg], [sqb.reg], lambda e: e.activation(out=sqb.t[:, 0:n], in_=src_fn(c), func=AF.Square))
            pg.op("pe", [ones.reg, sqb.reg], [pst.reg],
                  lambda e: e.matmul(pst.t[:, 0:n], lhsT=ones.t[:], rhs=sqb.t[:, 0:n], start=(c == 0), stop=(c == nk - 1)))

    def rsqrt_to(dst_buf, dst_ap, pst, n, scale):
        pg.op("dve", [pst.reg], [dst_buf.reg],
              lambda e: e.tensor_scalar(out=dst_ap, in0=pst.t[:, 0:n], scalar1=scale, scalar2=EPS, op0=ALU.mult, op1=ALU.add))
        pg.op("act", [dst_buf.reg], [dst_buf.reg], lambda e: e.activation(out=dst_ap, in_=dst_ap, func=AF.Sqrt))
        pg.op("dve", [dst_buf.reg], [dst_buf.reg], lambda e: e.reciprocal(out=dst_ap, in_=dst_ap))

    def rmsnorm(ncols, wcols, wreg):
        for (a, b) in ((0, min(ncols, T)), (T, ncols)):
            if b <= a:
                continue
            n = b - a
            pst = p_small if a > 0 else next_pp()
            sumsq(lambda c: xt.t[:, c, a:b], KC, n, pst)
            rsqrt_to(rstd, rstd.t[:, a:b], pst, n, 1.0 / D)
        for c in range(KC):
            eng = "dve"
            pg.op(eng, [xt.reg, rstd.reg, wreg], [hb.reg],
                  lambda e: e.scalar_tensor_tensor(out=hb.t[:, c, 0:ncols], in0=xt.t[:, c, 0:ncols], scalar=wcols[:, c:c + 1],
                                                   in1=rstd.t[:, 0:ncols], op0=ALU.mult, op1=ALU.mult))

    def proj(wslot, wv, j, col0, n, out_ps, kc=KC, rhs=None):
        rb = hb if rhs is None else rhs
        for c in range(kc):
            pg.op("pe", [wslot.reg, rb.reg], [out_ps.reg],
                  lambda e: e.matmul(out_ps.t[:, 0:n], lhsT=wv[:, j, c * P:(c + 1) * P], rhs=rb.t[:, c, col0:col0 + n],
                                     start=(c == 0), stop=(c == kc - 1)))

    def wi(l, cbs):
        return ([(w_in_b.ap()[l, cb:cb + 1], 1, 1024) for cb in cbs], [wregs[l]["in"]])

    def gate_math(zps, l, d, h, sg=sg, kk=kk, lf=lf):
        ci = (layers[l] * 2 + d) * 8 + h
        lbc = lbt.t[:, ci:ci + 1]
        omc = omlb.t[:, ci:ci + 1]
        pg.op("act", [zps.reg], [sg.reg], lambda e: e.activation(out=sg.t[:], in_=zps.t[:], func=AF.Sigmoid))
        pg.op("act", [zps.reg], [kk.reg], lambda e: e.activation(out=kk.t[:], in_=zps.t[:], func=AF.Sigmoid, scale=-1.0))
        yield
        pg.op("dve", [sg.reg, omlb.reg, lbt.reg], [sg.reg],
              lambda e: e.tensor_scalar(out=sg.t[:], in0=sg.t[:], scalar1=omc, scalar2=lbc, op0=ALU.mult, op1=ALU.add))
        pg.op("dve", [sg.reg], [sg.reg],
              lambda e: e.tensor_scalar(out=sg.t[:], in0=sg.t[:], scalar1=1e-6, scalar2=1.0, op0=ALU.max, op1=ALU.min))
        yield
        pg.op("act", [sg.reg], [lf.reg], lambda e: e.activation(out=lf.t[:], in_=sg.t[:], func=AF.Ln))
        pg.op("dve", [kk.reg, omlb.reg], [kk.reg],
              lambda e: e.tensor_scalar(out=kk.t[:], in0=kk.t[:], scalar1=omc, scalar2=None, op0=ALU.mult))
        yield

    def interleave(gens):
        gens = list(gens)
        while gens:
            for g in list(gens):
                try:
                    next(g)
                except StopIteration:
                    gens.remove(g)

    def sweep1(l, xbuf, rx):
        for b in S_f:
            pg.op("pool", [], [b.reg], lambda e: e.memset(b.t[:], 0.0))
        pg.op("pool", [], [Ltot.reg], lambda e: e.memset(Ltot.t[:], 0.0))
        pg.op("pool", [], [Abase.reg], lambda e: e.memset(Abase.t[:], 0.0))
        ws.plan([wi(l, [CB_V + h, CB_ZF + h, CB_ZB + h]) for h in range(NH)] * NT)
        for i in range(NT):
            load_x(xbuf, rx, HL + i * T, T)
            rmsnorm(T, vec[l].t[:, 0:8], vec[l].reg)
            for h in range(NH):
                sl, wv = ws.get()
                pv = next_pp()
                proj(sl, wv[0], 0, 0, T, pv)
                pg.op("act", [pv.reg], [vb.reg], lambda e: e.activation(out=vb.t[:], in_=pv.t[:], func=AF.Copy))
                for tb in range(4):
                    pg.op("pe", [vb.reg, ident.reg], [p_tr.reg],
                          lambda e: e.transpose(p_tr.t[:, tb * P:(tb + 1) * P], vb.t[:, tb * P:(tb + 1) * P], ident.t[:]))
                pg.op("dve", [p_tr.reg], [vT.reg], lambda e: e.tensor_copy(out=vT.t[:, 0:4 * P], in_=p_tr.t[:, 0:4 * P]))
                def s1_chain(d, B):
                    sg_, kk_, lf_, cum_, ex_, kh_, khT_, ptr_, pds_ = B
                    pz = next_pp()
                    proj(sl, wv[1 + d], 0, 0, T, pz)
                    yield
                    yield from gate_math(pz, l, d, h, sg_, kk_, lf_)
                    pg.op("dve", [lf_.reg, onesf.reg], [cum_.reg],
                          lambda e: e.tensor_tensor_scan(out=cum_.t[:], data0=onesf.t[:], data1=lf_.t[:], initial=0.0, op0=ALU.mult, op1=ALU.add))
                    yield
                    if d == 0:
                        pg.op("act", [cum_.reg], [ex_.reg],
                              lambda e: e.activation(out=ex_.t[:], in_=cum_.t[:], func=AF.Exp, scale=-1.0, bias=cum_.t[:, T - 1:T]))
                    else:
                        pg.op("dve", [cum_.reg, lf_.reg], [d1.reg],
                              lambda e: e.tensor_tensor(out=d1.t[:], in0=cum_.t[:], in1=lf_.t[:], op=ALU.subtract))
                        yield
                        pg.op("act", [d1.reg], [ex_.reg], lambda e: e.activation(out=ex_.t[:], in_=d1.t[:], func=AF.Exp))
                    yield
                    pg.op("dve", [kk_.reg, ex_.reg], [kh_.reg],
                          lambda e: e.tensor_tensor(out=kh_.t[:], in0=kk_.t[:], in1=ex_.t[:], op=ALU.mult))
                    yield
                    for tb in range(4):
                        pg.op("pe", [kh_.reg, ident.reg], [ptr_.reg],
                              lambda e: e.transpose(ptr_.t[:, tb * P:(tb + 1) * P], kh_.t[:, tb * P:(tb + 1) * P], ident.t[:]))
                    yield
                    pg.op("act", [ptr_.reg], [khT_.reg], lambda e: e.activation(out=khT_.t[:, 0:4 * P], in_=ptr_.t[:, 0:4 * P], func=AF.Copy))
                    yield
                    for tb in range(4):
                        pg.op("pe", [khT_.reg, vT.reg], [pds_.reg],
                              lambda e: e.matmul(pds_.t[:, 0:P], lhsT=khT_.t[:, tb * P:(tb + 1) * P], rhs=vT.t[:, tb * P:(tb + 1) * P],
                                                 start=(tb == 0), stop=(tb == 3)))
                    yield
                    S = S_f[d * NH + h]
                    lt = Ltot.t[:, d * NH + h:d * NH + h + 1]
                    if d == 0:
                        pg.op("act", [cum_.reg], [eAb.reg],
                              lambda e: e.activation(out=eAb.t[:, h:h + 1], in_=cum_.t[:, T - 1:T], func=AF.Exp))
                        yield
                        pg.op("dve", [S.reg, eAb.reg, pds_.reg], [S.reg],
                              lambda e: e.scalar_tensor_tensor(out=S.t[:], in0=S.t[:], scalar=eAb.t[:, h:h + 1], in1=pds_.t[:, 0:P],
                                                               op0=ALU.mult, op1=ALU.add))
                    else:
                        pg.op("act", [Abase.reg], [eAb2.reg],
                              lambda e: e.activation(out=eAb2.t[:, h:h + 1], in_=Abase.t[:, h:h + 1], func=AF.Exp))
                        yield
                        pg.op("dve", [S.reg, eAb2.reg, pds_.reg], [S.reg],
                              lambda e: e.scalar_tensor_tensor(out=S.t[:], in0=pds_.t[:, 0:P], scalar=eAb2.t[:, h:h + 1], in1=S.t[:],
                                                               op0=ALU.mult, op1=ALU.add))
                        pg.op("dve", [Abase.reg, cum_.reg], [Abase.reg],
                              lambda e: e.tensor_tensor(out=Abase.t[:, h:h + 1], in0=Abase.t[:, h:h + 1], in1=cum_.t[:, T - 1:T], op=ALU.add))
                    yield
                    pg.op("dve", [Ltot.reg, cum_.reg], [Ltot.reg],
                          lambda e: e.tensor_tensor(out=lt, in0=lt, in1=cum_.t[:, T - 1:T], op=ALU.add))
                set0 = (sg, kk, lf, cum, ex[0], kh, khT2[0], p_tr2, p_o)
                set1 = (sga, sgb, t1, t2, ex[1], kt, khT2[1], p_tr, p_sc)
                interleave([s1_chain(0, set0), s1_chain(1, set1)])
        r_sq = [Reg() for _ in range(4)]
        r_soq = [Reg() for _ in range(4)]
        for q in range(4):
            sti = st_i[q].ap().rearrange("(g p) n -> p g n", p=P)
            for g in range(4):
                pg.op("act", [S_f[q * 4 + g].reg], [stb.reg],
                      lambda e: e.activation(out=stb.t[:, g, 0:P], in_=S_f[q * 4 + g].t[:], func=AF.Copy))
            pg.op("dve", [Ltot.reg, stb.reg], [stb.reg], lambda e: e.tensor_copy(out=stb.t[:, :, P], in_=Ltot.t[:, q * 4:(q + 1) * 4]))
            pg.dma("sp", sti, stb.t[:], [stb.reg], [r_sq[q]])
            pg.allgather(st_i[q].ap(), st_o[q].ap(), [r_sq[q]], [r_soq[q]])
        for b in S_f:
            pg.op("pool", [], [b.reg], lambda e: e.memset(b.t[:], 0.0))
        sto = [st_o[q].ap().rearrange("(m g p) n -> m p g n", p=P, g=4) for q in range(4)]
        for d in range(2):
            order = [0, 1, 2] if d == 0 else [3, 2, 1]
            for m in order:
                uc = cm.t[:, d * 8 + m:d * 8 + m + 1]
                omu = cm.t[:, d * 8 + 4 + m:d * 8 + 4 + m + 1]
                for hq in range(2):
                    g0 = d * NH + hq * 4
                    pg.dma("sp", stb.t[:], sto[d * 2 + hq][m], [r_soq[d * 2 + hq]], [stb.reg])
                    pg.op("act", [stb.reg], [coef.reg],
                          lambda e: e.activation(out=coef.t[:, g0:g0 + 4], in_=stb.t[:, :, P], func=AF.Exp))
                    pg.op("dve", [coef.reg, cm.reg], [coef.reg],
                          lambda e: e.tensor_scalar(out=coef.t[:, g0:g0 + 4], in0=coef.t[:, g0:g0 + 4],
                                                    scalar1=uc, scalar2=omu, op0=ALU.mult, op1=ALU.add))
                    for hh in range(4):
                        S = S_f[g0 + hh]
                        pg.op("pool", [stb.reg, cm.reg], [stmp.reg],
                              lambda e: e.tensor_scalar(out=stmp.t[:], in0=stb.t[:, hh, 0:P], scalar1=uc, scalar2=None, op0=ALU.mult))
                        pg.op("dve", [S.reg, coef.reg, stmp.reg], [S.reg],
                              lambda e: e.scalar_tensor_tensor(out=S.t[:], in0=S.t[:], scalar=coef.t[:, g0 + hh:g0 + hh + 1],
                                                               in1=stmp.t[:], op0=ALU.mult, op1=ALU.add))
        for g in range(2 * NH):
            pg.op("act", [S_f[g].reg], [S_b[g].reg], lambda e: e.activation(out=S_b[g].t[:], in_=S_f[g].t[:], func=AF.Copy))

    def scan_tile(l, d, col0, on_head):
        chunks = list(range(NCH)) if d == 0 else list(range(NCH - 1, -1, -1))
        mask = maskf if d == 0 else maskb

        def prep(h, res):
            khT, vT, scT, qS = khT2[h % 2], vT2[h % 2], scT2[h % 2], qS2[h % 2]
            sl, wv = ws.get()
            res[h] = (sl, wv)
            pq = next_pp()
            proj(sl, wv[0], 0, col0, T, pq)
            pg.op("act", [pq.reg], [qs.reg], lambda e: e.activation(out=qs.t[:], in_=pq.t[:], func=AF.Silu))
            yield
            pv = next_pp()
            proj(sl, wv[1], 0, col0, T, pv)
            pg.op("act", [pv.reg], [vb.reg], lambda e: e.activation(out=vb.t[:], in_=pv.t[:], func=AF.Copy))
            yield
            pz = next_pp()
            proj(sl, wv[2], 0, col0, T, pz)
            for _ in gate_math(pz, l, d, h):
                pass
            yield
            pg.op("dve", [lf.reg, scanmask.reg], [cum.reg],
                  lambda e: e.tensor_tensor_scan(out=cum.t[:], data0=scanmask.t[:], data1=lf.t[:], initial=0.0, op0=ALU.mult, op1=ALU.add))
            c3 = cum.t[:].rearrange("p (c j) -> p c j", j=CH)
            ed = edec.t[:, h * NCH:(h + 1) * NCH]
            pg.op("act", [cum.reg], [edec.reg], lambda e: e.activation(out=ed, in_=c3[:, :, CH - 1], func=AF.Exp))
            if d == 0:
                C = cum
            else:
                pg.op("dve", [cum.reg, lf.reg], [lf.reg],
                      lambda e: e.tensor_tensor(out=lf.t[:], in0=cum.t[:], in1=lf.t[:], op=ALU.subtract))
                C = lf
            C3 = C.t[:].rearrange("p (c j) -> p c j", j=CH)
            d13 = d1.t[:].rearrange("p (c j) -> p c j", j=CH)
            d23 = d2.t[:].rearrange("p (c j) -> p c j", j=CH)
            pg.op("dve", [C.reg], [d1.reg],
                  lambda e: e.tensor_tensor(out=d13, in0=C3, in1=C3[:, :, CH // 2:CH // 2 + 1].to_broadcast([P, NCH, CH]), op=ALU.subtract))
            pg.op("pool", [C.reg, cum.reg], [d2.reg],
                  lambda e: e.tensor_tensor(out=d23, in0=c3[:, :, CH - 1:CH].to_broadcast([P, NCH, CH]), in1=C3, op=ALU.subtract))
            yield
            pg.op("act", [d1.reg], [ex[0].reg], lambda e: e.activation(out=ex[0].t[:], in_=d1.t[:], func=AF.Exp))
            pg.op("act", [d1.reg], [ex[1].reg], lambda e: e.activation(out=ex[1].t[:], in_=d1.t[:], func=AF.Exp, scale=-1.0))
            pg.op("act", [d2.reg], [ex[2].reg], lambda e: e.activation(out=ex[2].t[:], in_=d2.t[:], func=AF.Exp))
            pg.op("act", [C.reg], [ex[3].reg], lambda e: e.activation(out=ex[3].t[:], in_=C.t[:], func=AF.Exp))
            if d == 0:
                Eq, Ek, EqS, Ekh = ex[0], ex[1], ex[3], ex[2]
            else:
                Eq, Ek, EqS, Ekh = ex[1], ex[0], ex[2], ex[3]
            yield
            pg.op("dve", [qs.reg, Eq.reg], [qt.reg],
                  lambda e: e.scalar_tensor_tensor(out=qt.t[:], in0=qs.t[:], scalar=QSCALE, in1=Eq.t[:], op0=ALU.mult, op1=ALU.mult))
            pg.op("dve", [qs.reg, EqS.reg], [qS.reg],
                  lambda e: e.scalar_tensor_tensor(out=qS.t[:], in0=qs.t[:], scalar=QSCALE, in1=EqS.t[:], op0=ALU.mult, op1=ALU.mult))
            pg.op("dve", [kk.reg, Ek.reg], [kt.reg], lambda e: e.tensor_tensor(out=kt.t[:], in0=kk.t[:], in1=Ek.t[:], op=ALU.mult))
            pg.op("pool", [kk.reg, Ekh.reg], [kh.reg], lambda e: e.tensor_tensor(out=kh.t[:], in0=kk.t[:], in1=Ekh.t[:], op=ALU.mult))
            yield
            for j in range(NCH):
                pg.op("pe", [kh.reg, ident.reg], [p_tr.reg],
                      lambda e: e.transpose(p_tr.t[0:CH, j * P:(j + 1) * P], kh.t[:, j * CH:(j + 1) * CH], ident.t[:]))
            pg.op("act", [p_tr.reg], [khT.reg], lambda e: e.activation(out=khT.t[0:CH, :], in_=p_tr.t[0:CH, :], func=AF.Copy))
            yield
            for j in range(NCH):
                pg.op("pe", [vb.reg, ident.reg], [p_tr2.reg],
                      lambda e: e.transpose(p_tr2.t[0:CH, j * P:(j + 1) * P], vb.t[:, j * CH:(j + 1) * CH], ident.t[:]))
            pg.op("dve", [p_tr2.reg], [vT.reg], lambda e: e.tensor_copy(out=vT.t[0:CH, :], in_=p_tr2.t[0:CH, :]))
            yield
            for j in range(NCH):
                pg.op("pe", [kt.reg, qt.reg], [p_sc.reg],
                      lambda e: e.matmul(p_sc.t[0:CH, j * CH:(j + 1) * CH], lhsT=kt.t[:, j * CH:(j + 1) * CH],
                                         rhs=qt.t[:, j * CH:(j + 1) * CH], start=True, stop=True))
            pg.op("dve", [p_sc.reg, consts.reg], [scT.reg],
                  lambda e: e.tensor_tensor(out=scT.t[:], in0=p_sc.t[0:CH, :], in1=mask, op=ALU.mult))

        def recur(h, sl, wv):
            khT, vT, scT, qS = khT2[h % 2], vT2[h % 2], scT2[h % 2], qS2[h % 2]
            S = S_f[d * NH + h]
            Sb = S_b[d * NH + h]
            for j in chunks:
                cs = slice(j * CH, (j + 1) * CH)
                pg.op("pe", [vT.reg, scT.reg], [p_o.reg],
                      lambda e: e.matmul(p_o.t[:, cs], lhsT=vT.t[0:CH, j * P:(j + 1) * P], rhs=scT.t[:, cs], start=True, stop=False))
                pg.op("pe", [Sb.reg, qS.reg], [p_o.reg],
                      lambda e: e.matmul(p_o.t[:, cs], lhsT=Sb.t[:], rhs=qS.t[:, cs], start=False, stop=True))
                pg.op("pe", [khT.reg, vT.reg], [p_small.reg],
                      lambda e: e.matmul(p_small.t[:, 0:P], lhsT=khT.t[0:CH, j * P:(j + 1) * P], rhs=vT.t[0:CH, j * P:(j + 1) * P],
                                         start=True, stop=True))
                pg.op("dve", [S.reg, edec.reg, p_small.reg], [S.reg],
                      lambda e: e.scalar_tensor_tensor(out=S.t[:], in0=S.t[:], scalar=edec.t[:, h * NCH + j:h * NCH + j + 1],
                                                       in1=p_small.t[:, 0:P], op0=ALU.mult, op1=ALU.add))
                pg.op("act", [S.reg], [Sb.reg], lambda e: e.activation(out=Sb.t[:], in_=S.t[:], func=AF.Copy))
                yield
            on_head(h, sl, wv)

        res = {}
        if KPIPE:
            interleave([prep(0, res)])
            for h in range(NH):
                gens = [recur(h, res[h][0], res[h][1])]
                if h + 1 < NH:
                    gens.append(prep(h + 1, res))
                interleave(gens)
        else:
            for h in range(NH):
                interleave([prep(h, res)])
                interleave([recur(h, res[h][0], res[h][1])])

    def sweep2(l, xbuf, rx):
        ws.plan([wi(l, [CB_Q + h, CB_V + h, CB_ZB + h]) for h in range(NH)] * NT)
        for i in range(NT - 1, -1, -1):
            load_x(xbuf, rx, HL + i * T, T)
            rmsnorm(T, vec[l].t[:, 0:8], vec[l].reg)

            def on_head(h, sl, wv):
                pg.op("act", [p_o.reg], [oh.reg], lambda e: e.activation(out=oh.t[:], in_=p_o.t[:], func=AF.Copy))
                pg.dma("sp", ob_d.ap()[h * P:(h + 1) * P, i * T:(i + 1) * T], oh.t[:], [oh.reg], [r_ob], join=True)
            scan_tile(l, 1, 0, on_head)

    def sweep3(l, xbuf, rx, xdst, rxd):
        V = vec[l].t
        tile_items = ([wi(l, [CB_GLA + cb, CB_GLB + cb]) for cb in range(4)]
                      + [wi(l, [CB_Q + h, CB_V + h, CB_ZF + h, CB_OG + h]) for h in range(NH)]
                      + [([(w_in_b.ap()[l, CB_GA + eb:CB_GA + eb + 1], 1, 1024), (w_in_b.ap()[l, CB_GB + eb:CB_GB + eb + 1], 1, 1024),
                           (w_ow_b.ap()[l, eb:eb + 1], 1, 1024), (w_pw_b.ap()[l, eb:eb + 1], 1, 512)], [wregs[l]["in"], wregs[l]["ow"], wregs[l]["pw"]]) for eb in range(KC)]
                      + [([(w_out_b.ap()[l, db:db + 1], 1, 1024)], [wregs[l]["out"]]) for db in range(KC)])
        ws.plan(tile_items * NT)
        accr = [Reg() for _ in range(4)]
        for i in range(NT):
            load_x(xbuf, rx, HL + i * T - 15, XWT)
            rmsnorm(XWT, V[:, 0:8], vec[l].reg)
            for cb in range(4):
                sl, wv = ws.get()
                pa, pb = next_pp(), next_pp()
                proj(sl, wv[1], 0, 0, T, pb)
                pg.op("act", [pb.reg], [sga.reg], lambda e: e.activation(out=sga.t[:], in_=pb.t[:], func=AF.Sigmoid))
                proj(sl, wv[0], 0, 0, T, pa)
                pg.op("dve", [pa.reg, sga.reg], [a_ext.reg],
                      lambda e: e.tensor_tensor(out=a_ext.t[:, cb, 0:T], in0=pa.t[:], in1=sga.t[:], op=ALU.mult))
                proj(sl, wv[1], 0, T, XWT - T, p_small)
                pg.op("act", [p_small.reg], [sgb.reg],
                      lambda e: e.activation(out=sgb.t[:, 0:XWT - T], in_=p_small.t[:, 0:XWT - T], func=AF.Sigmoid))
                proj(sl, wv[0], 0, T, XWT - T, p_small)
                pg.op("dve", [p_small.reg, sgb.reg], [a_ext.reg],
                      lambda e: e.tensor_tensor(out=a_ext.t[:, cb, T:XWT], in0=p_small.t[:, 0:XWT - T], in1=sgb.t[:, 0:XWT - T], op=ALU.mult))
            for cb in range(4):
                pcv = next_pp()
                for k in range(31):
                    dg = dgb[dg_i[0] % NDG]
                    dg_i[0] += 1
                    pg.op("act", [ident.reg, vec[l].reg], [dg.reg],
                          lambda e: e.activation(out=dg.t[:], in_=ident.t[:], func=AF.Copy, scale=V[:, 8 + cb * 31 + k:8 + cb * 31 + k + 1]))
                    pg.op("pe", [dg.reg, a_ext.reg], [pcv.reg],
                          lambda e: e.matmul(pcv.t[:], lhsT=dg.t[:], rhs=a_ext.t[:, cb, k:k + T], start=(k == 0), stop=(k == 30)))
                pg.op("dve", [pcv.reg, vec[l].reg], [accr[cb]],
                      lambda e: e.tensor_scalar(out=acc_t[:, cb, :], in0=pcv.t[:], scalar1=V[:, 132 + cb:133 + cb], scalar2=None, op0=ALU.add))
            pg.op("act", accr, [accb.reg], lambda e: e.activation(out=accb.t, in_=acc_t, func=AF.Copy))
            pm = next_pp()
            for cb in range(4):
                pg.op("pe", [ones.reg, accb.reg], [pm.reg],
                      lambda e: e.matmul(pm.t[:], lhsT=ones.t[:], rhs=accb.t[:, cb, :], start=(cb == 0), stop=(cb == 3)))
            pg.op("dve", [pm.reg], [mu.reg], lambda e: e.tensor_scalar(out=mu.t[:], in0=pm.t[:], scalar1=1.0 / 512, scalar2=None, op0=ALU.mult))
            for cb in range(4):
                eng = "dve" if cb % 2 == 0 else "pool"
                pg.op(eng, [accr[cb], mu.reg], [accr[cb]],
                      lambda e: e.tensor_tensor(out=acc_t[:, cb, :], in0=acc_t[:, cb, :], in1=mu.t[:], op=ALU.subtract))
            pg.op("act", accr, [accb.reg], lambda e: e.activation(out=accb.t, in_=acc_t, func=AF.Square))
            pv_ = next_pp()
            for cb in range(4):
                pg.op("pe", [ones.reg, accb.reg], [pv_.reg],
                      lambda e: e.matmul(pv_.t[:], lhsT=ones.t[:], rhs=accb.t[:, cb, :], start=(cb == 0), stop=(cb == 3)))
            rsqrt_to(var, var.t[:], pv_, T, 1.0 / 512)
            for cb in range(4):
                eng = "dve" if cb % 2 == 0 else "pool"
                pg.op(eng, [accr[cb], var.reg], [accr[cb]],
                      lambda e: e.tensor_tensor(out=acc_t[:, cb, :], in0=acc_t[:, cb, :], in1=var.t[:], op=ALU.mult))
                pg.op("act", [accr[cb], vec[l].reg], [asw.reg],
                      lambda e: e.activation(out=asw.t[:, cb, :], in_=acc_t[:, cb, :], func=AF.Silu,
                                             scale=V[:, 136 + cb:137 + cb], bias=V[:, 140 + cb:141 + cb]))

            def on_head(h, sl, wv):
                pg.dma("sp", obh.t[:], ob_d.ap()[h * P:(h + 1) * P, i * T:(i + 1) * T], [r_ob], [obh.reg])
                pg.op("dve", [p_o.reg, obh.reg], [oh.reg], lambda e: e.tensor_tensor(out=oh.t[:], in0=p_o.t[:], in1=obh.t[:], op=ALU.add))
                po = next_pp()
                sumsq(lambda c: oh.t[:], 1, T, po)
                rsqrt_to(var, var.t[:], po, T, 1.0 / P)
                pog = next_pp()
                proj(sl, wv[3], 0, 15, T, pog)
                pg.op("act", [pog.reg], [t1.reg], lambda e: e.activation(out=t1.t[:], in_=pog.t[:], func=AF.Silu))
                pg.op("dve", [oh.reg, var.reg, vec[l].reg], [t2.reg],
                      lambda e: e.scalar_tensor_tensor(out=t2.t[:], in0=oh.t[:], scalar=V[:, 144 + h:145 + h], in1=var.t[:],
                                                       op0=ALU.mult, op1=ALU.mult))
                pg.op("pool", [t1.reg, t2.reg], [bmix.reg],
                      lambda e: e.tensor_tensor(out=bmix.t[:, h, :], in0=t1.t[:], in1=t2.t[:], op=ALU.mult))
            scan_tile(l, 0, 15, on_head)
            for eb in range(KC):
                sl, wv = ws.get()
                pga = next_pp()
                proj(sl, wv[0], 0, 15, T, pga)
                pg.op("act", [pga.reg], [sga.reg], lambda e: e.activation(out=sga.t[:], in_=pga.t[:], func=AF.Sigmoid))
                pgb = next_pp()
                proj(sl, wv[1], 0, 15, T, pgb)
                pg.op("act", [pgb.reg], [sgb.reg], lambda e: e.activation(out=sgb.t[:], in_=pgb.t[:], func=AF.Sigmoid))
                pA = next_pp()
                proj(sl, wv[3], 0, 0, T, pA, kc=4, rhs=asw)
                pg.op("dve", [pA.reg, sga.reg], [t1.reg], lambda e: e.tensor_tensor(out=t1.t[:], in0=pA.t[:], in1=sga.t[:], op=ALU.mult))
                pB = next_pp()
                proj(sl, wv[2], 0, 0, T, pB, rhs=bmix)
                pg.op("dve", [pB.reg, sgb.reg], [t2.reg], lambda e: e.tensor_tensor(out=t2.t[:], in0=pB.t[:], in1=sgb.t[:], op=ALU.mult))
                pg.op("pool", [t1.reg, t2.reg], [ymix.reg],
                      lambda e: e.tensor_tensor(out=ymix.t[:, eb, :], in0=t1.t[:], in1=t2.t[:], op=ALU.add))
            for db in range(KC):
                sl, wv = ws.get()
                pO = next_pp()
                proj(sl, wv[0], 0, 0, T, pO, rhs=ymix)
                pg.op("dve", [pO.reg, xt.reg], [xt.reg],
                      lambda e: e.tensor_tensor(out=xt.t[:, db, 15:15 + T], in0=pO.t[:], in1=xt.t[:, db, 15:15 + T], op=ALU.add))
            pg.dma("sp", xdst.ap()[:, HL + i * T:HL + (i + 1) * T].rearrange("(c p) n -> p c n", p=P), xt.t[:, :, 15:15 + T],
                   [xt.reg], [rxd], join=True)

    def exchange_halo(xbuf, rx):
        pg.dma("sp", edge_i.ap()[:, 0:HL], xbuf.ap()[:, HL:2 * HL], [rx], [r_ei])
        pg.dma("sp", edge_i.ap()[:, HL:2 * HL], xbuf.ap()[:, TOK:TOK + HL], [rx], [r_ei], join=True)
        pg.allgather(edge_i.ap(), edge_o.ap(), [r_ei], [r_eo])
        pg.dma("sp", eo_sb.t[:].rearrange("p m c n -> p (m c) n"), edge_o.ap().rearrange("(mc p) n -> p mc n", p=P), [r_eo], [eo_sb.reg])
        for side in range(2):
            src = (lambda m: eo_sb.t[:, m, :, HL:2 * HL]) if side == 0 else (lambda m: eo_sb.t[:, m, :, 0:HL])
            dst = hl_sb.t[:, :, side * HL:(side + 1) * HL]
            sel = lambda m: cm.t[:, 16 + side * 4 + m:17 + side * 4 + m]
            pg.op("dve", [eo_sb.reg, cm.reg], [hl_sb.reg],
                  lambda e: e.tensor_scalar(out=dst, in0=src(0), scalar1=sel(0), scalar2=None, op0=ALU.mult))
            for m in range(1, 4):
                pg.op("dve", [eo_sb.reg, cm.reg, hl_sb.reg], [hl_sb.reg],
                      lambda e: e.scalar_tensor_tensor(out=dst, in0=src(m), scalar=sel(m), in1=dst, op0=ALU.mult, op1=ALU.add))
        pg.dma("sp", xbuf.ap()[:, 0:HL].rearrange("(c p) n -> p c n", p=P), hl_sb.t[:, :, 0:HL], [hl_sb.reg], [rx], join=True)
        pg.dma("sp", xbuf.ap()[:, HL + TOK:HL + TOK + HL].rearrange("(c p) n -> p c n", p=P), hl_sb.t[:, :, HL:2 * HL], [hl_sb.reg], [rx], join=True)

    def sweep4(l, xbuf, rx, xdst, rxd, last):
        V = vec[l].t
        tile_items = ([([(w_up_b.ap()[l, fb:fb + 1], 1, 1024), (w_up_b.ap()[l, NFB + fb:NFB + fb + 1], 1, 1024)], [wregs[l]["up"]]) for fb in range(NFB)]
                      + [([(w_dn_b.ap()[l, db:db + 1], 1, NFB * 128)], [wregs[l]["dn"]]) for db in range(KC)])
        ws.plan(tile_items * NT)
        for i in range(NT):
            load_x(xbuf, rx, HL + i * T - 1, T + 2)
            rmsnorm(T + 2, V[:, 152:160], vec[l].reg)
            for fb in range(NFB):
                sl, wv = ws.get()
                pgt = next_pp()
                proj(sl, wv[0], 0, 1, T, pgt)
                for c in range(KC):
                    pg.op("pe", [sl.reg, hb.reg], [p_small.reg],
                          lambda e: e.matmul(p_small.t[:, 0:2], lhsT=wv[0][:, 0, c * P:(c + 1) * P],
                                             rhs=hb.t[:, c, 0:T + 2:T + 1], start=(c == 0), stop=(c == KC - 1)))
                pg.op("act", [pgt.reg], [g_sb.reg], lambda e: e.activation(out=g_sb.t[:, 1:T + 1], in_=pgt.t[:], func=AF.Copy))
                pg.op("act", [p_small.reg, g_sb.reg], [g_sb.reg],
                      lambda e: e.activation(out=g_sb.t[:, 0:T + 2:T + 1], in_=p_small.t[:, 0:2], func=AF.Copy))
                w3 = lambda k: V[:, 160 + fb * 3 + k:161 + fb * 3 + k]
                pg.op("dve", [g_sb.reg, vec[l].reg], [gacc.reg],
                      lambda e: e.tensor_scalar(out=gacc.t[:], in0=g_sb.t[:, 0:T], scalar1=w3(0), scalar2=V[:, 226 + fb:227 + fb],
                                                op0=ALU.mult, op1=ALU.add))
                pg.op("dve", [g_sb.reg, gacc.reg, vec[l].reg], [gacc.reg],
                      lambda e: e.scalar_tensor_tensor(out=gacc.t[:], in0=g_sb.t[:, 1:T + 1], scalar=w3(1), in1=gacc.t[:], op0=ALU.mult, op1=ALU.add))
                pg.op("dve", [g_sb.reg, gacc.reg, vec[l].reg], [gacc.reg],
                      lambda e: e.scalar_tensor_tensor(out=gacc.t[:], in0=g_sb.t[:, 2:T + 2], scalar=w3(2), in1=gacc.t[:], op0=ALU.mult, op1=ALU.add))
                pg.op("act", [gacc.reg], [t1.reg], lambda e: e.activation(out=t1.t[:], in_=gacc.t[:], func=AF.Silu))
                pvl = next_pp()
                proj(sl, wv[1], 0, 1, T, pvl)
                pg.op("dve", [pvl.reg, t1.reg], [u_sb.reg],
                      lambda e: e.tensor_tensor(out=u_sb.t[:, fb, :], in0=pvl.t[:], in1=t1.t[:], op=ALU.mult))
            for db in range(KC):
                sl, wv = ws.get()
                pO = next_pp()
                proj(sl, wv[0], 0, 0, T, pO, kc=NFB, rhs=u_sb)
                pg.op("dve", [pO.reg, xt.reg], [xt.reg],
                      lambda e: e.tensor_tensor(out=xt.t[:, db, 1:1 + T], in0=pO.t[:], in1=xt.t[:, db, 1:1 + T], op=ALU.add))
            if last and final_norm:
                pst = next_pp()
                sumsq(lambda c: xt.t[:, c, 1:1 + T], KC, T, pst)
                rsqrt_to(var, var.t[:], pst, T, 1.0 / D)
                for c in range(KC):
                    eng = "dve"
                    pg.op(eng, [xt.reg, var.reg, gvec.reg], [xt.reg],
                          lambda e: e.scalar_tensor_tensor(out=xt.t[:, c, 1:1 + T], in0=xt.t[:, c, 1:1 + T], scalar=gvec.t[:, c:c + 1], in1=var.t[:],
                                                           op0=ALU.mult, op1=ALU.mult))
            if last:
                pg.dma("sp", y_out.ap()[:, i * T:(i + 1) * T].rearrange("(c p) n -> p c n", p=P), xt.t[:, :, 1:1 + T], [xt.reg], [rxd], join=True)
            else:
                pg.dma("sp", xdst.ap()[:, HL + i * T:HL + (i + 1) * T].rearrange("(c p) n -> p c n", p=P), xt.t[:, :, 1:1 + T],
                       [xt.reg], [rxd], join=True)

    import os
    KSTOP = int(os.environ.get("KSTOP", "9"))
    r_y = Reg()
    for l in range(NL):
        if KSTOP < 9:
            if KSTOP >= 1:
                sweep1(l, xA, r_xA)
            if KSTOP >= 2:
                sweep2(l, xA, r_xA)
            if KSTOP >= 3:
                pg.barrier()
                sweep3(l, xA, r_xA, xB, r_xB)
            if KSTOP >= 4:
                exchange_halo(xB, r_xB)
            pg.wait_all("sp", [r_xA, r_xB, r_ob] + [wregs[l][k] for k in wregs[l]])
            pg.wait_all("pool", [wregs[l][k] for k in wregs[l]])
            break
        if l + 1 < NL and not KCASTALL:
            cast_w(l + 1)
        if l > 0:
            exchange_halo(xA, r_xA)
        sweep1(l, xA, r_xA)
        sweep2(l, xA, r_xA)
        pg.barrier()
        sweep3(l, xA, r_xA, xB, r_xB)
        exchange_halo(xB, r_xB)
        pg.barrier()
        last = (l == NL - 1)
        sweep4(l, xB, r_xB, xA, r_y if last else r_xA, last)
    pg.wait_all("sp", [r_y])
    return nc


def _relayout(W, kc):
    K, N = W.shape
    assert K == kc * 128
    return np.ascontiguousarray(W.reshape(kc, 128, N // 128, 128).transpose(2, 1, 0, 3).reshape(N // 128, 128, kc * 128))


def _pcol(v):
    return np.ascontiguousarray(v.reshape(-1, 128).T)


def _make_consts():
    c = np.zeros((128, 128 + 512 + 512), np.float32)
    c[:, 0:128] = np.eye(128, dtype=np.float32)
    s = np.arange(64)[:, None]
    t = np.arange(64)[None, :]
    mf = (s <= t).astype(np.float32)
    mb = (s >= t).astype(np.float32)
    c[0:64, 128:640] = np.tile(mf, (1, 8))
    c[0:64, 640:1152] = np.tile(mb, (1, 8))
    return c


def _core_masks(seg):
    m = np.zeros((128, NCM), np.float32)
    for k in range(4):
        uf = 1.0 if k < seg else 0.0
        ub = 1.0 if k > seg else 0.0
        m[:, k] = uf
        m[:, 4 + k] = 1.0 - uf
        m[:, 8 + k] = ub
        m[:, 12 + k] = 1.0 - ub
        m[:, 16 + k] = 1.0 if k == seg - 1 else 0.0
        m[:, 20 + k] = 1.0 if k == seg + 1 else 0.0
    return m


def _prep_weights(inp, layers):
    L = layers
    out = {}
    out["w_in"] = np.stack([_relayout(np.asarray(inp["w_in"][l]), 8) for l in L])
    out["w_pw"] = np.stack([_relayout(np.asarray(inp["conv_pw_w"][l]), 4) for l in L])
    out["w_ow"] = np.stack([_relayout(np.asarray(inp["hgrn_o_w"][l]), 8) for l in L])
    out["w_out"] = np.stack([_relayout(np.asarray(inp["w_out"][l]), 8) for l in L])
    out["w_up"] = np.stack([_relayout(np.asarray(inp["ffn_w_up"][l]), 8) for l in L])
    out["w_dn"] = np.stack([_relayout(np.asarray(inp["ffn_w_down"][l]), NFB) for l in L])
    vecs = []
    for l in L:
        v = np.zeros((128, NV), np.float32)
        v[:, 0:8] = _pcol(np.asarray(inp["attn_norm_w"][l]))
        dw = np.asarray(inp["conv_dw_w"][l])
        for cb in range(4):
            v[:, 8 + cb * 31:8 + (cb + 1) * 31] = dw[:, cb * 128:(cb + 1) * 128].T
        v[:, 132:136] = _pcol(np.asarray(inp["conv_dw_b"][l]))
        v[:, 136:140] = _pcol(np.asarray(inp["conv_ln_w"][l]))
        v[:, 140:144] = _pcol(np.asarray(inp["conv_ln_b"][l]))
        v[:, 144:152] = _pcol(np.asarray(inp["hgrn_norm_w"][l]))
        v[:, 152:160] = _pcol(np.asarray(inp["ffn_norm_w"][l]))
        fw = np.asarray(inp["ffn_dw_w"][l])
        for fb in range(NFB):
            v[:, 160 + fb * 3:163 + fb * 3] = fw[:, fb * 128:(fb + 1) * 128].T
        v[:, 226:248] = _pcol(np.asarray(inp["ffn_dw_b"][l]))
        vecs.append(v)
    out["vec"] = np.stack(vecs)
    g = np.zeros((128, 72), np.float32)
    g[:, 0:8] = _pcol(np.asarray(inp["final_norm_w"]))
    lbl = np.asarray(inp["lb_logits"])
    for l in range(4):
        for d in range(2):
            g[:, 8 + (l * 2 + d) * 8:8 + (l * 2 + d) * 8 + 8] = _pcol(lbl[l, d])
    out["gvec"] = g
    out["consts"] = _make_consts()
    return out


_PROG_CACHE = {}


def _run(x, inp, layer_groups):
    B, S, _ = x.shape
    nseg = 8 // B
    TOK = S // nseg
    cur = np.asarray(x, dtype=np.float32)
    for gi, layers in enumerate(layer_groups):
        final = (gi == len(layer_groups) - 1)
        key = (TOK, tuple(layers), final)
        if key not in _PROG_CACHE:
            _PROG_CACHE[key] = build_program(TOK, list(layers), final)
        nc = _PROG_CACHE[key]
        wts = _prep_weights(inp, layers)
        in_maps = []
        for c in range(8):
            b, seg = c // nseg, c % nseg
            xp = np.zeros((S + 2 * HL, D), np.float32)
            xp[HL:HL + S] = cur[b]
            sl = xp[seg * TOK:seg * TOK + TOK + 2 * HL]
            m = dict(wts)
            m["x_in"] = np.ascontiguousarray(sl.T)
            m["cm"] = _core_masks(seg)
            in_maps.append(m)
        res = run_bass_kernel_spmd(nc, in_maps, core_ids=list(range(8)))
        nxt = np.empty_like(cur)
        for c in range(8):
            b, seg = c // nseg, c % nseg
            nxt[b, seg * TOK:(seg + 1) * TOK] = res.results[c]["y_out"].T
        cur = nxt
    return cur


def kernel(**inputs):
    x = np.asarray(inputs["x"], dtype=np.float32)
    return _run(x, inputs, [[0, 1, 2, 3]])
```
